# Optimizing a Trainium2 kernel written in Bass

```python
import math
import jax, jax.numpy as jnp
from jax import lax
import numpy as np

D_MODEL = 2048
BATCH = 4
SEQ = 8192
DEPTH = 1

MEM_LEN = 256
SWA_HEADS = 16
SWA_KV_HEADS = 2
SWA_HEAD_DIM = 64
WINDOW = 128
BLOCK = 128
REL_BUCKETS = 32
REL_MAX_DIST = 128
GDN_HEADS = 8
GDN_HEAD_DIM = 128
GDN_CONV = 4
GDN_CHUNK = 64
MEM_HEADS = 4
MEM_HEAD_DIM = 128
D_FF = 5504
FFN_CONV = 3

NORM_EPS = 1e-5
DEEPNORM_ALPHA = (2 * DEPTH) ** 0.25
DEEPNORM_BETA = (8 * DEPTH) ** -0.25
NEG_INF = -1e30

SWA_Q = SWA_HEADS * SWA_HEAD_DIM
SWA_KV = SWA_KV_HEADS * SWA_HEAD_DIM
GDN_W = GDN_HEADS * GDN_HEAD_DIM
MEM_W = MEM_HEADS * MEM_HEAD_DIM
IN_WIDTHS = (SWA_Q, SWA_KV, SWA_KV, GDN_W, GDN_W, GDN_W, GDN_W, GDN_HEADS, GDN_HEADS, D_MODEL, D_MODEL)
IN_DIM = sum(IN_WIDTHS)

kernel_name = "hybrid_swa_gdn_gated_merge_deepnorm"


def split_columns(t, widths):
    points, acc = [], 0
    for w in widths[:-1]:
        acc += w
        points.append(acc)
    return jnp.split(t, points, axis=-1)


def layer_norm(x, g, b):
    xf = x.astype(jnp.float32)
    mu = jnp.mean(xf, axis=-1, keepdims=True)
    var = jnp.mean(jnp.square(xf - mu), axis=-1, keepdims=True)
    y = (xf - mu) * lax.rsqrt(var + NORM_EPS) * g.astype(jnp.float32) + b.astype(jnp.float32)
    return y.astype(x.dtype)


def causal_dwconv(x, w):
    width, ch = w.shape
    return lax.conv_general_dilated(
        x, w[:, None, :].astype(x.dtype), window_strides=(1,), padding=[(width - 1, 0)],
        dimension_numbers=('NWC', 'WIO', 'NWC'), feature_group_count=ch)


def t5_causal_bucket(dist):
    max_exact = REL_BUCKETS // 2
    d = jnp.maximum(dist, 1).astype(jnp.float32)
    large = max_exact + (jnp.log(d / max_exact) / math.log(REL_MAX_DIST / max_exact)
                         * (REL_BUCKETS - max_exact)).astype(jnp.int32)
    large = jnp.minimum(large, REL_BUCKETS - 1)
    return jnp.where(dist < max_exact, dist, large)


def swa_attention(q, k, v, sinks, rel_bias):
    b, s = q.shape[:2]
    nb = s // BLOCK
    grp = SWA_HEADS // SWA_KV_HEADS
    qb = q.reshape(b, nb, BLOCK, SWA_KV_HEADS, grp, SWA_HEAD_DIM)

    def band(t):
        tb = t.reshape(b, nb, BLOCK, SWA_KV_HEADS, SWA_HEAD_DIM)
        prev = jnp.pad(tb[:, :-1], ((0, 0), (1, 0), (0, 0), (0, 0), (0, 0)))
        return jnp.concatenate([prev, tb], axis=2)

    kb, vb = band(k), band(v)
    scores = jnp.einsum('bnqhgd,bnkhd->bnhgqk', qb, kb).astype(jnp.float32) * (SWA_HEAD_DIM ** -0.5)

    qi = jnp.arange(BLOCK)[:, None]
    kj = jnp.arange(2 * BLOCK)[None, :]
    dist = qi + BLOCK - kj
    in_window = (dist >= 0) & (dist < WINDOW)
    has_prev = (jnp.arange(nb)[:, None, None] > 0) | (kj >= BLOCK)[None]
    mask = in_window[None] & has_prev
    bias = rel_bias.astype(jnp.float32)[t5_causal_bucket(jnp.maximum(dist, 0))]
    bias = bias.transpose(2, 0, 1).reshape(SWA_KV_HEADS, grp, BLOCK, 2 * BLOCK)
    scores = jnp.where(mask[None, :, None, None], scores + bias, NEG_INF)

    sink = jnp.broadcast_to(sinks.astype(jnp.float32).reshape(1, 1, SWA_KV_HEADS, grp, 1, 1),
                            scores.shape[:-1] + (1,))
    probs = jax.nn.softmax(jnp.concatenate([scores, sink], axis=-1), axis=-1)[..., :-1]
    out = jnp.einsum('bnhgqk,bnkhd->bnqhgd', probs.astype(vb.dtype), vb)
    return out.reshape(b, s, SWA_Q)


def l2norm(t):
    return t * lax.rsqrt(jnp.sum(jnp.square(t), axis=-1, keepdims=True) + 1e-6)


def gated_delta_rule(q, k, v, g, beta):
    b, s, h, dk = q.shape
    dv = v.shape[-1]
    c = GDN_CHUNK
    n = s // c
    q = q * (dk ** -0.5)

    def chunk(t):
        return jnp.swapaxes(t.reshape((b, n, c, h) + t.shape[3:]), 2, 3)

    qc, kc, vc = chunk(q), chunk(k), chunk(v)
    gc = jnp.cumsum(chunk(g), axis=-1)
    bc = chunk(beta)
    kbeta = kc * bc[..., None]
    vbeta = vc * bc[..., None]

    tril = jnp.tril(jnp.ones((c, c), dtype=bool))
    strict = jnp.tril(jnp.ones((c, c), dtype=bool), -1)
    diff = gc[..., :, None] - gc[..., None, :]
    decay = jnp.where(tril, jnp.exp(jnp.where(tril, diff, 0.0)), 0.0)

    a_low = jnp.where(strict, jnp.einsum('bnhid,bnhjd->bnhij', kbeta, kc) * decay, 0.0)
    t_mat = a_low + jnp.eye(c, dtype=jnp.float32)
    rhs = jnp.concatenate([vbeta, kbeta * jnp.exp(gc)[..., None]], axis=-1)
    sol = lax.linalg.triangular_solve(t_mat, rhs, left_side=True, lower=True)
    u, w = sol[..., :dv], sol[..., dv:]

    attn_intra = jnp.where(tril, jnp.einsum('bnhid,bnhjd->bnhij', qc, kc) * decay, 0.0)
    q_dec = qc * jnp.exp(gc)[..., None]
    k_dec = kc * jnp.exp(gc[..., -1:] - gc)[..., None]
    g_last = jnp.exp(gc[..., -1])

    def step(state, inp):
        qd, kd, uu, ww, ai, gl = inp
        v_new = uu - jnp.einsum('bhck,bhkv->bhcv', ww, state)
        o = jnp.einsum('bhck,bhkv->bhcv', qd, state) + jnp.einsum('bhij,bhjv->bhiv', ai, v_new)
        state = state * gl[..., None, None] + jnp.einsum('bhck,bhcv->bhkv', kd, v_new)
        return state, o

    xs = tuple(jnp.moveaxis(t, 1, 0) for t in (q_dec, k_dec, u, w, attn_intra, g_last))
    state0 = jnp.zeros((b, h, dk, dv), jnp.float32)
    _, o = lax.scan(step, state0, xs)
    return jnp.swapaxes(jnp.moveaxis(o, 0, 1), 2, 3).reshape(b, s, h, dv)


def gated_deltanet(gq, gk, gv, gz, gb, ga, conv_w, a_log, dt_bias, norm_w):
    b, s, _ = gq.shape
    dtype = gq.dtype
    qkv = jax.nn.silu(causal_dwconv(jnp.concatenate([gq, gk, gv], axis=-1), conv_w))
    q, k, v = split_columns(qkv.astype(jnp.float32), (GDN_W, GDN_W, GDN_W))
    q = l2norm(q.reshape(b, s, GDN_HEADS, GDN_HEAD_DIM))
    k = l2norm(k.reshape(b, s, GDN_HEADS, GDN_HEAD_DIM))
    v = v.reshape(b, s, GDN_HEADS, GDN_HEAD_DIM)
    beta = jax.nn.sigmoid(gb.astype(jnp.float32))
    g = -jnp.exp(a_log.astype(jnp.float32)) * jax.nn.softplus(ga.astype(jnp.float32) + dt_bias.astype(jnp.float32))
    o = gated_delta_rule(q, k, v, g, beta)
    o = o * lax.rsqrt(jnp.mean(jnp.square(o), axis=-1, keepdims=True) + 1e-6) * norm_w.astype(jnp.float32)
    z = jax.nn.silu(gz.astype(jnp.float32)).reshape(b, s, GDN_HEADS, GDN_HEAD_DIM)
    return (o * z).reshape(b, s, GDN_W).astype(dtype)


def memory_attention(x, mem, w_q, w_kv, w_o):
    b, s, _ = x.shape
    m = mem.shape[1]
    q = (x @ w_q).reshape(b, s, MEM_HEADS, MEM_HEAD_DIM)
    k, v = split_columns(mem @ w_kv, (MEM_W, MEM_W))
    k = k.reshape(b, m, MEM_HEADS, MEM_HEAD_DIM)
    v = v.reshape(b, m, MEM_HEADS, MEM_HEAD_DIM)
    scores = jnp.einsum('bshd,bmhd->bhsm', q, k).astype(jnp.float32) * (MEM_HEAD_DIM ** -0.5)
    p = jax.nn.softmax(scores, axis=-1)
    o = jnp.einsum('bhsm,bmhd->bshd', p.astype(v.dtype), v).reshape(b, s, MEM_W)
    return o @ w_o


def setup_inputs(seed: int = 0) -> dict:
    key = jax.random.key(seed)
    ks = jax.random.split(key, 32)
    f32 = jnp.float32
    L = DEPTH

    def nrm(k, shape, scale):
        return jax.random.normal(k, shape, f32) * scale

    dt = jnp.exp(jax.random.uniform(ks[7], (L, GDN_HEADS), f32, math.log(1e-3), math.log(1e-1)))
    return {
        "x": nrm(ks[0], (BATCH, SEQ, D_MODEL), 1.0),
        "mem": nrm(ks[1], (BATCH, MEM_LEN, D_MODEL), 1.0),
        "w_in": nrm(ks[2], (L, D_MODEL, IN_DIM), D_MODEL ** -0.5),
        "rel_bias": nrm(ks[3], (REL_BUCKETS, SWA_HEADS), 0.5),
        "swa_sinks": nrm(ks[4], (L, SWA_HEADS), 1.0),
        "gdn_conv_w": nrm(ks[5], (L, GDN_CONV, 3 * GDN_W), GDN_CONV ** -0.5),
        "gdn_a_log": jnp.log(jax.random.uniform(ks[6], (L, GDN_HEADS), f32, 1.0, 16.0)),
        "gdn_dt_bias": dt + jnp.log(-jnp.expm1(-dt)),
        "gdn_norm_w": 1.0 + nrm(ks[8], (L, GDN_HEAD_DIM), 0.02),
        "w_br_swa": nrm(ks[9], (L, SWA_Q, D_MODEL), SWA_Q ** -0.5),
        "w_br_gdn": nrm(ks[10], (L, GDN_W, D_MODEL), GDN_W ** -0.5),
        "w_mix_o": nrm(ks[11], (L, D_MODEL, D_MODEL), D_MODEL ** -0.5 * DEEPNORM_BETA),
        "ln1_g": 1.0 + nrm(ks[12], (L, D_MODEL), 0.02),
        "ln1_b": nrm(ks[13], (L, D_MODEL), 0.02),
        "w_mem_q": nrm(ks[14], (L, D_MODEL, MEM_W), D_MODEL ** -0.5),
        "w_mem_kv": nrm(ks[15], (L, D_MODEL, 2 * MEM_W), D_MODEL ** -0.5),
        "w_mem_o": nrm(ks[16], (L, MEM_W, D_MODEL), MEM_W ** -0.5 * DEEPNORM_BETA),
        "ln2_g": 1.0 + nrm(ks[17], (L, D_MODEL), 0.02),
        "ln2_b": nrm(ks[18], (L, D_MODEL), 0.02),
        "w_up": nrm(ks[19], (L, D_MODEL, 2 * D_FF), D_MODEL ** -0.5),
        "ffn_conv_w": nrm(ks[20], (L, FFN_CONV, 2 * D_FF), FFN_CONV ** -0.5),
        "ffn_conv_b": nrm(ks[21], (L, 2 * D_FF), 0.02),
        "w_down": nrm(ks[22], (L, D_FF, D_MODEL), D_FF ** -0.5 * DEEPNORM_BETA),
        "ln3_g": 1.0 + nrm(ks[23], (L, D_MODEL), 0.02),
        "ln3_b": nrm(ks[24], (L, D_MODEL), 0.02),
    }


def reference(x, mem, w_in, rel_bias, swa_sinks, gdn_conv_w, gdn_a_log, gdn_dt_bias, gdn_norm_w,
              w_br_swa, w_br_gdn, w_mix_o, ln1_g, ln1_b, w_mem_q, w_mem_kv, w_mem_o, ln2_g, ln2_b,
              w_up, ffn_conv_w, ffn_conv_b, w_down, ln3_g, ln3_b):
    b, s, _ = x.shape
    for l in range(DEPTH):
        proj = x @ w_in[l]
        (sq, sk, sv, gq, gk, gv, gz, gb, ga, gate_swa, gate_gdn) = split_columns(proj, IN_WIDTHS)
        y_swa = swa_attention(
            sq.reshape(b, s, SWA_HEADS, SWA_HEAD_DIM),
            sk.reshape(b, s, SWA_KV_HEADS, SWA_HEAD_DIM),
            sv.reshape(b, s, SWA_KV_HEADS, SWA_HEAD_DIM),
            swa_sinks[l], rel_bias) @ w_br_swa[l]
        y_gdn = gated_deltanet(gq, gk, gv, gz, gb, ga, gdn_conv_w[l], gdn_a_log[l],
                               gdn_dt_bias[l], gdn_norm_w[l]) @ w_br_gdn[l]
        mixed = jax.nn.sigmoid(gate_swa) * y_swa + jax.nn.sigmoid(gate_gdn) * y_gdn
        x = layer_norm(DEEPNORM_ALPHA * x + mixed @ w_mix_o[l], ln1_g[l], ln1_b[l])
        c = memory_attention(x, mem, w_mem_q[l], w_mem_kv[l], w_mem_o[l])
        x = layer_norm(DEEPNORM_ALPHA * x + c, ln2_g[l], ln2_b[l])
        hcat = causal_dwconv(x @ w_up[l], ffn_conv_w[l]) + ffn_conv_b[l]
        h_gate, h_up = split_columns(hcat, (D_FF, D_FF))
        f = (jax.nn.silu(h_gate) * h_up) @ w_down[l]
        x = layer_norm(DEEPNORM_ALPHA * x + f, ln3_g[l], ln3_b[l])
    return x
```

```python
import numpy as np
import ml_dtypes
from contextlib import ExitStack
import concourse.bass as bass
import concourse.mybir as mybir
from concourse.bass_utils import run_bass_kernel_spmd

F32 = mybir.dt.float32
BF16 = mybir.dt.bfloat16
AF = mybir.ActivationFunctionType
ALU = mybir.AluOpType
AX = mybir.AxisListType

P = 128
D = 2048
KC = 16
IN_DIM = 9488
DFF = 5504
NFF = 43
ALPHA = 2.0 ** 0.25
EPS = 1e-5
TMAX = 512
ARENA_KB = 200

C_SQ, C_SK, C_SV, C_GQ, C_GK, C_GV, C_GZ, C_GB, C_GA, C_GS, C_GG = (
    0, 1024, 1152, 1280, 2304, 3328, 4352, 5376, 5384, 5392, 7440)


class Buf:
    __slots__ = ("w", "r")

    def __init__(self, r=None):
        self.w = None
        self.r = dict(r) if r else {}


class T:
    def __init__(self, ap, bufs):
        self.ap = ap
        self.bufs = bufs

    def __getitem__(self, idx):
        return self.ap[idx]


def _bufs(lst):
    out = []
    for x in lst:
        if x is None:
            continue
        if isinstance(x, Buf):
            out.append(x)
        elif isinstance(x, T):
            out.extend(x.bufs)
        else:
            out.extend(_bufs(x))
    return out


class Kb:
    def __init__(self, nc, es):
        self.nc = nc
        self.eng = {"pe": nc.tensor, "dve": nc.vector, "act": nc.scalar, "pool": nc.gpsimd, "sp": nc.sync}
        self.sem = {}
        self.cnt = {}
        self.seen = {e: {} for e in self.eng}
        self.es = es
        for e in ("pe", "dve", "act", "pool"):
            self.newkey(e)
        self.arena = es.enter_context(nc.sbuf_tensor("arena", [P, ARENA_KB * 256], F32))
        self.off = 0
        self.free_deps = {}
        self.peak = 0
        self.psum = es.enter_context(nc.psum_tensor("psum", [P, 4096], F32))
        self.pbanks = [T(self.psum[:, i * 512:(i + 1) * 512], [Buf()]) for i in range(8)]
        self.pidx = 0
        self.ninstr = 0

    def newkey(self, name):
        self.sem[name] = self.es.enter_context(self.nc.semaphore(name))
        self.cnt[name] = 0

    def alloc(self, nelem, dtype=F32, nbuf=1):
        size = {F32: 4, BF16: 2}[dtype]
        nbytes = (nelem * size + 63) // 64 * 64
        assert self.off + nbytes <= ARENA_KB * 1024, ("SBUF arena overflow", self.off, nbytes)
        ap = self.arena[:, self.off // 4:(self.off + nbytes) // 4]
        if dtype != F32:
            ap = ap.bitcast(dtype)
        ap = ap[:, 0:nelem]
        self.off += nbytes
        self.peak = max(self.peak, self.off)
        return T(ap, [Buf(self.free_deps)])

    def mark(self):
        return self.off

    def release(self, mark, tiles):
        for b in _bufs(tiles):
            if b.w:
                self.free_deps[b.w[0]] = max(self.free_deps.get(b.w[0], 0), b.w[1])
            for k, v in b.r.items():
                self.free_deps[k] = max(self.free_deps.get(k, 0), v)
        self.off = mark

    def ps(self):
        t = self.pbanks[self.pidx]
        self.pidx = (self.pidx + 1) % 8
        return t

    def _wait(self, e, key, val):
        if key == e and e == "pe":
            return
        if self.seen[e].get(key, 0) >= val:
            return
        self.eng[e].wait_ge(self.sem[key], val)
        self.seen[e][key] = val

    def _deps(self, R, W):
        deps = {}
        for b in R:
            if b.w:
                deps[b.w[0]] = max(deps.get(b.w[0], 0), b.w[1])
        for b in W:
            if b.w:
                deps[b.w[0]] = max(deps.get(b.w[0], 0), b.w[1])
            for k, v in b.r.items():
                deps[k] = max(deps.get(k, 0), v)
        return deps

    def op(self, e, fn, R=(), W=()):
        R = _bufs(R)
        W = _bufs(W)
        for k, v in self._deps(R, W).items():
            self._wait(e, k, v)
        ins = fn(self.eng[e])
        self.cnt[e] += 1
        ins.then_inc(self.sem[e], 1)
        seq = self.cnt[e]
        for b in R:
            b.r[e] = seq
        for b in W:
            b.w = (e, seq)
            b.r = {}
        self.ninstr += 1
        return ins

    def dma(self, q, key, out, in_, R=(), W=()):
        R = _bufs(R)
        W = _bufs(W)
        for k, v in self._deps(R, W).items():
            if k != key:
                self._wait(q, k, v)
        self.eng[q].dma_start(out=out, in_=in_).then_inc(self.sem[key], 16)
        self.cnt[key] += 16
        seq = self.cnt[key]
        for b in R:
            b.r[key] = seq
        for b in W:
            b.w = (key, seq)
            b.r = {}
        self.ninstr += 1

    def wait_all(self, e):
        for k, v in self.cnt.items():
            if v:
                self._wait(e, k, v)

    def tt(self, out, in0, in1, op, R, W, e="dve"):
        return self.op(e, lambda g: g.tensor_tensor(out=out, in0=in0, in1=in1, op=op), R, W)

    def ts(self, out, in0, s1, s2, op0, op1, R, W, e="dve"):
        if op1 is None:
            return self.op(e, lambda g: g.tensor_scalar(out=out, in0=in0, scalar1=s1, scalar2=None, op0=op0), R, W)
        return self.op(e, lambda g: g.tensor_scalar(out=out, in0=in0, scalar1=s1, scalar2=s2, op0=op0, op1=op1), R, W)

    def stt(self, out, in0, sc, in1, op0, op1, R, W):
        return self.op("dve", lambda g: g.scalar_tensor_tensor(out=out, in0=in0, scalar=sc, in1=in1, op0=op0, op1=op1), R, W)

    def act(self, out, in_, func, R, W, bias=None, scale=None):
        kw = {}
        if bias is not None:
            kw["bias"] = bias
        if scale is not None:
            kw["scale"] = scale
        return self.op("act", lambda g: g.activation(out=out, in_=in_, func=func, **kw), R, W)

    def copy(self, out, in_, R, W, e="act"):
        if e == "act":
            return self.op("act", lambda g: g.activation(out=out, in_=in_, func=AF.Copy), R, W)
        return self.op(e, lambda g: g.tensor_copy(out=out, in_=in_), R, W)

    def mm(self, out, lhsT, rhs, start, stop, R, W):
        return self.op("pe", lambda g: g.matmul(out, lhsT=lhsT, rhs=rhs, start=start, stop=stop), R, W)

    def tr(self, out, in_, ident, R, W):
        return self.op("pe", lambda g: g.transpose(out, in_, ident), R, W)


class WStream:
    def __init__(self, k, groups, nslot=2, slot_elems=8192):
        self.k = k
        self.groups = groups
        self.nslot = nslot
        self.slots = [k.alloc(slot_elems, BF16) for _ in range(nslot)]
        for i in range(nslot):
            k.newkey("w%d" % i)
        self.issued = 0
        self.taken = 0

    def _issue(self):
        i = self.issued
        if i >= len(self.groups):
            return
        tag, parts = self.groups[i]
        s = i % self.nslot
        slot = self.slots[s]
        off = 0
        for (w, r0, kc, c0, nc_) in parts:
            dst = slot.ap[:, off:off + kc * nc_].rearrange("p (k n) -> p k n", k=kc)
            src = w[r0:r0 + kc * P, c0:c0 + nc_].rearrange("(k p) n -> p k n", p=P)
            self.k.dma("pool", "w%d" % s, dst, src, R=(), W=[slot])
            off += kc * nc_
        self.issued += 1

    def next(self, tag):
        while self.issued < min(self.taken + self.nslot, len(self.groups)):
            self._issue()
        gtag, parts = self.groups[self.taken]
        assert gtag == tag, (gtag, tag, self.taken)
        slot = self.slots[self.taken % self.nslot]
        self.taken += 1
        views = []
        off = 0
        for (w, r0, kc, c0, nc_) in parts:
            views.append(slot.ap[:, off:off + kc * nc_].rearrange("p (k n) -> p k n", k=kc))
            off += kc * nc_
        return slot, views

    def prefetch(self):
        while self.issued < min(self.taken + self.nslot, len(self.groups)):
            self._issue()


def tile_plan(ntok_own):
    pre = []
    c = 0
    end = ntok_own - P
    while c < end:
        n = min(TMAX, end - c)
        pre.append((c, n))
        c += n
    main = [(ntok_own - P, P)]
    c = ntok_own
    while c < 2 * ntok_own:
        n = min(TMAX, 2 * ntok_own - c)
        main.append((c, n))
        c += n
    return pre, main


class _Stop(Exception):
    pass


def build(ntok_own, tap=None, stop=None):
    nc = bass.Bass("TRN2", target_bir_lowering=False)
    NT = 2 * ntok_own

    def din(name, shape):
        return nc.dram_tensor(name, list(shape), F32, kind="ExternalInput").ap()

    xT = din("xT", (D, NT))
    memT = din("memT", (D, 256))
    w_in = din("w_in", (D, IN_DIM))
    w_skv = din("w_skv", (D, 512))
    w_brs = din("w_brs", (1024, D))
    w_brg = din("w_brg", (1024, D))
    w_mix = din("w_mix", (D, D))
    w_mq = din("w_mq", (D, 512))
    w_mkv = din("w_mkv", (D, 1024))
    w_mo = din("w_mo", (512, D))
    w_up = din("w_up", (D, 2 * DFF))
    w_dn = din("w_dn", (DFF, D))
    biasT_d = din("biasT", (2, P, 16 * P))
    maskneg_d = din("maskneg", (2, P, P))
    cvec = {}
    for name, n in (("ln1g", 16), ("ln1b", 16), ("ln2g", 16), ("ln2b", 16), ("ln3g", 16), ("ln3b", 16),
                    ("gcw", 96), ("fcw", 258), ("fcb", 86), ("sinkT", 8), ("alog", 8), ("dtb", 8),
                    ("normw", 128), ("valid", 1), ("ident", 128), ("tril", 128), ("strict", 128), ("triu", 128)):
        cvec[name] = (din(name, (P, n)), n)
    yT = nc.dram_tensor("yT", [D, ntok_own], F32, kind="ExternalOutput").ap()
    dbg = None
    if tap is not None:
        dbg = nc.dram_tensor("dbg", [P, tap[1]], F32, kind="ExternalOutput").ap()

    pre_tiles, main_tiles = tile_plan(ntok_own)

    with ExitStack() as es:
        k = Kb(nc, es)
        for key in ("cst", "xf", "xb", "out", "mem"):
            k.newkey(key)

        groups = []

        def G(tag, *parts):
            groups.append((tag, list(parts)))

        G("mk", (w_mkv, 0, KC, 0, 512))
        G("mv", (w_mkv, 0, KC, 512, 512))

        def gdn_kv_groups():
            for kind, c0 in (("k", C_GK), ("v", C_GV)):
                for hh in range(2):
                    G("g%s%d" % (kind, hh), (w_in, 0, KC, c0 + hh * 512, 512))
            G("gba", (w_in, 0, KC, C_GB, 16))

        for ti, (c0, n) in enumerate(pre_tiles):
            if ti == len(pre_tiles) - 1:
                G("skv", (w_skv, 0, KC, 0, 512))
            gdn_kv_groups()
        for ti, (c0, n) in enumerate(main_tiles):
            G("skv", (w_skv, 0, KC, 0, 512))
            G("sq0", (w_in, 0, KC, C_SQ, 512))
            G("sq1", (w_in, 0, KC, C_SQ + 512, 512))
            gdn_kv_groups()
            G("gq0", (w_in, 0, KC, C_GQ, 512))
            G("gq1", (w_in, 0, KC, C_GQ + 512, 512))
            G("gz0", (w_in, 0, KC, C_GZ, 512))
            G("gz1", (w_in, 0, KC, C_GZ + 512, 512))
            for og in range(8):
                G("ms%d" % og, (w_in, 0, KC, C_GS + og * 256, 256), (w_brs, 0, 8, og * 256, 256))
                G("mg%d" % og, (w_in, 0, KC, C_GG + og * 256, 256), (w_brg, 0, 8, og * 256, 256))
            for og in range(4):
                G("mix%d" % og, (w_mix, 0, KC, og * 512, 512))
            G("mq", (w_mq, 0, KC, 0, 512))
            G("mo", (w_mo, 0, 4, 0, 2048))
            for j0 in range(0, NFF, 2):
                npair = min(2, NFF - j0)
                G("up%d" % j0, (w_up, 0, KC, j0 * P, npair * P), (w_up, 0, KC, DFF + j0 * P, npair * P))
            if ti > 0:
                for c in range(16):
                    G("dn%d" % c, (w_dn, 0, NFF, c * P, P))

        cst = {}
        for name, (ap_d, n) in cvec.items():
            cst[name] = k.alloc(n)
            k.dma("sp", "cst", cst[name].ap, ap_d, W=[cst[name]])
        biasm = [k.alloc(16 * P) for _ in range(2)]
        ident_bf = k.alloc(P, BF16)
        ones_bf = k.alloc(P, BF16)
        ones_f = k.alloc(P)
        esinkT = k.alloc(8)
        expA = k.alloc(8)
        KmT = k.alloc(4 * 256, BF16)
        Vm = [k.alloc(512, BF16) for _ in range(2)]
        S = [k.alloc(512) for _ in range(2)]
        ghalo = k.alloc(24 * 3)
        fhalo = k.alloc(86 * 2)
        kTd = [k.alloc(P + TMAX, BF16) for _ in range(2)]
        Vd = [k.alloc(2 * P, BF16) for _ in range(1 + TMAX // P)]
        xf_all = k.alloc(16 * TMAX)
        xf3 = xf_all.ap.rearrange("p (c t) -> p c t", c=16)
        x_f = [T(xf_all.ap[:, c * TMAX:(c + 1) * TMAX], [Buf()]) for c in range(16)]
        x_bf = k.alloc(16 * TMAX, BF16)
        x_bf3 = x_bf.ap.rearrange("p (c t) -> p c t", c=16)
        y_swaT = k.alloc(8 * TMAX, BF16)
        y_gdnT = k.alloc(8 * TMAX, BF16)
        ysw3 = y_swaT.ap.rearrange("p (c t) -> p c t", c=8)
        ygd3 = y_gdnT.ap.rearrange("p (c t) -> p c t", c=8)
        ws = WStream(k, groups, nslot=2)

        k.op("dve", lambda g: g.memset(ones_f.ap, 1.0), W=[ones_f])
        k.op("dve", lambda g: g.memset(ones_bf.ap, 1.0), W=[ones_bf])
        for t in S + [ghalo, fhalo] + kTd + Vd:
            k.op("dve", lambda g, t=t: g.memset(t.ap, 0.0), W=[t])
        m0 = k.mark()
        tmpm = [k.alloc(P) for _ in range(2)]
        for kb in range(2):
            k.dma("sp", "cst", biasm[kb].ap, biasT_d[kb], W=[biasm[kb]])
            k.dma("sp", "cst", tmpm[kb].ap, maskneg_d[kb], W=[tmpm[kb]])
        for b in _bufs(list(cst.values()) + biasm + tmpm):
            b.w = ("cst", k.cnt["cst"])
        k.copy(ident_bf.ap, cst["ident"].ap, [cst["ident"]], [ident_bf], e="dve")
        k.act(esinkT.ap, cst["sinkT"].ap, AF.Exp, [cst["sinkT"]], [esinkT])
        k.act(expA.ap, cst["alog"].ap, AF.Exp, [cst["alog"]], [expA])
        for kb in range(2):
            b3 = biasm[kb].ap.rearrange("p (h q) -> p h q", h=16)
            k.tt(b3, b3, tmpm[kb].ap.unsqueeze(1).to_broadcast([P, 16, P]), ALU.add, [biasm[kb], tmpm[kb]], [biasm[kb]])
        k.dma("pool", "mem", x_bf3[:, :, 0:256], memT.rearrange("(c p) t -> p c t", p=P), W=[x_bf])
        slot, (wv,) = ws.next("mk")
        for h in range(4):
            ps = k.ps()
            for kc in range(KC):
                k.mm(ps.ap[:, 0:256], wv[:, kc, h * P:(h + 1) * P], x_bf3[:, kc, 0:256], kc == 0, kc == KC - 1,
                     [slot, x_bf], [ps])
            k.copy(KmT.ap[:, h * 256:(h + 1) * 256], ps.ap[:, 0:256], [ps], [KmT])
        slot, (wv,) = ws.next("mv")
        for mb in range(2):
            ps = k.ps()
            for kc in range(KC):
                k.mm(ps.ap[:, 0:512], x_bf3[:, kc, mb * P:(mb + 1) * P], wv[:, kc, :], kc == 0, kc == KC - 1,
                     [slot, x_bf], [ps])
            k.copy(Vm[mb].ap, ps.ap, [ps], [Vm[mb]])
        k.release(m0, tmpm)

        def load_x(c0, n, want_f32):
            k.dma("pool", "xb", x_bf3[:, :, 0:n], xT[:, c0:c0 + n].rearrange("(c p) t -> p c t", p=P), W=[x_bf])
            if want_f32:
                k.dma("sp", "xf", xf3[:, :, 0:n], xT[:, c0:c0 + n].rearrange("(c p) t -> p c t", p=P), W=x_f)

        def fm_linear(tag, act3, actbufs, kcn, n, consume, nchunks=4, view=0):
            slot, views = ws.next(tag)
            wv = views[view]
            for oc in range(nchunks):
                ps = k.ps()
                for kc in range(kcn):
                    k.mm(ps.ap[:, 0:n], wv[:, kc, oc * P:(oc + 1) * P], act3[:, kc, 0:n], kc == 0, kc == kcn - 1,
                         [slot] + actbufs, [ps])
                consume(oc, ps)
            return slot, views

        def swa_kv(n):
            nb = n // P
            slot, (wv,) = ws.next("skv")
            for g in range(2):
                ps = k.ps()
                for kc in range(KC):
                    k.mm(ps.ap[:, 0:n], wv[:, kc, g * P:(g + 1) * P], x_bf3[:, kc, 0:n], kc == 0, kc == KC - 1,
                         [slot, x_bf], [ps])
                k.copy(kTd[g].ap[:, P:P + n], ps.ap[:, 0:n], [ps], [kTd[g]])
            for blk in range(nb):
                ps = k.ps()
                for kc in range(KC):
                    k.mm(ps.ap[:, 0:256], x_bf3[:, kc, blk * P:(blk + 1) * P], wv[:, kc, 256:512], kc == 0,
                         kc == KC - 1, [slot, x_bf], [ps])
                k.copy(Vd[1 + blk].ap, ps.ap[:, 0:256], [ps], [Vd[1 + blk]], e="dve")

        def swa_rotate(n):
            nb = n // P
            for g in range(2):
                k.copy(kTd[g].ap[:, 0:P], kTd[g].ap[:, n:n + P], [kTd[g]], [kTd[g]], e="dve")
            k.copy(Vd[0].ap, Vd[nb].ap, [Vd[nb]], [Vd[0]], e="dve")

        def swa_attn(n, first_own):
            nb = n // P
            mk = k.mark()
            qT = k.alloc(8 * TMAX, BF16)
            q3 = qT.ap.rearrange("p (c t) -> p c t", c=8)
            tmp = [k.alloc(512) for _ in range(2)]
            PT = [[k.alloc(512, BF16) for _ in range(2)] for _ in range(2)]
            den = k.alloc(512)
            rden = k.alloc(512)
            for hh in range(2):
                fm_linear("sq%d" % hh, x_bf3, [x_bf], KC, n,
                          lambda oc, ps, hh=hh: k.copy(q3[:, hh * 4 + oc, 0:n], ps.ap[:, 0:n], [ps], [qT]))
            ti = 0
            for blk in range(nb):
                qs = slice(blk * P, (blk + 1) * P)
                for g in range(2):
                    pts = []
                    for par in range(2):
                        half = slice(par * 64, par * 64 + 64)
                        for kb in range(2):
                            ps = k.ps()
                            k.mm(ps.ap.rearrange("p (a b) -> p a b", a=4),
                                 kTd[g].ap[half, (blk + kb) * P:(blk + kb + 1) * P],
                                 q3[half, 4 * g:4 * g + 4, qs], True, True, [kTd[g], qT], [ps])
                            bm = biasm[kb].ap.rearrange("p (c two q) -> p c two q", two=2, q=P)[:, 4 * g:4 * g + 4, par, :]
                            t = tmp[ti % 2]
                            ti += 1
                            k.stt(t.ap.rearrange("p (a b) -> p a b", a=4), ps.ap.rearrange("p (a b) -> p a b", a=4),
                                  0.125, bm, ALU.mult, ALU.add, [ps, biasm[kb]], [t])
                            pt = PT[par][kb]
                            k.act(pt.ap, t.ap, AF.Exp, [t], [pt])
                            if first_own and blk == 0 and kb == 0:
                                k.ts(pt.ap, pt.ap, cst["valid"].ap[:, 0:1], None, ALU.mult, None, [pt, cst["valid"]], [pt])
                    for par in range(2):
                        half = slice(par * 64, par * 64 + 64)
                        psv = k.ps()
                        psd = k.ps()
                        for kb in range(2):
                            k.mm(psv.ap, Vd[blk + kb].ap[:, g * P:(g + 1) * P], PT[par][kb].ap, kb == 0, kb == 1,
                                 [Vd[blk + kb], PT[par][kb]], [psv])
                        for kb in range(2):
                            k.mm(psd.ap, ones_bf.ap, PT[par][kb].ap, kb == 0, kb == 1, [ones_bf, PT[par][kb]], [psd])
                        d3 = den.ap.rearrange("p (a b) -> p a b", a=4)
                        r3 = rden.ap.rearrange("p (a b) -> p a b", a=4)
                        k.tt(d3[half], psd.ap.rearrange("p (a b) -> p a b", a=4)[half],
                             esinkT.ap[half, 4 * g:4 * g + 4].unsqueeze(2).to_broadcast([64, 4, P]), ALU.add,
                             [psd, esinkT], [den])
                        k.op("dve", lambda e, half=half: e.reciprocal(out=r3[half], in_=d3[half]), [den], [rden])
                        k.tt(ysw3[half, 4 * g:4 * g + 4, qs], psv.ap.rearrange("p (a b) -> p a b", a=4)[half], r3[half],
                             ALU.mult, [psv, rden], [y_swaT])
            k.release(mk, [qT, den, rden] + tmp + PT[0] + PT[1])

        def gdn(n, full):
            nb = n // P
            mk = k.mark()
            kTn = k.alloc(8 * TMAX, BF16)
            vTn = k.alloc(8 * TMAX, BF16)
            qTn = k.alloc(8 * TMAX, BF16) if full else None
            k3 = kTn.ap.rearrange("p (c t) -> p c t", c=8)
            v3 = vTn.ap.rearrange("p (c t) -> p c t", c=8)
            q3 = qTn.ap.rearrange("p (c t) -> p c t", c=8) if full else None
            mk_conv = None
            bgd = []
            for blk in range(nb):
                d = {}
                for nm in ("beta", "nbeta", "gpos", "gcp", "glast", "eg", "ekd", "gl", "bge", "tmp"):
                    d[nm] = k.alloc(8)
                bgd.append(d)
            mk_conv = k.mark()
            pre = [k.alloc(3 + TMAX) for _ in range(2)]
            acc = [k.alloc(TMAX) for _ in range(2)]
            sl = [k.alloc(TMAX) for _ in range(2)]
            sq = [k.alloc(TMAX) for _ in range(2)]
            rt = [k.alloc(TMAX) for _ in range(2)]
            gcw = cst["gcw"].ap.rearrange("p (c j) -> p c j", j=4)
            gh3 = ghalo.ap.rearrange("p (c j) -> p c j", j=3)
            cnt = [0]

            def conv_chunk(kind, h, ps):
                ci = {"q": 0, "k": 1, "v": 2}[kind] * 8 + h
                i = cnt[0] % 2
                cnt[0] += 1
                pr, ac, s_, sq_, rt_ = pre[i], acc[i], sl[i], sq[i], rt[i]
                k.copy(pr.ap[:, 0:3], gh3[:, ci, :], [ghalo], [pr], e="dve")
                k.copy(pr.ap[:, 3:3 + n], ps.ap[:, 0:n], [ps], [pr])
                k.copy(gh3[:, ci, :], pr.ap[:, n:n + 3], [pr], [ghalo], e="dve")
                k.ts(ac.ap[:, 0:n], pr.ap[:, 0:n], gcw[:, ci, 0:1], None, ALU.mult, None, [pr, cst["gcw"]], [ac])
                for j in range(1, 4):
                    k.stt(ac.ap[:, 0:n], pr.ap[:, j:j + n], gcw[:, ci, j:j + 1], ac.ap[:, 0:n], ALU.mult, ALU.add,
                          [pr, cst["gcw"], ac], [ac])
                if kind == "v":
                    k.act(v3[:, h, 0:n], ac.ap[:, 0:n], AF.Silu, [ac], [vTn])
                    return
                k.act(s_.ap[:, 0:n], ac.ap[:, 0:n], AF.Silu, [ac], [s_])
                k.act(sq_.ap[:, 0:n], s_.ap[:, 0:n], AF.Square, [s_], [sq_])
                psn = k.ps()
                k.mm(psn.ap[:, 0:n], ones_f.ap, sq_.ap[:, 0:n], True, True, [ones_f, sq_], [psn])
                k.act(rt_.ap[:, 0:n], psn.ap[:, 0:n], AF.Sqrt, [psn], [rt_], bias=cst_eps6.ap[:, 0:1])
                k.op("dve", lambda e: e.reciprocal(out=rt_.ap[:, 0:n], in_=rt_.ap[:, 0:n]), [rt_], [rt_])
                if kind == "k":
                    k.tt(k3[:, h, 0:n], s_.ap[:, 0:n], rt_.ap[:, 0:n], ALU.mult, [s_, rt_], [kTn])
                else:
                    k.stt(q3[:, h, 0:n], s_.ap[:, 0:n], float(P ** -0.5), rt_.ap[:, 0:n], ALU.mult, ALU.mult,
                          [s_, rt_], [qTn])

            for kind in ("k", "v"):
                for hh in range(2):
                    fm_linear("g%s%d" % (kind, hh), x_bf3, [x_bf], KC, n,
                              lambda oc, ps, kind=kind, hh=hh: conv_chunk(kind, hh * 4 + oc, ps))
            ck("g_conv")
            slot_ba, (wba,) = ws.next("gba")
            bg = []
            for blk in range(nb):
                ps = k.ps()
                for kc in range(KC):
                    k.mm(ps.ap[:, 0:16], x_bf3[:, kc, blk * P:(blk + 1) * P], wba[:, kc, :], kc == 0, kc == KC - 1,
                         [slot_ba, x_bf], [ps])
                d = bgd[blk]
                k.act(d["beta"].ap, ps.ap[:, 0:8], AF.Sigmoid, [ps], [d["beta"]])
                k.tt(d["tmp"].ap, ps.ap[:, 8:16], cst["dtb"].ap, ALU.add, [ps, cst["dtb"]], [d["tmp"]])
                k.act(d["tmp"].ap, d["tmp"].ap, AF.Exp, [d["tmp"]], [d["tmp"]])
                k.act(d["tmp"].ap, d["tmp"].ap, AF.Ln, [d["tmp"]], [d["tmp"]], bias=cst_one.ap[:, 0:1])
                k.tt(d["gpos"].ap, d["tmp"].ap, expA.ap, ALU.mult, [d["tmp"], expA], [d["gpos"]])
                k.ts(d["nbeta"].ap, d["beta"].ap, -1.0, None, ALU.mult, None, [d["beta"]], [d["nbeta"]])
                ps2 = k.ps()
                k.mm(ps2.ap[:, 0:8], cst["triu"].ap, d["gpos"].ap, True, True, [cst["triu"], d["gpos"]], [ps2])
                k.mm(ps2.ap[:, 8:16], ones_f.ap, d["gpos"].ap, True, True, [ones_f, d["gpos"]], [ps2])
                k.copy(d["gcp"].ap, ps2.ap[:, 0:8], [ps2], [d["gcp"]], e="dve")
                k.copy(d["glast"].ap, ps2.ap[:, 8:16], [ps2], [d["glast"]], e="dve")
                k.act(d["eg"].ap, d["gcp"].ap, AF.Exp, [d["gcp"]], [d["eg"]], scale=-1.0)
                k.act(d["gl"].ap, d["glast"].ap, AF.Exp, [d["glast"]], [d["gl"]], scale=-1.0)
                k.tt(d["tmp"].ap, d["gcp"].ap, d["glast"].ap, ALU.subtract, [d["gcp"], d["glast"]], [d["tmp"]])
                k.act(d["ekd"].ap, d["tmp"].ap, AF.Exp, [d["tmp"]], [d["ekd"]])
                k.tt(d["bge"].ap, d["beta"].ap, d["eg"].ap, ALU.mult, [d["beta"], d["eg"]], [d["bge"]])
                bg.append(d)
            ck("g_bg")
            if full:
                for hh in range(2):
                    fm_linear("gq%d" % hh, x_bf3, [x_bf], KC, n,
                              lambda oc, ps, hh=hh: conv_chunk("q", hh * 4 + oc, ps))

            k.release(mk_conv, pre + acc + sl + sq + rt)
            def a4(dt=F32):
                return k.alloc(512, dt)

            Ug = [k.alloc(P) for _ in range(2)]
            k_tok, v_tok, aiT, XTb, vbeta, kbg, wTb, qdT, kd, Sb, vnew = [a4(BF16) for _ in range(11)]
            Nn, NT, XT, c1, c2, u_sb, o_sb, egB = [a4() for _ in range(8)]
            ss = k.alloc(4)

            def v4(t):
                return t.ap.rearrange("p (a b) -> p a b", a=4)

            idf = cst["ident"].ap.unsqueeze(1).to_broadcast([P, 4, P])
            strict_b = cst["strict"].ap.unsqueeze(1).to_broadcast([P, 4, P])
            triu_b = cst["triu"].ap.unsqueeze(1).to_broadcast([P, 4, P])
            normw_b = cst["normw"].ap.unsqueeze(1).to_broadcast([P, 4, P])

            for blk in range(nb):
                d = bg[blk]
                ts_ = slice(blk * P, (blk + 1) * P)
                for hg in range(2):
                    hs = slice(4 * hg, 4 * hg + 4)
                    psB = k.ps()
                    for j in range(4):
                        h = 4 * hg + j
                        ug = Ug[j % 2]
                        k.ts(ug.ap, cst["triu"].ap, d["gpos"].ap[:, h:h + 1], None, ALU.mult, None,
                             [cst["triu"], d["gpos"]], [ug])
                        k.mm(psB.ap[:, j * P:(j + 1) * P], ones_f.ap, ug.ap, True, True, [ones_f, ug], [psB])
                    for j in range(4):
                        h = 4 * hg + j
                        k.ts(c1.ap[:, j * P:(j + 1) * P], psB.ap[:, j * P:(j + 1) * P], d["gcp"].ap[:, h:h + 1], 0.0,
                             ALU.subtract, ALU.min, [psB, d["gcp"]], [c1])
                        if full:
                            k.ts(c2.ap[:, j * P:(j + 1) * P], psB.ap[:, j * P:(j + 1) * P], d["gcp"].ap[:, h:h + 1], 0.0,
                                 ALU.subtract, ALU.max, [psB, d["gcp"]], [c2])
                    k.act(c1.ap, c1.ap, AF.Exp, [c1], [c1])
                    k.tt(v4(c1), v4(c1), strict_b, ALU.mult, [c1, cst["strict"]], [c1])
                    if full:
                        k.act(c2.ap, c2.ap, AF.Exp, [c2], [c2], scale=-1.0)
                        k.tt(v4(c2), v4(c2), triu_b, ALU.mult, [c2, cst["triu"]], [c2])
                        k.act(egB.ap, psB.ap, AF.Exp, [psB], [egB], scale=-1.0)
                        k.tt(v4(qdT), q3[:, hs, ts_], v4(egB), ALU.mult, [qTn, egB], [qdT])
                    ck("g_dec")
                    for src3, dst in ((k3, k_tok), (v3, v_tok)):
                        pst = k.ps()
                        pstb = pst.ap.bitcast(BF16)
                        for j in range(4):
                            k.tr(pstb[:, j * P:(j + 1) * P], src3[:, 4 * hg + j, ts_], ident_bf.ap,
                                 [kTn, vTn, ident_bf], [pst])
                        k.copy(dst.ap, pstb[:, 0:512], [pst], [dst])
                    ck("g_tr")
                    psK = k.ps()
                    for j in range(4):
                        h = 4 * hg + j
                        k.mm(psK.ap[:, j * P:(j + 1) * P], k3[:, h, ts_], k3[:, h, ts_], True, True, [kTn], [psK])
                    for j in range(4):
                        h = 4 * hg + j
                        k.stt(Nn.ap[:, j * P:(j + 1) * P], psK.ap[:, j * P:(j + 1) * P], d["nbeta"].ap[:, h:h + 1],
                              c1.ap[:, j * P:(j + 1) * P], ALU.mult, ALU.mult, [psK, d["nbeta"], c1], [Nn])
                    if full:
                        psQ = k.ps()
                        for j in range(4):
                            h = 4 * hg + j
                            k.mm(psQ.ap[:, j * P:(j + 1) * P], k3[:, h, ts_], q3[:, h, ts_], True, True, [kTn, qTn], [psQ])
                        k.tt(aiT.ap, psQ.ap, c2.ap, ALU.mult, [psQ, c2], [aiT])
                    ck("g_N")
                    psT = k.ps()
                    for j in range(4):
                        k.mm(psT.ap[:, j * P:(j + 1) * P], Nn.ap[:, j * P:(j + 1) * P], cst["ident"].ap, True, True,
                             [Nn, cst["ident"]], [psT])
                    ck("g_NT0")
                    k.copy(NT.ap, psT.ap, [psT], [NT])
                    ck("g_NT1")
                    k.tt(v4(XT), v4(NT), idf, ALU.add, [NT, cst["ident"]], [XT])
                    ck("g_NT")
                    for it in range(6):
                        psM = k.ps()
                        for j in range(4):
                            js = slice(j * P, (j + 1) * P)
                            k.mm(psM.ap[:, js], NT.ap[:, js], Nn.ap[:, js], True, True, [NT, Nn], [psM])
                        if it < 5:
                            psMT = k.ps()
                            for j in range(4):
                                js = slice(j * P, (j + 1) * P)
                                k.mm(psMT.ap[:, js], Nn.ap[:, js], NT.ap[:, js], True, True, [NT, Nn], [psMT])
                        k.copy(Nn.ap, psM.ap, [psM], [Nn])
                        if it < 5:
                            k.copy(NT.ap, psMT.ap, [psMT], [NT], e="dve")
                        psX = k.ps()
                        for j in range(4):
                            js = slice(j * P, (j + 1) * P)
                            k.mm(psX.ap[:, js], Nn.ap[:, js], XT.ap[:, js], True, True, [Nn, XT], [psX])
                        k.tt(XT.ap, XT.ap, psX.ap, ALU.add, [XT, psX], [XT])
                    ck("g_neu")
                    k.copy(XTb.ap, XT.ap, [XT], [XTb])
                    for j in range(4):
                        h = 4 * hg + j
                        js = slice(j * P, (j + 1) * P)
                        k.ts(vbeta.ap[:, js], v_tok.ap[:, js], d["beta"].ap[:, h:h + 1], None, ALU.mult, None,
                             [v_tok, d["beta"]], [vbeta])
                        k.ts(kbg.ap[:, js], k_tok.ap[:, js], d["bge"].ap[:, h:h + 1], None, ALU.mult, None,
                             [k_tok, d["bge"]], [kbg])
                        k.ts(kd.ap[:, js], k_tok.ap[:, js], d["ekd"].ap[:, h:h + 1], None, ALU.mult, None,
                             [k_tok, d["ekd"]], [kd])
                    psU = k.ps()
                    psW = k.ps()
                    for j in range(4):
                        js = slice(j * P, (j + 1) * P)
                        k.mm(psU.ap[:, js], XTb.ap[:, js], vbeta.ap[:, js], True, True, [XTb, vbeta], [psU])
                    for j in range(4):
                        js = slice(j * P, (j + 1) * P)
                        k.mm(psW.ap[:, js], kbg.ap[:, js], XTb.ap[:, js], True, True, [XTb, kbg], [psW])
                    k.copy(u_sb.ap, psU.ap, [psU], [u_sb])
                    k.copy(wTb.ap, psW.ap, [psW], [wTb], e="dve")
                    ck("g_uw")
                    k.copy(Sb.ap, S[hg].ap, [S[hg]], [Sb])
                    psWS = k.ps()
                    for j in range(4):
                        js = slice(j * P, (j + 1) * P)
                        k.mm(psWS.ap[:, js], wTb.ap[:, js], Sb.ap[:, js], True, True, [wTb, Sb], [psWS])
                    k.tt(vnew.ap, u_sb.ap, psWS.ap, ALU.subtract, [u_sb, psWS], [vnew])
                    if full:
                        psO = k.ps()
                        for j in range(4):
                            js = slice(j * P, (j + 1) * P)
                            k.mm(psO.ap[:, js], qdT.ap[:, js], Sb.ap[:, js], True, False, [qdT, Sb], [psO])
                            k.mm(psO.ap[:, js], aiT.ap[:, js], vnew.ap[:, js], False, True, [aiT, vnew], [psO])
                    psS = k.ps()
                    for j in range(4):
                        js = slice(j * P, (j + 1) * P)
                        k.mm(psS.ap[:, js], kd.ap[:, js], vnew.ap[:, js], True, True, [kd, vnew], [psS])
                    for j in range(4):
                        h = 4 * hg + j
                        js = slice(j * P, (j + 1) * P)
                        k.stt(S[hg].ap[:, js], S[hg].ap[:, js], d["gl"].ap[:, h:h + 1], psS.ap[:, js], ALU.mult, ALU.add,
                              [S[hg], d["gl"], psS], [S[hg]])
                    if full:
                        k.copy(o_sb.ap, psO.ap, [psO], [o_sb])
                        k.tt(c2.ap, o_sb.ap, o_sb.ap, ALU.mult, [o_sb], [c2])
                        k.op("dve", lambda e: e.tensor_reduce(out=ss.ap, in_=v4(c2), axis=AX.X, op=ALU.add), [c2], [ss])
                        k.act(ss.ap, ss.ap, AF.Sqrt, [ss], [ss], bias=cst_eps6.ap[:, 0:1], scale=1.0 / P)
                        k.op("dve", lambda e: e.reciprocal(out=ss.ap, in_=ss.ap), [ss], [ss])
                        k.tt(v4(o_sb), v4(o_sb), ss.ap.unsqueeze(2).to_broadcast([P, 4, P]), ALU.mult, [o_sb, ss], [o_sb])
                        k.tt(v4(kbg), v4(o_sb), normw_b, ALU.mult, [o_sb, cst["normw"]], [kbg])
                        pst = k.ps()
                        pstb = pst.ap.bitcast(BF16)
                        for j in range(4):
                            js = slice(j * P, (j + 1) * P)
                            k.tr(pstb[:, js], kbg.ap[:, js], ident_bf.ap, [kbg, ident_bf], [pst])
                        k.copy(ygd3[:, hs, ts_], pstb[:, 0:512].rearrange("p (a b) -> p a b", a=4), [pst], [y_gdnT])
            if full:
                zs = [k.alloc(TMAX) for _ in range(2)]
                zi = [0]

                def zgate(h, ps):
                    z = zs[zi[0] % 2]
                    zi[0] += 1
                    k.act(z.ap[:, 0:n], ps.ap[:, 0:n], AF.Silu, [ps], [z])
                    k.tt(ygd3[:, h, 0:n], ygd3[:, h, 0:n], z.ap[:, 0:n], ALU.mult, [y_gdnT, z], [y_gdnT])

                for hh in range(2):
                    fm_linear("gz%d" % hh, x_bf3, [x_bf], KC, n, lambda oc, ps, hh=hh: zgate(hh * 4 + oc, ps))
            tl = [kTn, vTn, qTn] + (zs if full else []) + Ug + [k_tok, v_tok, aiT, XTb, vbeta, kbg, wTb, qdT, kd, Sb,
                                                                  vnew, Nn, NT, XT, c1, c2, u_sb, o_sb, egB, ss]
            for d in bg:
                tl += list(d.values())
            k.release(mk, tl)

        def layer_norm(n, gname, bname, want_bf):
            mk = k.mark()
            sqs = [k.alloc(TMAX) for _ in range(2)]
            mean, rstd, nmr, t1 = [k.alloc(TMAX) for _ in range(4)]
            tn = [k.alloc(TMAX) for _ in range(2)]
            pss = k.ps()
            psq = k.ps()
            for c in range(16):
                s_ = sqs[c % 2]
                k.mm(pss.ap[:, 0:n], ones_f.ap, x_f[c].ap[:, 0:n], c == 0, c == 15, [ones_f, x_f[c]], [pss])
                k.act(s_.ap[:, 0:n], x_f[c].ap[:, 0:n], AF.Square, [x_f[c]], [s_])
                k.mm(psq.ap[:, 0:n], ones_f.ap, s_.ap[:, 0:n], c == 0, c == 15, [ones_f, s_], [psq])
            k.ts(mean.ap[:, 0:n], pss.ap[:, 0:n], 1.0 / D, None, ALU.mult, None, [pss], [mean])
            k.tt(t1.ap[:, 0:n], mean.ap[:, 0:n], mean.ap[:, 0:n], ALU.mult, [mean], [t1])
            k.stt(t1.ap[:, 0:n], psq.ap[:, 0:n], 1.0 / D, t1.ap[:, 0:n], ALU.mult, ALU.subtract, [psq, t1], [t1])
            k.act(rstd.ap[:, 0:n], t1.ap[:, 0:n], AF.Sqrt, [t1], [rstd], bias=cst_eps5.ap[:, 0:1])
            k.op("dve", lambda e: e.reciprocal(out=rstd.ap[:, 0:n], in_=rstd.ap[:, 0:n]), [rstd], [rstd])
            k.stt(nmr.ap[:, 0:n], mean.ap[:, 0:n], -1.0, rstd.ap[:, 0:n], ALU.mult, ALU.mult, [mean, rstd], [nmr])
            for c in range(16):
                t = tn[c % 2]
                k.tt(t.ap[:, 0:n], x_f[c].ap[:, 0:n], rstd.ap[:, 0:n], ALU.mult, [x_f[c], rstd], [t])
                k.tt(t.ap[:, 0:n], t.ap[:, 0:n], nmr.ap[:, 0:n], ALU.add, [t, nmr], [t])
                k.act(x_f[c].ap[:, 0:n], t.ap[:, 0:n], AF.Identity, [t, cst[gname], cst[bname]], [x_f[c]],
                      bias=cst[bname].ap[:, c:c + 1], scale=cst[gname].ap[:, c:c + 1])
                if want_bf:
                    k.copy(x_bf3[:, c, 0:n], x_f[c].ap[:, 0:n], [x_f[c]], [x_bf], e="dve")
            k.release(mk, sqs + [mean, rstd, nmr, t1] + tn)

        def resid_add(c, ps, n):
            k.stt(x_f[c].ap[:, 0:n], x_f[c].ap[:, 0:n], ALPHA, ps.ap[:, 0:n], ALU.mult, ALU.add, [x_f[c], ps], [x_f[c]])

        def merge_and_mix(n):
            mk = k.mark()
            mixed = k.alloc(16 * TMAX, BF16)
            mx3 = mixed.ap.rearrange("p (c t) -> p c t", c=16)
            m1 = [k.alloc(TMAX) for _ in range(4)]
            sg = [k.alloc(TMAX) for _ in range(2)]
            si = [0]
            for og in range(8):
                for br, ysrc, ybuf in (("ms", ysw3, y_swaT), ("mg", ygd3, y_gdnT)):
                    slot, (wg, wb) = ws.next("%s%d" % (br, og))
                    for oc in range(2):
                        psg = k.ps()
                        for kc in range(KC):
                            k.mm(psg.ap[:, 0:n], wg[:, kc, oc * P:(oc + 1) * P], x_bf3[:, kc, 0:n], kc == 0, kc == KC - 1,
                                 [slot, x_bf], [psg])
                        psy = k.ps()
                        for kc in range(8):
                            k.mm(psy.ap[:, 0:n], wb[:, kc, oc * P:(oc + 1) * P], ysrc[:, kc, 0:n], kc == 0, kc == 7,
                                 [slot, ybuf], [psy])
                        s_ = sg[si[0] % 2]
                        si[0] += 1
                        k.act(s_.ap[:, 0:n], psg.ap[:, 0:n], AF.Sigmoid, [psg], [s_])
                        if br == "ms":
                            k.tt(m1[oc].ap[:, 0:n], s_.ap[:, 0:n], psy.ap[:, 0:n], ALU.mult, [s_, psy], [m1[oc]])
                        else:
                            k.tt(s_.ap[:, 0:n], s_.ap[:, 0:n], psy.ap[:, 0:n], ALU.mult, [s_, psy], [s_])
                            k.tt(mx3[:, og * 2 + oc, 0:n], s_.ap[:, 0:n], m1[oc].ap[:, 0:n], ALU.add, [s_, m1[oc]], [mixed])
            for og in range(4):
                fm_linear("mix%d" % og, mx3, [mixed], KC, n, lambda oc, ps, og=og: resid_add(og * 4 + oc, ps, n))
            k.release(mk, [mixed] + m1 + sg)

        def mem_attn(n):
            mk = k.mark()
            qm = k.alloc(4 * TMAX, BF16)
            om = k.alloc(4 * TMAX, BF16)
            qm3 = qm.ap.rearrange("p (c t) -> p c t", c=4)
            om3 = om.ap.rearrange("p (c t) -> p c t", c=4)
            PTm = [k.alloc(TMAX, BF16) for _ in range(2)]
            rd = k.alloc(TMAX)
            fm_linear("mq", x_bf3, [x_bf], KC, n, lambda oc, ps: k.copy(qm3[:, oc, 0:n], ps.ap[:, 0:n], [ps], [qm]))
            km3 = KmT.ap.rearrange("p (h m) -> p h m", h=4)
            for h in range(4):
                for mb in range(2):
                    ps = k.ps()
                    k.mm(ps.ap[:, 0:n], km3[:, h, mb * P:(mb + 1) * P], qm3[:, h, 0:n], True, True, [KmT, qm], [ps])
                    k.act(PTm[mb].ap[:, 0:n], ps.ap[:, 0:n], AF.Exp, [ps], [PTm[mb]], scale=float(P ** -0.5))
                pso = k.ps()
                psd = k.ps()
                for mb in range(2):
                    k.mm(pso.ap[:, 0:n], Vm[mb].ap[:, h * P:(h + 1) * P], PTm[mb].ap[:, 0:n], mb == 0, mb == 1,
                         [Vm[mb], PTm[mb]], [pso])
                for mb in range(2):
                    k.mm(psd.ap[:, 0:n], ones_bf.ap, PTm[mb].ap[:, 0:n], mb == 0, mb == 1, [ones_bf, PTm[mb]], [psd])
                k.op("dve", lambda e: e.reciprocal(out=rd.ap[:, 0:n], in_=psd.ap[:, 0:n]), [psd], [rd])
                k.tt(om3[:, h, 0:n], pso.ap[:, 0:n], rd.ap[:, 0:n], ALU.mult, [pso, rd], [om])
            slot, (wv,) = ws.next("mo")
            for c in range(16):
                ps = k.ps()
                for kc in range(4):
                    k.mm(ps.ap[:, 0:n], wv[:, kc, c * P:(c + 1) * P], om3[:, kc, 0:n], kc == 0, kc == 3, [slot, om], [ps])
                resid_add(c, ps, n)
            k.release(mk, [qm, om, rd] + PTm)

        def ffn(n, halo_only):
            mk = k.mark()
            a = None if halo_only else k.alloc(NFF * TMAX, BF16)
            a3 = None if halo_only else a.ap.rearrange("p (c t) -> p c t", c=NFF)
            hb = [k.alloc(2 + TMAX) for _ in range(4)]
            tc_ = [k.alloc(TMAX) for _ in range(4)]
            fcw = cst["fcw"].ap.rearrange("p (c j) -> p c j", j=3)
            fcb = cst["fcb"].ap
            fh3 = fhalo.ap.rearrange("p (c j) -> p c j", j=2)
            cn = [0]

            def conv(ci, ps):
                i = cn[0] % 4
                cn[0] += 1
                h_, t_ = hb[i], tc_[i]
                k.copy(h_.ap[:, 0:2], fh3[:, ci, :], [fhalo], [h_], e="dve")
                k.copy(h_.ap[:, 2:2 + n], ps.ap[:, 0:n], [ps], [h_])
                if halo_only:
                    k.ts(fh3[:, ci, :], h_.ap[:, n:n + 2], cst["valid"].ap[:, 0:1], None, ALU.mult, None,
                         [h_, cst["valid"]], [fhalo])
                    return None
                k.copy(fh3[:, ci, :], h_.ap[:, n:n + 2], [h_], [fhalo], e="dve")
                k.ts(t_.ap[:, 0:n], h_.ap[:, 0:n], fcw[:, ci, 0:1], fcb[:, ci:ci + 1], ALU.mult, ALU.add,
                     [h_, cst["fcw"], cst["fcb"]], [t_])
                for j in (1, 2):
                    k.stt(t_.ap[:, 0:n], h_.ap[:, j:j + n], fcw[:, ci, j:j + 1], t_.ap[:, 0:n], ALU.mult, ALU.add,
                          [h_, cst["fcw"], t_], [t_])
                return t_

            for j0 in range(0, NFF, 2):
                npair = min(2, NFF - j0)
                slot, (wg, wu) = ws.next("up%d" % j0)
                for jj in range(npair):
                    j = j0 + jj
                    psg = k.ps()
                    for kc in range(KC):
                        k.mm(psg.ap[:, 0:n], wg[:, kc, jj * P:(jj + 1) * P], x_bf3[:, kc, 0:n], kc == 0, kc == KC - 1,
                             [slot, x_bf], [psg])
                    psu = k.ps()
                    for kc in range(KC):
                        k.mm(psu.ap[:, 0:n], wu[:, kc, jj * P:(jj + 1) * P], x_bf3[:, kc, 0:n], kc == 0, kc == KC - 1,
                             [slot, x_bf], [psu])
                    tg = conv(j, psg)
                    tu = conv(NFF + j, psu)
                    if not halo_only:
                        k.act(tg.ap[:, 0:n], tg.ap[:, 0:n], AF.Silu, [tg], [tg])
                        k.tt(a3[:, j, 0:n], tg.ap[:, 0:n], tu.ap[:, 0:n], ALU.mult, [tg, tu], [a])
            if not halo_only:
                for c in range(16):
                    slot, (wv,) = ws.next("dn%d" % c)
                    ps = k.ps()
                    for kc in range(NFF):
                        k.mm(ps.ap[:, 0:n], wv[:, kc, :], a3[:, kc, 0:n], kc == 0, kc == NFF - 1, [slot, a], [ps])
                    resid_add(c, ps, n)
            k.release(mk, [a] + hb + tc_)

        def do_tap(name, t, ncols):
            if tap is not None and tap[0] == name:
                k.dma("sp", "out", dbg[:, 0:ncols], t, R=[x_f, x_bf, y_swaT, y_gdnT, S, kTd, Vd])

        cst_eps6 = k.alloc(1)
        cst_eps5 = k.alloc(1)
        cst_one = k.alloc(1)
        k.op("dve", lambda g: g.memset(cst_eps6.ap, 1e-6), W=[cst_eps6])
        k.op("dve", lambda g: g.memset(cst_eps5.ap, EPS), W=[cst_eps5])
        k.op("dve", lambda g: g.memset(cst_one.ap, 1.0), W=[cst_one])

        def ck(name):
            if stop == name:
                raise _Stop()

        def body():
            ck("setup")
            run_all()

        def run_all():
          for ti, (c0, n) in enumerate(pre_tiles):
            load_x(c0, n, False)
            ck("pre_load")
            if ti == len(pre_tiles) - 1:
                swa_kv(n)
                swa_rotate(n)
                ck("pre_kv")
            gdn(n, False)
            ck("pre_gdn")
          for ti, (c0, n) in enumerate(main_tiles):
            halo = ti == 0
            load_x(c0, n, True)
            swa_kv(n)
            ck("kv")
            swa_attn(n, ti == 1)
            swa_rotate(n)
            ck("attn")
            if ti == 1:
                do_tap("yswa", y_swaT.ap.bitcast(F32)[:, 0:4 * TMAX], 4 * TMAX)
            gdn(n, True)
            ck("gdn")
            if ti == 1:
                do_tap("ygdn", y_gdnT.ap.bitcast(F32)[:, 0:4 * TMAX], 4 * TMAX)
                do_tap("S", S[0].ap, 512)
            merge_and_mix(n)
            ck("merge")
            layer_norm(n, "ln1g", "ln1b", True)
            ck("ln1")
            if ti == 1:
                do_tap("x1", x_f[0].ap, TMAX)
            mem_attn(n)
            ck("mem")
            layer_norm(n, "ln2g", "ln2b", True)
            if ti == 1:
                do_tap("x2", x_f[0].ap, TMAX)
            ffn(n, halo)
            ck("ffn")
            if not halo:
                layer_norm(n, "ln3g", "ln3b", False)
                oc0 = c0 - ntok_own
                k.dma("sp", "out", yT[:, oc0:oc0 + n].rearrange("(c p) t -> p c t", p=P), xf3[:, :, 0:n], R=x_f)

        try:
            body()
            assert ws.taken == len(groups), (ws.taken, len(groups))
        except _Stop:
            pass
        k.wait_all("sp")
        k.wait_all("act")
        build.stats = dict(ninstr=k.ninstr, peak_kb=k.peak / 1024.0)
    return nc


def _t5_bucket(dist):
    d = np.maximum(dist, 1).astype(np.float32)
    large = 16 + (np.log(d / np.float32(16)) / np.float32(np.log(128 / 16)) * np.float32(16)).astype(np.int32)
    large = np.minimum(large, 31)
    return np.where(dist < 16, dist, large)


def _pc(v, n):
    return np.ascontiguousarray(np.asarray(v, np.float32).reshape(n, P).T)


def prepare_inputs(inp, ntok_own, n_batch):
    f = lambda a: np.ascontiguousarray(np.asarray(a, dtype=np.float32))
    w_in = f(inp["w_in"][0])
    sk = w_in[:, C_SK:C_SK + 128]
    sv = w_in[:, C_SV:C_SV + 128]
    w_skv = np.concatenate([sk[:, 0:64], sk[:, 0:64], sk[:, 64:128], sk[:, 64:128],
                            sv[:, 0:64], sv[:, 0:64], sv[:, 64:128], sv[:, 64:128]], axis=1)
    rel_bias = f(inp["rel_bias"])
    kk = np.arange(P)[:, None]
    qq = np.arange(P)[None, :]
    biasT = np.zeros((2, P, 16, P), np.float32)
    maskneg = np.zeros((2, P, P), np.float32)
    for kb in range(2):
        dist = qq - kk + (P if kb == 0 else 0)
        inwin = (dist >= 0) & (dist < 128)
        bkt = _t5_bucket(np.maximum(dist, 0))
        g = rel_bias[bkt]
        g = np.where(inwin[:, :, None], g, np.float32(0))
        biasT[kb] = np.transpose(g, (0, 2, 1))
        maskneg[kb] = np.where(inwin, np.float32(0), np.float32(-30000.0))
    gcw = f(inp["gdn_conv_w"][0])
    gcw_l = np.ascontiguousarray(np.transpose(gcw.reshape(4, 24, P), (2, 1, 0))).reshape(P, 96)
    fw = f(inp["ffn_conv_w"][0])
    fcw_l = np.ascontiguousarray(np.transpose(fw.reshape(3, 86, P), (2, 1, 0))).reshape(P, 258)
    sinks = f(inp["swa_sinks"][0])
    sinkT = np.zeros((P, 8), np.float32)
    for c in range(8):
        sinkT[0:64, c] = sinks[2 * c]
        sinkT[64:128, c] = sinks[2 * c + 1]
    ii = np.arange(P)
    common = {
        "w_in": w_in, "w_skv": np.ascontiguousarray(w_skv),
        "w_brs": f(inp["w_br_swa"][0]), "w_brg": f(inp["w_br_gdn"][0]), "w_mix": f(inp["w_mix_o"][0]),
        "w_mq": f(inp["w_mem_q"][0]), "w_mkv": f(inp["w_mem_kv"][0]), "w_mo": f(inp["w_mem_o"][0]),
        "w_up": f(inp["w_up"][0]), "w_dn": f(inp["w_down"][0]),
        "biasT": biasT.reshape(2, P, 16 * P), "maskneg": maskneg,
        "ln1g": _pc(inp["ln1_g"][0], 16), "ln1b": _pc(inp["ln1_b"][0], 16),
        "ln2g": _pc(inp["ln2_g"][0], 16), "ln2b": _pc(inp["ln2_b"][0], 16),
        "ln3g": _pc(inp["ln3_g"][0], 16), "ln3b": _pc(inp["ln3_b"][0], 16),
        "gcw": gcw_l, "fcw": fcw_l, "fcb": _pc(inp["ffn_conv_b"][0], 86), "sinkT": sinkT,
        "alog": np.ascontiguousarray(np.broadcast_to(f(inp["gdn_a_log"][0])[None, :], (P, 8))),
        "dtb": np.ascontiguousarray(np.broadcast_to(f(inp["gdn_dt_bias"][0])[None, :], (P, 8))),
        "normw": np.ascontiguousarray(np.broadcast_to(f(inp["gdn_norm_w"][0])[None, :], (P, P))),
        "ident": np.eye(P, dtype=np.float32),
        "tril": (ii[:, None] >= ii[None, :]).astype(np.float32),
        "strict": (ii[:, None] > ii[None, :]).astype(np.float32),
        "triu": (ii[:, None] <= ii[None, :]).astype(np.float32),
    }
    x = np.asarray(inp["x"], np.float32)
    mem = np.asarray(inp["mem"], np.float32)
    maps = []
    for b in range(n_batch):
        xbT = np.ascontiguousarray(x[b].T)
        memT = np.ascontiguousarray(mem[b].T)
        for h in range(2):
            m = dict(common)
            if h == 0:
                xl = np.concatenate([np.zeros((D, ntok_own), np.float32), xbT[:, 0:ntok_own]], axis=1)
            else:
                xl = xbT[:, 0:2 * ntok_own]
            m["xT"] = np.ascontiguousarray(xl)
            m["memT"] = memT
            m["valid"] = np.full((P, 1), float(h), np.float32)
            maps.append(m)
    return maps


def run(inp, ntok_own, n_batch, tap=None, stop=None):
    nc = build(ntok_own, tap, stop)
    maps = prepare_inputs(inp, ntok_own, n_batch)
    res = run_bass_kernel_spmd(nc, maps, core_ids=list(range(len(maps))))
    out = np.zeros((n_batch, 2 * ntok_own, D), np.float32)
    for b in range(n_batch):
        for h in range(2):
            out[b, h * ntok_own:(h + 1) * ntok_own, :] = res.results[2 * b + h]["yT"].T
    dbg = [r.get("dbg") for r in res.results] if tap is not None else None
    return out, dbg


def kernel(**inputs):
    out, _ = run(inputs, 4096, 4)
    return out
```

```python
import numpy as np
import ml_dtypes
from contextlib import ExitStack
import concourse.bass as bass
import concourse.mybir as mybir
from concourse.bass_utils import run_bass_kernel_spmd

F32 = mybir.dt.float32
BF16 = mybir.dt.bfloat16
AF = mybir.ActivationFunctionType
ALU = mybir.AluOpType
AX = mybir.AxisListType

P = 128
D = 2048
KC = 16
IN_DIM = 9488
DFF = 5504
NFF = 43
ALPHA = 2.0 ** 0.25
EPS = 1e-5
TMAX = 512
ARENA_KB = 200

C_SQ, C_SK, C_SV, C_GQ, C_GK, C_GV, C_GZ, C_GB, C_GA, C_GS, C_GG = (
    0, 1024, 1152, 1280, 2304, 3328, 4352, 5376, 5384, 5392, 7440)


class Buf:
    __slots__ = ("w", "r")

    def __init__(self, r=None):
        self.w = None
        self.r = dict(r) if r else {}


class T:
    def __init__(self, ap, bufs):
        self.ap = ap
        self.bufs = bufs

    def __getitem__(self, idx):
        return self.ap[idx]


def _bufs(lst):
    out = []
    for x in lst:
        if x is None:
            continue
        if isinstance(x, Buf):
            out.append(x)
        elif isinstance(x, T):
            out.extend(x.bufs)
        else:
            out.extend(_bufs(x))
    return out


class Kb:
    def __init__(self, nc, es):
        self.nc = nc
        self.eng = {"pe": nc.tensor, "dve": nc.vector, "act": nc.scalar, "pool": nc.gpsimd, "sp": nc.sync}
        self.sem = {}
        self.cnt = {}
        self.seen = {e: {} for e in self.eng}
        self.es = es
        for e in ("pe", "dve", "act", "pool"):
            self.newkey(e)
        self.arena = es.enter_context(nc.sbuf_tensor("arena", [P, ARENA_KB * 256], F32))
        self.off = 0
        self.free_deps = {}
        self.peak = 0
        self.psum = es.enter_context(nc.psum_tensor("psum", [P, 4096], F32))
        self.pbanks = [T(self.psum[:, i * 512:(i + 1) * 512], [Buf()]) for i in range(8)]
        self.pidx = 0
        self.ninstr = 0

    def newkey(self, name):
        self.sem[name] = self.es.enter_context(self.nc.semaphore(name))
        self.cnt[name] = 0

    def alloc(self, nelem, dtype=F32, nbuf=1):
        size = {F32: 4, BF16: 2}[dtype]
        nbytes = (nelem * size + 63) // 64 * 64
        assert self.off + nbytes <= ARENA_KB * 1024, ("SBUF arena overflow", self.off, nbytes)
        ap = self.arena[:, self.off // 4:(self.off + nbytes) // 4]
        if dtype != F32:
            ap = ap.bitcast(dtype)
        ap = ap[:, 0:nelem]
        self.off += nbytes
        self.peak = max(self.peak, self.off)
        return T(ap, [Buf(self.free_deps)])

    def mark(self):
        return self.off

    def release(self, mark, tiles):
        for b in _bufs(tiles):
            if b.w:
                self.free_deps[b.w[0]] = max(self.free_deps.get(b.w[0], 0), b.w[1])
            for k, v in b.r.items():
                self.free_deps[k] = max(self.free_deps.get(k, 0), v)
        self.off = mark

    def ps(self):
        t = self.pbanks[self.pidx]
        self.pidx = (self.pidx + 1) % 8
        return t

    def _wait(self, e, key, val):
        if key == e and e == "pe":
            return
        if self.seen[e].get(key, 0) >= val:
            return
        self.eng[e].wait_ge(self.sem[key], val)
        self.seen[e][key] = val

    def _deps(self, R, W):
        deps = {}
        for b in R:
            if b.w:
                deps[b.w[0]] = max(deps.get(b.w[0], 0), b.w[1])
        for b in W:
            if b.w:
                deps[b.w[0]] = max(deps.get(b.w[0], 0), b.w[1])
            for k, v in b.r.items():
                deps[k] = max(deps.get(k, 0), v)
        return deps

    def op(self, e, fn, R=(), W=(), inc=True):
        R = _bufs(R)
        W = _bufs(W)
        for k, v in self._deps(R, W).items():
            self._wait(e, k, v)
        ins = fn(self.eng[e])
        if inc:
            self.cnt[e] += 1
            ins.then_inc(self.sem[e], 1)
            seq = self.cnt[e]
        else:
            assert e == "pe"
            seq = self.cnt[e] + 1
        for b in R:
            b.r[e] = seq
        for b in W:
            b.w = (e, seq)
            b.r = {}
        self.ninstr += 1
        return ins

    def dma(self, q, key, out, in_, R=(), W=()):
        R = _bufs(R)
        W = _bufs(W)
        for k, v in self._deps(R, W).items():
            if k != key:
                self._wait(q, k, v)
        self.eng[q].dma_start(out=out, in_=in_).then_inc(self.sem[key], 16)
        self.cnt[key] += 16
        seq = self.cnt[key]
        for b in R:
            b.r[key] = seq
        for b in W:
            b.w = (key, seq)
            b.r = {}
        self.ninstr += 1

    def wait_all(self, e):
        for k, v in self.cnt.items():
            if v:
                self._wait(e, k, v)

    def tt(self, out, in0, in1, op, R, W, e="dve"):
        return self.op(e, lambda g: g.tensor_tensor(out=out, in0=in0, in1=in1, op=op), R, W)

    def ts(self, out, in0, s1, s2, op0, op1, R, W, e="dve"):
        if op1 is None:
            return self.op(e, lambda g: g.tensor_scalar(out=out, in0=in0, scalar1=s1, scalar2=None, op0=op0), R, W)
        return self.op(e, lambda g: g.tensor_scalar(out=out, in0=in0, scalar1=s1, scalar2=s2, op0=op0, op1=op1), R, W)

    def stt(self, out, in0, sc, in1, op0, op1, R, W):
        return self.op("dve", lambda g: g.scalar_tensor_tensor(out=out, in0=in0, scalar=sc, in1=in1, op0=op0, op1=op1), R, W)

    def act(self, out, in_, func, R, W, bias=None, scale=None):
        kw = {}
        if bias is not None:
            kw["bias"] = bias
        if scale is not None:
            kw["scale"] = scale
        return self.op("act", lambda g: g.activation(out=out, in_=in_, func=func, **kw), R, W)

    def copy(self, out, in_, R, W, e="act"):
        if e == "act":
            return self.op("act", lambda g: g.activation(out=out, in_=in_, func=AF.Copy), R, W)
        return self.op(e, lambda g: g.tensor_copy(out=out, in_=in_), R, W)

    def mm(self, out, lhsT, rhs, start, stop, R, W, inc=True):
        return self.op("pe", lambda g: g.matmul(out, lhsT=lhsT, rhs=rhs, start=start, stop=stop), R, W, inc=inc)

    def tr(self, out, in_, ident, R, W):
        return self.op("pe", lambda g: g.transpose(out, in_, ident), R, W)


class WStream:
    def __init__(self, k, groups, nslot=2, slot_elems=8192):
        self.k = k
        self.groups = groups
        self.nslot = nslot
        self.slots = [k.alloc(slot_elems, BF16) for _ in range(nslot)]
        for i in range(nslot):
            k.newkey("w%d" % i)
        self.issued = 0
        self.taken = 0

    def _issue(self):
        i = self.issued
        if i >= len(self.groups):
            return
        tag, parts = self.groups[i]
        s = i % self.nslot
        slot = self.slots[s]
        off = 0
        for (w, r0, kc, c0, nc_) in parts:
            dst = slot.ap[:, off:off + kc * nc_].rearrange("p (k n) -> p k n", k=kc)
            src = w[r0:r0 + kc * P, c0:c0 + nc_].rearrange("(k p) n -> p k n", p=P)
            self.k.dma("pool", "w%d" % s, dst, src, R=(), W=[slot])
            off += kc * nc_
        self.issued += 1

    def next(self, tag):
        while self.issued < min(self.taken + self.nslot, len(self.groups)):
            self._issue()
        gtag, parts = self.groups[self.taken]
        assert gtag == tag, (gtag, tag, self.taken)
        slot = self.slots[self.taken % self.nslot]
        self.taken += 1
        views = []
        off = 0
        for (w, r0, kc, c0, nc_) in parts:
            views.append(slot.ap[:, off:off + kc * nc_].rearrange("p (k n) -> p k n", k=kc))
            off += kc * nc_
        return slot, views

    def prefetch(self):
        while self.issued < min(self.taken + self.nslot, len(self.groups)):
            self._issue()


def tile_plan(ntok_own):
    pre = []
    c = 0
    end = ntok_own - P
    while c < end:
        n = min(TMAX, end - c)
        pre.append((c, n))
        c += n
    main = [(ntok_own - P, P)]
    c = ntok_own
    while c < 2 * ntok_own:
        n = min(TMAX, 2 * ntok_own - c)
        main.append((c, n))
        c += n
    return pre, main


class _Stop(Exception):
    pass


def build(ntok_own, tap=None, stop=None):
    nc = bass.Bass("TRN2", target_bir_lowering=False)
    NT = 2 * ntok_own

    def din(name, shape):
        return nc.dram_tensor(name, list(shape), F32, kind="ExternalInput").ap()

    xT = din("xT", (D, NT))
    memT = din("memT", (D, 256))
    w_in = din("w_in", (D, IN_DIM))
    w_skv = din("w_skv", (D, 512))
    w_brs = din("w_brs", (1024, D))
    w_brg = din("w_brg", (1024, D))
    w_mix = din("w_mix", (D, D))
    w_mq = din("w_mq", (D, 512))
    w_mkv = din("w_mkv", (D, 1024))
    w_mo = din("w_mo", (512, D))
    w_up = din("w_up", (D, 2 * DFF))
    w_dn = din("w_dn", (DFF, D))
    biasT_d = din("biasT", (2, P, 16 * P))
    maskneg_d = din("maskneg", (2, P, P))
    cvec = {}
    for name, n in (("ln1g", 16), ("ln1b", 16), ("ln2g", 16), ("ln2b", 16), ("ln3g", 16), ("ln3b", 16),
                    ("gcw", 96), ("fcw", 258), ("fcb", 86), ("sinkT", 8), ("alog", 8), ("dtb", 8),
                    ("normw", 128), ("valid", 1), ("ident", 128), ("tril", 128), ("strict", 128), ("triu", 128)):
        cvec[name] = (din(name, (P, n)), n)
    yT = nc.dram_tensor("yT", [D, ntok_own], F32, kind="ExternalOutput").ap()
    dbg = None
    if tap is not None:
        dbg = nc.dram_tensor("dbg", [P, tap[1]], F32, kind="ExternalOutput").ap()

    pre_tiles, main_tiles = tile_plan(ntok_own)

    with ExitStack() as es:
        k = Kb(nc, es)
        for key in ("cst", "xf", "xb", "out", "mem", "bm"):
            k.newkey(key)

        groups = []

        def G(tag, *parts):
            groups.append((tag, list(parts)))

        G("mk", (w_mkv, 0, KC, 0, 512))
        G("mv", (w_mkv, 0, KC, 512, 512))

        def gdn_kv_groups():
            for kind, c0 in (("k", C_GK), ("v", C_GV)):
                for hh in range(2):
                    G("g%s%d" % (kind, hh), (w_in, 0, KC, c0 + hh * 512, 512))
            G("gba", (w_in, 0, KC, C_GB, 16))

        for ti, (c0, n) in enumerate(pre_tiles):
            if ti == len(pre_tiles) - 1:
                G("skv", (w_skv, 0, KC, 0, 512))
            gdn_kv_groups()
        for ti, (c0, n) in enumerate(main_tiles):
            G("skv", (w_skv, 0, KC, 0, 512))
            G("sq0", (w_in, 0, KC, C_SQ, 512))
            G("sq1", (w_in, 0, KC, C_SQ + 512, 512))
            gdn_kv_groups()
            G("gq0", (w_in, 0, KC, C_GQ, 512))
            G("gq1", (w_in, 0, KC, C_GQ + 512, 512))
            G("gz0", (w_in, 0, KC, C_GZ, 512))
            G("gz1", (w_in, 0, KC, C_GZ + 512, 512))
            for og in range(8):
                G("ms%d" % og, (w_in, 0, KC, C_GS + og * 256, 256), (w_brs, 0, 8, og * 256, 256))
                G("mg%d" % og, (w_in, 0, KC, C_GG + og * 256, 256), (w_brg, 0, 8, og * 256, 256))
            for og in range(4):
                G("mix%d" % og, (w_mix, 0, KC, og * 512, 512))
            G("mq", (w_mq, 0, KC, 0, 512))
            G("mo", (w_mo, 0, 4, 0, 2048))
            for j0 in range(0, NFF, 2):
                npair = min(2, NFF - j0)
                G("up%d" % j0, (w_up, 0, KC, j0 * P, npair * P), (w_up, 0, KC, DFF + j0 * P, npair * P))
            if ti > 0:
                for c in range(16):
                    G("dn%d" % c, (w_dn, 0, NFF, c * P, P))

        cst = {}
        for name, (ap_d, n) in cvec.items():
            cst[name] = k.alloc(n)
            k.dma("sp", "cst", cst[name].ap, ap_d, W=[cst[name]])
        mneg = [k.alloc(P) for _ in range(2)]
        ident_bf = k.alloc(P, BF16)
        ones_bf = k.alloc(P, BF16)
        ones_f = k.alloc(P)
        esinkT = k.alloc(8)
        expA = k.alloc(8)
        KmT = k.alloc(4 * 256, BF16)
        Vm = [k.alloc(512, BF16) for _ in range(2)]
        S = [k.alloc(512) for _ in range(2)]
        ghalo = k.alloc(24 * 3)
        fhalo = k.alloc(86 * 2)
        kTd = [k.alloc(P + TMAX, BF16) for _ in range(2)]
        Vd = [k.alloc(2 * P, BF16) for _ in range(1 + TMAX // P)]
        xf_all = k.alloc(16 * TMAX)
        xf3 = xf_all.ap.rearrange("p (c t) -> p c t", c=16)
        x_f = [T(xf_all.ap[:, c * TMAX:(c + 1) * TMAX], [Buf()]) for c in range(16)]
        x_bf = k.alloc(16 * TMAX, BF16)
        x_bf3 = x_bf.ap.rearrange("p (c t) -> p c t", c=16)
        y_swaT = k.alloc(8 * TMAX, BF16)
        y_gdnT = k.alloc(8 * TMAX, BF16)
        ysw3 = y_swaT.ap.rearrange("p (c t) -> p c t", c=8)
        ygd3 = y_gdnT.ap.rearrange("p (c t) -> p c t", c=8)
        ws = WStream(k, groups, nslot=2)

        k.op("dve", lambda g: g.memset(ones_f.ap, 1.0), W=[ones_f])
        k.op("dve", lambda g: g.memset(ones_bf.ap, 1.0), W=[ones_bf])
        for t in S + [ghalo, fhalo] + kTd + Vd:
            k.op("dve", lambda g, t=t: g.memset(t.ap, 0.0), W=[t])
        for kb in range(2):
            k.dma("sp", "cst", mneg[kb].ap, maskneg_d[kb], W=[mneg[kb]])
        for b in _bufs(list(cst.values()) + mneg):
            b.w = ("cst", k.cnt["cst"])
        k.copy(ident_bf.ap, cst["ident"].ap, [cst["ident"]], [ident_bf], e="dve")
        k.act(esinkT.ap, cst["sinkT"].ap, AF.Exp, [cst["sinkT"]], [esinkT])
        k.act(expA.ap, cst["alog"].ap, AF.Exp, [cst["alog"]], [expA])
        m0 = k.mark()
        k.dma("pool", "mem", x_bf3[:, :, 0:256], memT.rearrange("(c p) t -> p c t", p=P), W=[x_bf])
        slot, (wv,) = ws.next("mk")
        for h in range(4):
            ps = k.ps()
            for kc in range(KC):
                k.mm(ps.ap[:, 0:256], wv[:, kc, h * P:(h + 1) * P], x_bf3[:, kc, 0:256], kc == 0, kc == KC - 1,
                     [slot, x_bf], [ps])
            k.copy(KmT.ap[:, h * 256:(h + 1) * 256], ps.ap[:, 0:256], [ps], [KmT])
        slot, (wv,) = ws.next("mv")
        for mb in range(2):
            ps = k.ps()
            for kc in range(KC):
                k.mm(ps.ap[:, 0:512], x_bf3[:, kc, mb * P:(mb + 1) * P], wv[:, kc, :], kc == 0, kc == KC - 1,
                     [slot, x_bf], [ps])
            k.copy(Vm[mb].ap, ps.ap, [ps], [Vm[mb]])
        k.release(m0, [])

        def load_x(c0, n, want_f32):
            k.dma("pool", "xb", x_bf3[:, :, 0:n], xT[:, c0:c0 + n].rearrange("(c p) t -> p c t", p=P), W=[x_bf])
            if want_f32:
                k.dma("sp", "xf", xf3[:, :, 0:n], xT[:, c0:c0 + n].rearrange("(c p) t -> p c t", p=P), W=x_f)

        def fm_linear(tag, act3, actbufs, kcn, n, consume, nchunks=4, view=0):
            slot, views = ws.next(tag)
            wv = views[view]
            for oc in range(nchunks):
                ps = k.ps()
                for kc in range(kcn):
                    k.mm(ps.ap[:, 0:n], wv[:, kc, oc * P:(oc + 1) * P], act3[:, kc, 0:n], kc == 0, kc == kcn - 1,
                         [slot] + actbufs, [ps], inc=(kc == kcn - 1))
                consume(oc, ps)
            return slot, views

        def swa_kv(n):
            nb = n // P
            slot, (wv,) = ws.next("skv")
            for g in range(2):
                ps = k.ps()
                for kc in range(KC):
                    k.mm(ps.ap[:, 0:n], wv[:, kc, g * P:(g + 1) * P], x_bf3[:, kc, 0:n], kc == 0, kc == KC - 1,
                         [slot, x_bf], [ps])
                k.copy(kTd[g].ap[:, P:P + n], ps.ap[:, 0:n], [ps], [kTd[g]])
            for blk in range(nb):
                ps = k.ps()
                for kc in range(KC):
                    k.mm(ps.ap[:, 0:256], x_bf3[:, kc, blk * P:(blk + 1) * P], wv[:, kc, 256:512], kc == 0,
                         kc == KC - 1, [slot, x_bf], [ps])
                k.copy(Vd[1 + blk].ap, ps.ap[:, 0:256], [ps], [Vd[1 + blk]], e="dve")

        def swa_rotate(n):
            nb = n // P
            for g in range(2):
                k.copy(kTd[g].ap[:, 0:P], kTd[g].ap[:, n:n + P], [kTd[g]], [kTd[g]], e="dve")
            k.copy(Vd[0].ap, Vd[nb].ap, [Vd[nb]], [Vd[0]], e="dve")

        def swa_attn(n, first_own):
            nb = n // P
            mk = k.mark()
            qT = k.alloc(8 * TMAX, BF16)
            q3 = qT.ap.rearrange("p (c t) -> p c t", c=8)
            tmp = [k.alloc(512) for _ in range(2)]
            PT = [[k.alloc(512, BF16) for _ in range(2)] for _ in range(2)]
            den = k.alloc(512)
            rden = k.alloc(512)
            biasm = [k.alloc(16 * P) for _ in range(2)]
            for kb in range(2):
                k.dma("sp", "bm", biasm[kb].ap, biasT_d[kb], W=[biasm[kb]])
            for b in _bufs(biasm):
                b.w = ("bm", k.cnt["bm"])
            for kb in range(2):
                b3 = biasm[kb].ap.rearrange("p (h q) -> p h q", h=16)
                k.tt(b3, b3, mneg[kb].ap.unsqueeze(1).to_broadcast([P, 16, P]), ALU.add, [biasm[kb], mneg[kb]], [biasm[kb]])
            for hh in range(2):
                fm_linear("sq%d" % hh, x_bf3, [x_bf], KC, n,
                          lambda oc, ps, hh=hh: k.copy(q3[:, hh * 4 + oc, 0:n], ps.ap[:, 0:n], [ps], [qT]))
            ti = 0
            for blk in range(nb):
                qs = slice(blk * P, (blk + 1) * P)
                for g in range(2):
                    pts = []
                    for par in range(2):
                        half = slice(par * 64, par * 64 + 64)
                        for kb in range(2):
                            ps = k.ps()
                            k.mm(ps.ap.rearrange("p (a b) -> p a b", a=4),
                                 kTd[g].ap[half, (blk + kb) * P:(blk + kb + 1) * P],
                                 q3[half, 4 * g:4 * g + 4, qs], True, True, [kTd[g], qT], [ps])
                            bm = biasm[kb].ap.rearrange("p (c two q) -> p c two q", two=2, q=P)[:, 4 * g:4 * g + 4, par, :]
                            t = tmp[ti % 2]
                            ti += 1
                            k.stt(t.ap.rearrange("p (a b) -> p a b", a=4), ps.ap.rearrange("p (a b) -> p a b", a=4),
                                  0.125, bm, ALU.mult, ALU.add, [ps, biasm[kb]], [t])
                            pt = PT[par][kb]
                            k.act(pt.ap, t.ap, AF.Exp, [t], [pt])
                            if first_own and blk == 0 and kb == 0:
                                k.ts(pt.ap, pt.ap, cst["valid"].ap[:, 0:1], None, ALU.mult, None, [pt, cst["valid"]], [pt])
                    for par in range(2):
                        half = slice(par * 64, par * 64 + 64)
                        psv = k.ps()
                        psd = k.ps()
                        for kb in range(2):
                            k.mm(psv.ap, Vd[blk + kb].ap[:, g * P:(g + 1) * P], PT[par][kb].ap, kb == 0, kb == 1,
                                 [Vd[blk + kb], PT[par][kb]], [psv])
                        for kb in range(2):
                            k.mm(psd.ap, ones_bf.ap, PT[par][kb].ap, kb == 0, kb == 1, [ones_bf, PT[par][kb]], [psd])
                        d3 = den.ap.rearrange("p (a b) -> p a b", a=4)
                        r3 = rden.ap.rearrange("p (a b) -> p a b", a=4)
                        k.tt(d3[half], psd.ap.rearrange("p (a b) -> p a b", a=4)[half],
                             esinkT.ap[half, 4 * g:4 * g + 4].unsqueeze(2).to_broadcast([64, 4, P]), ALU.add,
                             [psd, esinkT], [den])
                        k.op("dve", lambda e, half=half: e.reciprocal(out=r3[half], in_=d3[half]), [den], [rden])
                        k.tt(ysw3[half, 4 * g:4 * g + 4, qs], psv.ap.rearrange("p (a b) -> p a b", a=4)[half], r3[half],
                             ALU.mult, [psv, rden], [y_swaT])
            k.release(mk, [qT, den, rden] + tmp + PT[0] + PT[1] + biasm)

        def gdn(n, full):
            nb = n // P
            mk = k.mark()
            kTn = k.alloc(8 * TMAX, BF16)
            vTn = k.alloc(8 * TMAX, BF16)
            qTn = k.alloc(8 * TMAX, BF16) if full else None
            k3 = kTn.ap.rearrange("p (c t) -> p c t", c=8)
            v3 = vTn.ap.rearrange("p (c t) -> p c t", c=8)
            q3 = qTn.ap.rearrange("p (c t) -> p c t", c=8) if full else None
            mk_conv = None
            bgd = []
            for blk in range(nb):
                d = {}
                for nm in ("beta", "nbeta", "gpos", "gcp", "glast", "eg", "ekd", "gl", "bge", "tmp"):
                    d[nm] = k.alloc(8)
                bgd.append(d)
            mk_conv = k.mark()
            pre = [k.alloc(3 + TMAX) for _ in range(2)]
            acc = [k.alloc(TMAX) for _ in range(2)]
            sl = [k.alloc(TMAX) for _ in range(2)]
            sq = [k.alloc(TMAX) for _ in range(2)]
            rt = [k.alloc(TMAX) for _ in range(2)]
            gcw = cst["gcw"].ap.rearrange("p (c j) -> p c j", j=4)
            gh3 = ghalo.ap.rearrange("p (c j) -> p c j", j=3)
            cnt = [0]

            def conv_chunk(kind, h, ps):
                ci = {"q": 0, "k": 1, "v": 2}[kind] * 8 + h
                i = cnt[0] % 2
                cnt[0] += 1
                pr, ac, s_, sq_, rt_ = pre[i], acc[i], sl[i], sq[i], rt[i]
                k.copy(pr.ap[:, 0:3], gh3[:, ci, :], [ghalo], [pr], e="dve")
                k.copy(pr.ap[:, 3:3 + n], ps.ap[:, 0:n], [ps], [pr])
                k.copy(gh3[:, ci, :], pr.ap[:, n:n + 3], [pr], [ghalo], e="dve")
                k.ts(ac.ap[:, 0:n], pr.ap[:, 0:n], gcw[:, ci, 0:1], None, ALU.mult, None, [pr, cst["gcw"]], [ac])
                for j in range(1, 4):
                    k.stt(ac.ap[:, 0:n], pr.ap[:, j:j + n], gcw[:, ci, j:j + 1], ac.ap[:, 0:n], ALU.mult, ALU.add,
                          [pr, cst["gcw"], ac], [ac])
                if kind == "v":
                    k.act(v3[:, h, 0:n], ac.ap[:, 0:n], AF.Silu, [ac], [vTn])
                    return
                k.act(s_.ap[:, 0:n], ac.ap[:, 0:n], AF.Silu, [ac], [s_])
                k.act(sq_.ap[:, 0:n], s_.ap[:, 0:n], AF.Square, [s_], [sq_])
                psn = k.ps()
                k.mm(psn.ap[:, 0:n], ones_f.ap, sq_.ap[:, 0:n], True, True, [ones_f, sq_], [psn])
                k.act(rt_.ap[:, 0:n], psn.ap[:, 0:n], AF.Sqrt, [psn], [rt_], bias=cst_eps6.ap[:, 0:1])
                k.op("dve", lambda e: e.reciprocal(out=rt_.ap[:, 0:n], in_=rt_.ap[:, 0:n]), [rt_], [rt_])
                if kind == "k":
                    k.tt(k3[:, h, 0:n], s_.ap[:, 0:n], rt_.ap[:, 0:n], ALU.mult, [s_, rt_], [kTn])
                else:
                    k.stt(q3[:, h, 0:n], s_.ap[:, 0:n], float(P ** -0.5), rt_.ap[:, 0:n], ALU.mult, ALU.mult,
                          [s_, rt_], [qTn])

            for kind in ("k", "v"):
                for hh in range(2):
                    fm_linear("g%s%d" % (kind, hh), x_bf3, [x_bf], KC, n,
                              lambda oc, ps, kind=kind, hh=hh: conv_chunk(kind, hh * 4 + oc, ps))
            ck("g_conv")
            slot_ba, (wba,) = ws.next("gba")
            bg = []
            for blk in range(nb):
                ps = k.ps()
                for kc in range(KC):
                    k.mm(ps.ap[:, 0:16], x_bf3[:, kc, blk * P:(blk + 1) * P], wba[:, kc, :], kc == 0, kc == KC - 1,
                         [slot_ba, x_bf], [ps])
                d = bgd[blk]
                k.act(d["beta"].ap, ps.ap[:, 0:8], AF.Sigmoid, [ps], [d["beta"]])
                k.tt(d["tmp"].ap, ps.ap[:, 8:16], cst["dtb"].ap, ALU.add, [ps, cst["dtb"]], [d["tmp"]])
                k.act(d["tmp"].ap, d["tmp"].ap, AF.Exp, [d["tmp"]], [d["tmp"]])
                k.act(d["tmp"].ap, d["tmp"].ap, AF.Ln, [d["tmp"]], [d["tmp"]], bias=cst_one.ap[:, 0:1])
                k.tt(d["gpos"].ap, d["tmp"].ap, expA.ap, ALU.mult, [d["tmp"], expA], [d["gpos"]])
                k.ts(d["nbeta"].ap, d["beta"].ap, -1.0, None, ALU.mult, None, [d["beta"]], [d["nbeta"]])
                ps2 = k.ps()
                k.mm(ps2.ap[:, 0:8], cst["triu"].ap, d["gpos"].ap, True, True, [cst["triu"], d["gpos"]], [ps2])
                k.mm(ps2.ap[:, 8:16], ones_f.ap, d["gpos"].ap, True, True, [ones_f, d["gpos"]], [ps2])
                k.copy(d["gcp"].ap, ps2.ap[:, 0:8], [ps2], [d["gcp"]], e="dve")
                k.copy(d["glast"].ap, ps2.ap[:, 8:16], [ps2], [d["glast"]], e="dve")
                k.act(d["eg"].ap, d["gcp"].ap, AF.Exp, [d["gcp"]], [d["eg"]], scale=-1.0)
                k.act(d["gl"].ap, d["glast"].ap, AF.Exp, [d["glast"]], [d["gl"]], scale=-1.0)
                k.tt(d["tmp"].ap, d["gcp"].ap, d["glast"].ap, ALU.subtract, [d["gcp"], d["glast"]], [d["tmp"]])
                k.act(d["ekd"].ap, d["tmp"].ap, AF.Exp, [d["tmp"]], [d["ekd"]])
                k.tt(d["bge"].ap, d["beta"].ap, d["eg"].ap, ALU.mult, [d["beta"], d["eg"]], [d["bge"]])
                bg.append(d)
            ck("g_bg")
            if full:
                for hh in range(2):
                    fm_linear("gq%d" % hh, x_bf3, [x_bf], KC, n,
                              lambda oc, ps, hh=hh: conv_chunk("q", hh * 4 + oc, ps))

            k.release(mk_conv, pre + acc + sl + sq + rt)
            def a4(dt=F32):
                return k.alloc(512, dt)

            Ug = [k.alloc(P) for _ in range(2)]
            k_tok, v_tok, aiT, XTb, vbeta, kbg, wTb, qdT, kd, Sb, vnew = [a4(BF16) for _ in range(11)]
            NCH = 4
            chs = [(a4(), a4(), a4()) for _ in range(NCH)]
            c1s = [a4() for _ in range(2)]
            c2, u_sb = a4(), a4()
            o_sb = u_sb
            egB = c1s[0]
            ss = k.alloc(4)

            def v4(t):
                return t.ap.rearrange("p (a b) -> p a b", a=4)

            idf = cst["ident"].ap.unsqueeze(1).to_broadcast([P, 4, P])
            strict_b = cst["strict"].ap.unsqueeze(1).to_broadcast([P, 4, P])
            triu_b = cst["triu"].ap.unsqueeze(1).to_broadcast([P, 4, P])
            normw_b = cst["normw"].ap.unsqueeze(1).to_broadcast([P, 4, P])
            JS = [slice(j * P, (j + 1) * P) for j in range(4)]

            def make_psB(d, hg):
                psB = k.ps()
                for j in range(4):
                    h = 4 * hg + j
                    ug = Ug[j % 2]
                    k.ts(ug.ap, cst["triu"].ap, d["gpos"].ap[:, h:h + 1], None, ALU.mult, None,
                         [cst["triu"], d["gpos"]], [ug])
                    k.mm(psB.ap[:, JS[j]], ones_f.ap, ug.ap, True, True, [ones_f, ug], [psB])
                return psB

            chains = [(blk, hg) for blk in range(nb) for hg in range(2)]
            for b0 in range(0, len(chains), NCH):
                batch = chains[b0:b0 + NCH]
                for s_i, (blk, hg) in enumerate(batch):
                    d = bg[blk]
                    ts_ = slice(blk * P, (blk + 1) * P)
                    Nn, NT, XT = chs[s_i]
                    c1 = c1s[s_i % 2]
                    psB = make_psB(d, hg)
                    for j in range(4):
                        h = 4 * hg + j
                        k.ts(c1.ap[:, JS[j]], psB.ap[:, JS[j]], d["gcp"].ap[:, h:h + 1], 0.0,
                             ALU.subtract, ALU.min, [psB, d["gcp"]], [c1])
                    k.act(c1.ap, c1.ap, AF.Exp, [c1], [c1])
                    k.tt(v4(c1), v4(c1), strict_b, ALU.mult, [c1, cst["strict"]], [c1])
                    psK = k.ps()
                    for j in range(4):
                        h = 4 * hg + j
                        k.mm(psK.ap[:, JS[j]], k3[:, h, ts_], k3[:, h, ts_], True, True, [kTn], [psK])
                    for j in range(4):
                        h = 4 * hg + j
                        k.stt(Nn.ap[:, JS[j]], psK.ap[:, JS[j]], d["nbeta"].ap[:, h:h + 1],
                              c1.ap[:, JS[j]], ALU.mult, ALU.mult, [psK, d["nbeta"], c1], [Nn])
                    psT = k.ps()
                    for j in range(4):
                        k.mm(psT.ap[:, JS[j]], Nn.ap[:, JS[j]], cst["ident"].ap, True, True, [Nn, cst["ident"]], [psT])
                    k.copy(NT.ap, psT.ap, [psT], [NT])
                    k.tt(v4(XT), v4(NT), idf, ALU.add, [NT, cst["ident"]], [XT])
                for it in range(6):
                    pms = []
                    for s_i in range(len(batch)):
                        Nn, NT, XT = chs[s_i]
                        psM = k.ps()
                        for j in range(4):
                            k.mm(psM.ap[:, JS[j]], NT.ap[:, JS[j]], Nn.ap[:, JS[j]], True, True, [NT, Nn], [psM])
                        psMT = None
                        if it < 5:
                            psMT = k.ps()
                            for j in range(4):
                                k.mm(psMT.ap[:, JS[j]], Nn.ap[:, JS[j]], NT.ap[:, JS[j]], True, True, [NT, Nn], [psMT])
                        pms.append((psM, psMT))
                    for s_i in range(len(batch)):
                        Nn, NT, XT = chs[s_i]
                        psM, psMT = pms[s_i]
                        k.copy(Nn.ap, psM.ap, [psM], [Nn])
                        if it < 5:
                            k.copy(NT.ap, psMT.ap, [psMT], [NT], e="dve")
                    pxs = []
                    for s_i in range(len(batch)):
                        Nn, NT, XT = chs[s_i]
                        psX = k.ps()
                        for j in range(4):
                            k.mm(psX.ap[:, JS[j]], Nn.ap[:, JS[j]], XT.ap[:, JS[j]], True, True, [Nn, XT], [psX])
                        pxs.append(psX)
                    for s_i in range(len(batch)):
                        Nn, NT, XT = chs[s_i]
                        k.tt(XT.ap, XT.ap, pxs[s_i].ap, ALU.add, [XT, pxs[s_i]], [XT])
                for s_i, (blk, hg) in enumerate(batch):
                    d = bg[blk]
                    ts_ = slice(blk * P, (blk + 1) * P)
                    hs = slice(4 * hg, 4 * hg + 4)
                    Nn, NT, XT = chs[s_i]
                    k.copy(XTb.ap, XT.ap, [XT], [XTb])
                    if full:
                        psB = make_psB(d, hg)
                        for j in range(4):
                            h = 4 * hg + j
                            k.ts(c2.ap[:, JS[j]], psB.ap[:, JS[j]], d["gcp"].ap[:, h:h + 1], 0.0,
                                 ALU.subtract, ALU.max, [psB, d["gcp"]], [c2])
                        k.act(c2.ap, c2.ap, AF.Exp, [c2], [c2], scale=-1.0)
                        k.tt(v4(c2), v4(c2), triu_b, ALU.mult, [c2, cst["triu"]], [c2])
                        k.act(egB.ap, psB.ap, AF.Exp, [psB], [egB], scale=-1.0)
                        k.tt(v4(qdT), q3[:, hs, ts_], v4(egB), ALU.mult, [qTn, egB], [qdT])
                    for src3, dst in ((k3, k_tok), (v3, v_tok)):
                        pst = k.ps()
                        pstb = pst.ap.bitcast(BF16)
                        for j in range(4):
                            k.tr(pstb[:, JS[j]], src3[:, 4 * hg + j, ts_], ident_bf.ap, [kTn, vTn, ident_bf], [pst])
                        k.copy(dst.ap, pstb[:, 0:512], [pst], [dst])
                    if full:
                        psQ = k.ps()
                        for j in range(4):
                            h = 4 * hg + j
                            k.mm(psQ.ap[:, JS[j]], k3[:, h, ts_], q3[:, h, ts_], True, True, [kTn, qTn], [psQ])
                        k.tt(aiT.ap, psQ.ap, c2.ap, ALU.mult, [psQ, c2], [aiT])
                    for j in range(4):
                        h = 4 * hg + j
                        js = JS[j]
                        k.ts(vbeta.ap[:, js], v_tok.ap[:, js], d["beta"].ap[:, h:h + 1], None, ALU.mult, None,
                             [v_tok, d["beta"]], [vbeta])
                        k.ts(kbg.ap[:, js], k_tok.ap[:, js], d["bge"].ap[:, h:h + 1], None, ALU.mult, None,
                             [k_tok, d["bge"]], [kbg])
                        k.ts(kd.ap[:, js], k_tok.ap[:, js], d["ekd"].ap[:, h:h + 1], None, ALU.mult, None,
                             [k_tok, d["ekd"]], [kd])
                    psU = k.ps()
                    psW = k.ps()
                    for j in range(4):
                        k.mm(psU.ap[:, JS[j]], XTb.ap[:, JS[j]], vbeta.ap[:, JS[j]], True, True, [XTb, vbeta], [psU])
                    for j in range(4):
                        k.mm(psW.ap[:, JS[j]], kbg.ap[:, JS[j]], XTb.ap[:, JS[j]], True, True, [XTb, kbg], [psW])
                    k.copy(u_sb.ap, psU.ap, [psU], [u_sb])
                    k.copy(wTb.ap, psW.ap, [psW], [wTb], e="dve")
                    k.copy(Sb.ap, S[hg].ap, [S[hg]], [Sb])
                    psWS = k.ps()
                    for j in range(4):
                        k.mm(psWS.ap[:, JS[j]], wTb.ap[:, JS[j]], Sb.ap[:, JS[j]], True, True, [wTb, Sb], [psWS])
                    k.tt(vnew.ap, u_sb.ap, psWS.ap, ALU.subtract, [u_sb, psWS], [vnew])
                    if full:
                        psO = k.ps()
                        for j in range(4):
                            k.mm(psO.ap[:, JS[j]], qdT.ap[:, JS[j]], Sb.ap[:, JS[j]], True, False, [qdT, Sb], [psO])
                            k.mm(psO.ap[:, JS[j]], aiT.ap[:, JS[j]], vnew.ap[:, JS[j]], False, True, [aiT, vnew], [psO])
                    psS = k.ps()
                    for j in range(4):
                        k.mm(psS.ap[:, JS[j]], kd.ap[:, JS[j]], vnew.ap[:, JS[j]], True, True, [kd, vnew], [psS])
                    for j in range(4):
                        h = 4 * hg + j
                        k.stt(S[hg].ap[:, JS[j]], S[hg].ap[:, JS[j]], d["gl"].ap[:, h:h + 1], psS.ap[:, JS[j]],
                              ALU.mult, ALU.add, [S[hg], d["gl"], psS], [S[hg]])
                    if full:
                        k.copy(o_sb.ap, psO.ap, [psO], [o_sb])
                        k.tt(c2.ap, o_sb.ap, o_sb.ap, ALU.mult, [o_sb], [c2])
                        k.op("dve", lambda e: e.tensor_reduce(out=ss.ap, in_=v4(c2), axis=AX.X, op=ALU.add), [c2], [ss])
                        k.act(ss.ap, ss.ap, AF.Sqrt, [ss], [ss], bias=cst_eps6.ap[:, 0:1], scale=1.0 / P)
                        k.op("dve", lambda e: e.reciprocal(out=ss.ap, in_=ss.ap), [ss], [ss])
                        k.tt(v4(o_sb), v4(o_sb), ss.ap.unsqueeze(2).to_broadcast([P, 4, P]), ALU.mult, [o_sb, ss], [o_sb])
                        k.tt(v4(kbg), v4(o_sb), normw_b, ALU.mult, [o_sb, cst["normw"]], [kbg])
                        pst = k.ps()
                        pstb = pst.ap.bitcast(BF16)
                        for j in range(4):
                            k.tr(pstb[:, JS[j]], kbg.ap[:, JS[j]], ident_bf.ap, [kbg, ident_bf], [pst])
                        k.copy(ygd3[:, hs, ts_], pstb[:, 0:512].rearrange("p (a b) -> p a b", a=4), [pst], [y_gdnT])
            if full:
                zs = [k.alloc(TMAX) for _ in range(2)]
                zi = [0]

                def zgate(h, ps):
                    z = zs[zi[0] % 2]
                    zi[0] += 1
                    k.act(z.ap[:, 0:n], ps.ap[:, 0:n], AF.Silu, [ps], [z])
                    k.tt(ygd3[:, h, 0:n], ygd3[:, h, 0:n], z.ap[:, 0:n], ALU.mult, [y_gdnT, z], [y_gdnT])

                for hh in range(2):
                    fm_linear("gz%d" % hh, x_bf3, [x_bf], KC, n, lambda oc, ps, hh=hh: zgate(hh * 4 + oc, ps))
            tl = [kTn, vTn, qTn] + (zs if full else []) + Ug + [k_tok, v_tok, aiT, XTb, vbeta, kbg, wTb, qdT, kd, Sb,
                                                                  vnew, c2, u_sb, ss] + c1s + [t for c in chs for t in c]
            for d in bg:
                tl += list(d.values())
            k.release(mk, tl)

        def layer_norm(n, gname, bname, want_bf):
            mk = k.mark()
            sqs = [k.alloc(TMAX) for _ in range(2)]
            mean, rstd, nmr, t1 = [k.alloc(TMAX) for _ in range(4)]
            tn = [k.alloc(TMAX) for _ in range(2)]
            pss = k.ps()
            psq = k.ps()
            for c in range(16):
                s_ = sqs[c % 2]
                k.mm(pss.ap[:, 0:n], ones_f.ap, x_f[c].ap[:, 0:n], c == 0, c == 15, [ones_f, x_f[c]], [pss])
                k.act(s_.ap[:, 0:n], x_f[c].ap[:, 0:n], AF.Square, [x_f[c]], [s_])
                k.mm(psq.ap[:, 0:n], ones_f.ap, s_.ap[:, 0:n], c == 0, c == 15, [ones_f, s_], [psq])
            k.ts(mean.ap[:, 0:n], pss.ap[:, 0:n], 1.0 / D, None, ALU.mult, None, [pss], [mean])
            k.tt(t1.ap[:, 0:n], mean.ap[:, 0:n], mean.ap[:, 0:n], ALU.mult, [mean], [t1])
            k.stt(t1.ap[:, 0:n], psq.ap[:, 0:n], 1.0 / D, t1.ap[:, 0:n], ALU.mult, ALU.subtract, [psq, t1], [t1])
            k.act(rstd.ap[:, 0:n], t1.ap[:, 0:n], AF.Sqrt, [t1], [rstd], bias=cst_eps5.ap[:, 0:1])
            k.op("dve", lambda e: e.reciprocal(out=rstd.ap[:, 0:n], in_=rstd.ap[:, 0:n]), [rstd], [rstd])
            k.stt(nmr.ap[:, 0:n], mean.ap[:, 0:n], -1.0, rstd.ap[:, 0:n], ALU.mult, ALU.mult, [mean, rstd], [nmr])
            for c in range(16):
                t = tn[c % 2]
                k.tt(t.ap[:, 0:n], x_f[c].ap[:, 0:n], rstd.ap[:, 0:n], ALU.mult, [x_f[c], rstd], [t])
                k.tt(t.ap[:, 0:n], t.ap[:, 0:n], nmr.ap[:, 0:n], ALU.add, [t, nmr], [t])
                k.act(x_f[c].ap[:, 0:n], t.ap[:, 0:n], AF.Identity, [t, cst[gname], cst[bname]], [x_f[c]],
                      bias=cst[bname].ap[:, c:c + 1], scale=cst[gname].ap[:, c:c + 1])
                if want_bf:
                    k.copy(x_bf3[:, c, 0:n], x_f[c].ap[:, 0:n], [x_f[c]], [x_bf], e="dve")
            k.release(mk, sqs + [mean, rstd, nmr, t1] + tn)

        def resid_add(c, ps, n):
            k.stt(x_f[c].ap[:, 0:n], x_f[c].ap[:, 0:n], ALPHA, ps.ap[:, 0:n], ALU.mult, ALU.add, [x_f[c], ps], [x_f[c]])

        def merge_and_mix(n):
            mk = k.mark()
            mixed = k.alloc(16 * TMAX, BF16)
            mx3 = mixed.ap.rearrange("p (c t) -> p c t", c=16)
            m1 = [k.alloc(TMAX) for _ in range(4)]
            sg = [k.alloc(TMAX) for _ in range(2)]
            si = [0]
            for og in range(8):
                for br, ysrc, ybuf in (("ms", ysw3, y_swaT), ("mg", ygd3, y_gdnT)):
                    slot, (wg, wb) = ws.next("%s%d" % (br, og))
                    for oc in range(2):
                        psg = k.ps()
                        for kc in range(KC):
                            k.mm(psg.ap[:, 0:n], wg[:, kc, oc * P:(oc + 1) * P], x_bf3[:, kc, 0:n], kc == 0, kc == KC - 1,
                                 [slot, x_bf], [psg], inc=(kc == KC - 1))
                        psy = k.ps()
                        for kc in range(8):
                            k.mm(psy.ap[:, 0:n], wb[:, kc, oc * P:(oc + 1) * P], ysrc[:, kc, 0:n], kc == 0, kc == 7,
                                 [slot, ybuf], [psy], inc=(kc == 7))
                        s_ = sg[si[0] % 2]
                        si[0] += 1
                        k.act(s_.ap[:, 0:n], psg.ap[:, 0:n], AF.Sigmoid, [psg], [s_])
                        if br == "ms":
                            k.tt(m1[oc].ap[:, 0:n], s_.ap[:, 0:n], psy.ap[:, 0:n], ALU.mult, [s_, psy], [m1[oc]])
                        else:
                            k.tt(s_.ap[:, 0:n], s_.ap[:, 0:n], psy.ap[:, 0:n], ALU.mult, [s_, psy], [s_])
                            k.tt(mx3[:, og * 2 + oc, 0:n], s_.ap[:, 0:n], m1[oc].ap[:, 0:n], ALU.add, [s_, m1[oc]], [mixed])
            for og in range(4):
                fm_linear("mix%d" % og, mx3, [mixed], KC, n, lambda oc, ps, og=og: resid_add(og * 4 + oc, ps, n))
            k.release(mk, [mixed] + m1 + sg)

        def mem_attn(n):
            mk = k.mark()
            qm = k.alloc(4 * TMAX, BF16)
            om = k.alloc(4 * TMAX, BF16)
            qm3 = qm.ap.rearrange("p (c t) -> p c t", c=4)
            om3 = om.ap.rearrange("p (c t) -> p c t", c=4)
            PTm = [k.alloc(TMAX, BF16) for _ in range(2)]
            rd = k.alloc(TMAX)
            fm_linear("mq", x_bf3, [x_bf], KC, n, lambda oc, ps: k.copy(qm3[:, oc, 0:n], ps.ap[:, 0:n], [ps], [qm]))
            km3 = KmT.ap.rearrange("p (h m) -> p h m", h=4)
            for h in range(4):
                for mb in range(2):
                    ps = k.ps()
                    k.mm(ps.ap[:, 0:n], km3[:, h, mb * P:(mb + 1) * P], qm3[:, h, 0:n], True, True, [KmT, qm], [ps])
                    k.act(PTm[mb].ap[:, 0:n], ps.ap[:, 0:n], AF.Exp, [ps], [PTm[mb]], scale=float(P ** -0.5))
                pso = k.ps()
                psd = k.ps()
                for mb in range(2):
                    k.mm(pso.ap[:, 0:n], Vm[mb].ap[:, h * P:(h + 1) * P], PTm[mb].ap[:, 0:n], mb == 0, mb == 1,
                         [Vm[mb], PTm[mb]], [pso])
                for mb in range(2):
                    k.mm(psd.ap[:, 0:n], ones_bf.ap, PTm[mb].ap[:, 0:n], mb == 0, mb == 1, [ones_bf, PTm[mb]], [psd])
                k.op("dve", lambda e: e.reciprocal(out=rd.ap[:, 0:n], in_=psd.ap[:, 0:n]), [psd], [rd])
                k.tt(om3[:, h, 0:n], pso.ap[:, 0:n], rd.ap[:, 0:n], ALU.mult, [pso, rd], [om])
            slot, (wv,) = ws.next("mo")
            for c in range(16):
                ps = k.ps()
                for kc in range(4):
                    k.mm(ps.ap[:, 0:n], wv[:, kc, c * P:(c + 1) * P], om3[:, kc, 0:n], kc == 0, kc == 3, [slot, om], [ps], inc=(kc == 3))
                resid_add(c, ps, n)
            k.release(mk, [qm, om, rd] + PTm)

        def ffn(n, halo_only):
            mk = k.mark()
            a = None if halo_only else k.alloc(NFF * TMAX, BF16)
            a3 = None if halo_only else a.ap.rearrange("p (c t) -> p c t", c=NFF)
            hb = [k.alloc(2 + TMAX) for _ in range(4)]
            tc_ = [k.alloc(TMAX) for _ in range(4)]
            fcw = cst["fcw"].ap.rearrange("p (c j) -> p c j", j=3)
            fcb = cst["fcb"].ap
            fh3 = fhalo.ap.rearrange("p (c j) -> p c j", j=2)
            cn = [0]

            def conv(ci, ps):
                i = cn[0] % 4
                cn[0] += 1
                h_, t_ = hb[i], tc_[i]
                k.copy(h_.ap[:, 0:2], fh3[:, ci, :], [fhalo], [h_], e="dve")
                k.copy(h_.ap[:, 2:2 + n], ps.ap[:, 0:n], [ps], [h_])
                if halo_only:
                    k.ts(fh3[:, ci, :], h_.ap[:, n:n + 2], cst["valid"].ap[:, 0:1], None, ALU.mult, None,
                         [h_, cst["valid"]], [fhalo])
                    return None
                k.copy(fh3[:, ci, :], h_.ap[:, n:n + 2], [h_], [fhalo], e="dve")
                k.ts(t_.ap[:, 0:n], h_.ap[:, 0:n], fcw[:, ci, 0:1], fcb[:, ci:ci + 1], ALU.mult, ALU.add,
                     [h_, cst["fcw"], cst["fcb"]], [t_])
                for j in (1, 2):
                    k.stt(t_.ap[:, 0:n], h_.ap[:, j:j + n], fcw[:, ci, j:j + 1], t_.ap[:, 0:n], ALU.mult, ALU.add,
                          [h_, cst["fcw"], t_], [t_])
                return t_

            for j0 in range(0, NFF, 2):
                npair = min(2, NFF - j0)
                slot, (wg, wu) = ws.next("up%d" % j0)
                for jj in range(npair):
                    j = j0 + jj
                    psg = k.ps()
                    for kc in range(KC):
                        k.mm(psg.ap[:, 0:n], wg[:, kc, jj * P:(jj + 1) * P], x_bf3[:, kc, 0:n], kc == 0, kc == KC - 1,
                             [slot, x_bf], [psg], inc=(kc == KC - 1))
                    psu = k.ps()
                    for kc in range(KC):
                        k.mm(psu.ap[:, 0:n], wu[:, kc, jj * P:(jj + 1) * P], x_bf3[:, kc, 0:n], kc == 0, kc == KC - 1,
                             [slot, x_bf], [psu], inc=(kc == KC - 1))
                    tg = conv(j, psg)
                    tu = conv(NFF + j, psu)
                    if not halo_only:
                        k.act(tg.ap[:, 0:n], tg.ap[:, 0:n], AF.Silu, [tg], [tg])
                        k.tt(a3[:, j, 0:n], tg.ap[:, 0:n], tu.ap[:, 0:n], ALU.mult, [tg, tu], [a])
            if not halo_only:
                for c in range(16):
                    slot, (wv,) = ws.next("dn%d" % c)
                    ps = k.ps()
                    for kc in range(NFF):
                        k.mm(ps.ap[:, 0:n], wv[:, kc, :], a3[:, kc, 0:n], kc == 0, kc == NFF - 1, [slot, a], [ps], inc=(kc == NFF - 1))
                    resid_add(c, ps, n)
            k.release(mk, [a] + hb + tc_)

        def do_tap(name, t, ncols):
            if tap is not None and tap[0] == name:
                k.dma("sp", "out", dbg[:, 0:ncols], t, R=[x_f, x_bf, y_swaT, y_gdnT, S, kTd, Vd])

        cst_eps6 = k.alloc(1)
        cst_eps5 = k.alloc(1)
        cst_one = k.alloc(1)
        k.op("dve", lambda g: g.memset(cst_eps6.ap, 1e-6), W=[cst_eps6])
        k.op("dve", lambda g: g.memset(cst_eps5.ap, EPS), W=[cst_eps5])
        k.op("dve", lambda g: g.memset(cst_one.ap, 1.0), W=[cst_one])

        def ck(name):
            if stop == name:
                raise _Stop()

        def body():
            ck("setup")
            run_all()

        def run_all():
          for ti, (c0, n) in enumerate(pre_tiles):
            load_x(c0, n, False)
            ck("pre_load")
            if ti == len(pre_tiles) - 1:
                swa_kv(n)
                swa_rotate(n)
                ck("pre_kv")
            gdn(n, False)
            ck("pre_gdn")
          for ti, (c0, n) in enumerate(main_tiles):
            halo = ti == 0
            load_x(c0, n, True)
            swa_kv(n)
            ck("kv")
            swa_attn(n, ti == 1)
            swa_rotate(n)
            ck("attn")
            if ti == 1:
                do_tap("yswa", y_swaT.ap.bitcast(F32)[:, 0:4 * TMAX], 4 * TMAX)
            gdn(n, True)
            ck("gdn")
            if ti == 1:
                do_tap("ygdn", y_gdnT.ap.bitcast(F32)[:, 0:4 * TMAX], 4 * TMAX)
                do_tap("S", S[0].ap, 512)
            merge_and_mix(n)
            ck("merge")
            layer_norm(n, "ln1g", "ln1b", True)
            ck("ln1")
            if ti == 1:
                do_tap("x1", x_f[0].ap, TMAX)
            mem_attn(n)
            ck("mem")
            layer_norm(n, "ln2g", "ln2b", True)
            if ti == 1:
                do_tap("x2", x_f[0].ap, TMAX)
            ffn(n, halo)
            ck("ffn")
            if not halo:
                layer_norm(n, "ln3g", "ln3b", False)
                oc0 = c0 - ntok_own
                k.dma("sp", "out", yT[:, oc0:oc0 + n].rearrange("(c p) t -> p c t", p=P), xf3[:, :, 0:n], R=x_f)

        try:
            body()
            assert ws.taken == len(groups), (ws.taken, len(groups))
        except _Stop:
            pass
        k.wait_all("sp")
        k.wait_all("act")
        build.stats = dict(ninstr=k.ninstr, peak_kb=k.peak / 1024.0)
    return nc


def _t5_bucket(dist):
    d = np.maximum(dist, 1).astype(np.float32)
    large = 16 + (np.log(d / np.float32(16)) / np.float32(np.log(128 / 16)) * np.float32(16)).astype(np.int32)
    large = np.minimum(large, 31)
    return np.where(dist < 16, dist, large)


def _pc(v, n):
    return np.ascontiguousarray(np.asarray(v, np.float32).reshape(n, P).T)


def prepare_inputs(inp, ntok_own, n_batch):
    f = lambda a: np.ascontiguousarray(np.asarray(a, dtype=np.float32))
    w_in = f(inp["w_in"][0])
    sk = w_in[:, C_SK:C_SK + 128]
    sv = w_in[:, C_SV:C_SV + 128]
    w_skv = np.concatenate([sk[:, 0:64], sk[:, 0:64], sk[:, 64:128], sk[:, 64:128],
                            sv[:, 0:64], sv[:, 0:64], sv[:, 64:128], sv[:, 64:128]], axis=1)
    rel_bias = f(inp["rel_bias"])
    kk = np.arange(P)[:, None]
    qq = np.arange(P)[None, :]
    biasT = np.zeros((2, P, 16, P), np.float32)
    maskneg = np.zeros((2, P, P), np.float32)
    for kb in range(2):
        dist = qq - kk + (P if kb == 0 else 0)
        inwin = (dist >= 0) & (dist < 128)
        bkt = _t5_bucket(np.maximum(dist, 0))
        g = rel_bias[bkt]
        g = np.where(inwin[:, :, None], g, np.float32(0))
        biasT[kb] = np.transpose(g, (0, 2, 1))
        maskneg[kb] = np.where(inwin, np.float32(0), np.float32(-30000.0))
    gcw = f(inp["gdn_conv_w"][0])
    gcw_l = np.ascontiguousarray(np.transpose(gcw.reshape(4, 24, P), (2, 1, 0))).reshape(P, 96)
    fw = f(inp["ffn_conv_w"][0])
    fcw_l = np.ascontiguousarray(np.transpose(fw.reshape(3, 86, P), (2, 1, 0))).reshape(P, 258)
    sinks = f(inp["swa_sinks"][0])
    sinkT = np.zeros((P, 8), np.float32)
    for c in range(8):
        sinkT[0:64, c] = sinks[2 * c]
        sinkT[64:128, c] = sinks[2 * c + 1]
    ii = np.arange(P)
    common = {
        "w_in": w_in, "w_skv": np.ascontiguousarray(w_skv),
        "w_brs": f(inp["w_br_swa"][0]), "w_brg": f(inp["w_br_gdn"][0]), "w_mix": f(inp["w_mix_o"][0]),
        "w_mq": f(inp["w_mem_q"][0]), "w_mkv": f(inp["w_mem_kv"][0]), "w_mo": f(inp["w_mem_o"][0]),
        "w_up": f(inp["w_up"][0]), "w_dn": f(inp["w_down"][0]),
        "biasT": biasT.reshape(2, P, 16 * P), "maskneg": maskneg,
        "ln1g": _pc(inp["ln1_g"][0], 16), "ln1b": _pc(inp["ln1_b"][0], 16),
        "ln2g": _pc(inp["ln2_g"][0], 16), "ln2b": _pc(inp["ln2_b"][0], 16),
        "ln3g": _pc(inp["ln3_g"][0], 16), "ln3b": _pc(inp["ln3_b"][0], 16),
        "gcw": gcw_l, "fcw": fcw_l, "fcb": _pc(inp["ffn_conv_b"][0], 86), "sinkT": sinkT,
        "alog": np.ascontiguousarray(np.broadcast_to(f(inp["gdn_a_log"][0])[None, :], (P, 8))),
        "dtb": np.ascontiguousarray(np.broadcast_to(f(inp["gdn_dt_bias"][0])[None, :], (P, 8))),
        "normw": np.ascontiguousarray(np.broadcast_to(f(inp["gdn_norm_w"][0])[None, :], (P, P))),
        "ident": np.eye(P, dtype=np.float32),
        "tril": (ii[:, None] >= ii[None, :]).astype(np.float32),
        "strict": (ii[:, None] > ii[None, :]).astype(np.float32),
        "triu": (ii[:, None] <= ii[None, :]).astype(np.float32),
    }
    x = np.asarray(inp["x"], np.float32)
    mem = np.asarray(inp["mem"], np.float32)
    maps = []
    for b in range(n_batch):
        xbT = np.ascontiguousarray(x[b].T)
        memT = np.ascontiguousarray(mem[b].T)
        for h in range(2):
            m = dict(common)
            if h == 0:
                xl = np.concatenate([np.zeros((D, ntok_own), np.float32), xbT[:, 0:ntok_own]], axis=1)
            else:
                xl = xbT[:, 0:2 * ntok_own]
            m["xT"] = np.ascontiguousarray(xl)
            m["memT"] = memT
            m["valid"] = np.full((P, 1), float(h), np.float32)
            maps.append(m)
    return maps


def run(inp, ntok_own, n_batch, tap=None, stop=None):
    nc = build(ntok_own, tap, stop)
    maps = prepare_inputs(inp, ntok_own, n_batch)
    res = run_bass_kernel_spmd(nc, maps, core_ids=list(range(len(maps))))
    out = np.zeros((n_batch, 2 * ntok_own, D), np.float32)
    for b in range(n_batch):
        for h in range(2):
            out[b, h * ntok_own:(h + 1) * ntok_own, :] = res.results[2 * b + h]["yT"].T
    dbg = [r.get("dbg") for r in res.results] if tap is not None else None
    return out, dbg


def kernel(**inputs):
    out, _ = run(inputs, 4096, 4)
    return out
```

```python
import numpy as np
import ml_dtypes
from contextlib import ExitStack
import concourse.bass as bass
import concourse.mybir as mybir
from concourse.bass_utils import run_bass_kernel_spmd

F32 = mybir.dt.float32
BF16 = mybir.dt.bfloat16
AF = mybir.ActivationFunctionType
ALU = mybir.AluOpType
AX = mybir.AxisListType

P = 128
D = 2048
KC = 16
IN_DIM = 9488
DFF = 5504
NFF = 43
ALPHA = 2.0 ** 0.25
EPS = 1e-5
TMAX = 512
ARENA_KB = 200

C_SQ, C_SK, C_SV, C_GQ, C_GK, C_GV, C_GZ, C_GB, C_GA, C_GS, C_GG = (
    0, 1024, 1152, 1280, 2304, 3328, 4352, 5376, 5384, 5392, 7440)


class Buf:
    __slots__ = ("w", "r")

    def __init__(self, r=None):
        self.w = None
        self.r = dict(r) if r else {}


class T:
    def __init__(self, ap, bufs):
        self.ap = ap
        self.bufs = bufs

    def __getitem__(self, idx):
        return self.ap[idx]


def _bufs(lst):
    out = []
    for x in lst:
        if x is None:
            continue
        if isinstance(x, Buf):
            out.append(x)
        elif isinstance(x, T):
            out.extend(x.bufs)
        else:
            out.extend(_bufs(x))
    return out


class Kb:
    def __init__(self, nc, es):
        self.nc = nc
        self.eng = {"pe": nc.tensor, "dve": nc.vector, "act": nc.scalar, "pool": nc.gpsimd, "sp": nc.sync}
        self.sem = {}
        self.cnt = {}
        self.seen = {e: {} for e in self.eng}
        self.es = es
        for e in ("pe", "dve", "act", "pool"):
            self.newkey(e)
        self.arena = es.enter_context(nc.sbuf_tensor("arena", [P, ARENA_KB * 256], F32))
        self.off = 0
        self.free_deps = {}
        self.peak = 0
        self.psum = es.enter_context(nc.psum_tensor("psum", [P, 4096], F32))
        self.pbanks = [T(self.psum[:, i * 512:(i + 1) * 512], [Buf()]) for i in range(8)]
        self.pidx = 0
        self.ninstr = 0

    def newkey(self, name):
        self.sem[name] = self.es.enter_context(self.nc.semaphore(name))
        self.cnt[name] = 0

    def alloc(self, nelem, dtype=F32, nbuf=1):
        size = {F32: 4, BF16: 2}[dtype]
        nbytes = (nelem * size + 63) // 64 * 64
        assert self.off + nbytes <= ARENA_KB * 1024, ("SBUF arena overflow", self.off, nbytes)
        ap = self.arena[:, self.off // 4:(self.off + nbytes) // 4]
        if dtype != F32:
            ap = ap.bitcast(dtype)
        ap = ap[:, 0:nelem]
        self.off += nbytes
        self.peak = max(self.peak, self.off)
        return T(ap, [Buf(self.free_deps)])

    def mark(self):
        return self.off

    def release(self, mark, tiles):
        for b in _bufs(tiles):
            if b.w:
                self.free_deps[b.w[0]] = max(self.free_deps.get(b.w[0], 0), b.w[1])
            for k, v in b.r.items():
                self.free_deps[k] = max(self.free_deps.get(k, 0), v)
        self.off = mark

    def ps(self):
        t = self.pbanks[self.pidx]
        self.pidx = (self.pidx + 1) % 8
        return t

    def _wait(self, e, key, val):
        if key == e and e == "pe":
            return
        if self.seen[e].get(key, 0) >= val:
            return
        self.eng[e].wait_ge(self.sem[key], val)
        self.seen[e][key] = val

    def _deps(self, R, W):
        deps = {}
        for b in R:
            if b.w:
                deps[b.w[0]] = max(deps.get(b.w[0], 0), b.w[1])
        for b in W:
            if b.w:
                deps[b.w[0]] = max(deps.get(b.w[0], 0), b.w[1])
            for k, v in b.r.items():
                deps[k] = max(deps.get(k, 0), v)
        return deps

    def op(self, e, fn, R=(), W=(), inc=True):
        R = _bufs(R)
        W = _bufs(W)
        for k, v in self._deps(R, W).items():
            self._wait(e, k, v)
        ins = fn(self.eng[e])
        if inc:
            self.cnt[e] += 1
            ins.then_inc(self.sem[e], 1)
            seq = self.cnt[e]
        else:
            assert e == "pe"
            seq = self.cnt[e] + 1
        for b in R:
            b.r[e] = seq
        for b in W:
            b.w = (e, seq)
            b.r = {}
        self.ninstr += 1
        return ins

    def dma(self, q, key, out, in_, R=(), W=()):
        R = _bufs(R)
        W = _bufs(W)
        for k, v in self._deps(R, W).items():
            if k != key:
                self._wait(q, k, v)
        self.eng[q].dma_start(out=out, in_=in_).then_inc(self.sem[key], 16)
        self.cnt[key] += 16
        seq = self.cnt[key]
        for b in R:
            b.r[key] = seq
        for b in W:
            b.w = (key, seq)
            b.r = {}
        self.ninstr += 1

    def wait_all(self, e):
        for k, v in self.cnt.items():
            if v:
                self._wait(e, k, v)

    def tt(self, out, in0, in1, op, R, W, e="dve"):
        return self.op(e, lambda g: g.tensor_tensor(out=out, in0=in0, in1=in1, op=op), R, W)

    def ts(self, out, in0, s1, s2, op0, op1, R, W, e="dve"):
        if op1 is None:
            return self.op(e, lambda g: g.tensor_scalar(out=out, in0=in0, scalar1=s1, scalar2=None, op0=op0), R, W)
        return self.op(e, lambda g: g.tensor_scalar(out=out, in0=in0, scalar1=s1, scalar2=s2, op0=op0, op1=op1), R, W)

    def stt(self, out, in0, sc, in1, op0, op1, R, W):
        return self.op("dve", lambda g: g.scalar_tensor_tensor(out=out, in0=in0, scalar=sc, in1=in1, op0=op0, op1=op1), R, W)

    def act(self, out, in_, func, R, W, bias=None, scale=None):
        kw = {}
        if bias is not None:
            kw["bias"] = bias
        if scale is not None:
            kw["scale"] = scale
        return self.op("act", lambda g: g.activation(out=out, in_=in_, func=func, **kw), R, W)

    def copy(self, out, in_, R, W, e="act"):
        if e == "act":
            return self.op("act", lambda g: g.activation(out=out, in_=in_, func=AF.Copy), R, W)
        return self.op(e, lambda g: g.tensor_copy(out=out, in_=in_), R, W)

    def mm(self, out, lhsT, rhs, start, stop, R, W, inc=True):
        return self.op("pe", lambda g: g.matmul(out, lhsT=lhsT, rhs=rhs, start=start, stop=stop), R, W, inc=inc)

    def tr(self, out, in_, ident, R, W):
        return self.op("pe", lambda g: g.transpose(out, in_, ident), R, W)


class WStream:
    def __init__(self, k, groups, nslot=2, slot_elems=8192):
        self.k = k
        self.groups = groups
        self.nslot = nslot
        self.slots = [k.alloc(slot_elems, BF16) for _ in range(nslot)]
        for i in range(nslot):
            k.newkey("w%d" % i)
        self.issued = 0
        self.taken = 0

    def _issue(self):
        i = self.issued
        if i >= len(self.groups):
            return
        tag, parts = self.groups[i]
        s = i % self.nslot
        slot = self.slots[s]
        off = 0
        for (w, r0, kc, c0, nc_) in parts:
            dst = slot.ap[:, off:off + kc * nc_].rearrange("p (k n) -> p k n", k=kc)
            src = w[r0:r0 + kc * P, c0:c0 + nc_].rearrange("(k p) n -> p k n", p=P)
            self.k.dma("pool", "w%d" % s, dst, src, R=(), W=[slot])
            off += kc * nc_
        self.issued += 1

    def next(self, tag):
        while self.issued < min(self.taken + self.nslot, len(self.groups)):
            self._issue()
        gtag, parts = self.groups[self.taken]
        assert gtag == tag, (gtag, tag, self.taken)
        slot = self.slots[self.taken % self.nslot]
        self.taken += 1
        views = []
        off = 0
        for (w, r0, kc, c0, nc_) in parts:
            views.append(slot.ap[:, off:off + kc * nc_].rearrange("p (k n) -> p k n", k=kc))
            off += kc * nc_
        return slot, views

    def prefetch(self):
        while self.issued < min(self.taken + self.nslot, len(self.groups)):
            self._issue()


def tile_plan(ntok_own):
    pre = []
    c = 0
    end = ntok_own - P
    while c < end:
        n = min(TMAX, end - c)
        pre.append((c, n))
        c += n
    main = [(ntok_own - P, P)]
    c = ntok_own
    while c < 2 * ntok_own:
        n = min(TMAX, 2 * ntok_own - c)
        main.append((c, n))
        c += n
    return pre, main


class _Stop(Exception):
    pass


def build(ntok_own, tap=None, stop=None):
    nc = bass.Bass("TRN2", target_bir_lowering=False)
    NT = 2 * ntok_own

    def din(name, shape):
        return nc.dram_tensor(name, list(shape), F32, kind="ExternalInput").ap()

    xT = din("xT", (D, NT))
    memT = din("memT", (D, 256))
    w_in = din("w_in", (D, IN_DIM))
    w_skv = din("w_skv", (D, 512))
    w_brs = din("w_brs", (1024, D))
    w_brg = din("w_brg", (1024, D))
    w_mix = din("w_mix", (D, D))
    w_mq = din("w_mq", (D, 512))
    w_mkv = din("w_mkv", (D, 1024))
    w_mo = din("w_mo", (512, D))
    w_up = din("w_up", (D, 2 * DFF))
    w_dn = din("w_dn", (DFF, D))
    biasT_d = din("biasT", (2, P, 16 * P))
    maskneg_d = din("maskneg", (2, P, P))
    cvec = {}
    for name, n in (("ln1g", 16), ("ln1b", 16), ("ln2g", 16), ("ln2b", 16), ("ln3g", 16), ("ln3b", 16),
                    ("gcw", 96), ("fcw", 258), ("fcb", 86), ("sinkT", 8), ("alog", 8), ("dtb", 8),
                    ("normw", 128), ("valid", 1), ("ident", 128), ("tril", 128), ("strict", 128), ("triu", 128)):
        cvec[name] = (din(name, (P, n)), n)
    yT = nc.dram_tensor("yT", [D, ntok_own], F32, kind="ExternalOutput").ap()
    dbg = None
    if tap is not None:
        dbg = nc.dram_tensor("dbg", [P, tap[1]], F32, kind="ExternalOutput").ap()

    pre_tiles, main_tiles = tile_plan(ntok_own)

    with ExitStack() as es:
        k = Kb(nc, es)
        for key in ("cst", "xf", "xb", "out", "mem", "bm"):
            k.newkey(key)

        groups = []

        def G(tag, *parts):
            groups.append((tag, list(parts)))

        G("mk", (w_mkv, 0, KC, 0, 512))
        G("mv", (w_mkv, 0, KC, 512, 512))

        def gdn_kv_groups():
            for kind, c0 in (("k", C_GK), ("v", C_GV)):
                for hh in range(2):
                    G("g%s%d" % (kind, hh), (w_in, 0, KC, c0 + hh * 512, 512))
            G("gba", (w_in, 0, KC, C_GB, 16))

        for ti, (c0, n) in enumerate(pre_tiles):
            if ti == len(pre_tiles) - 1:
                G("skv", (w_skv, 0, KC, 0, 512))
            gdn_kv_groups()
        for ti, (c0, n) in enumerate(main_tiles):
            G("skv", (w_skv, 0, KC, 0, 512))
            G("sq0", (w_in, 0, KC, C_SQ, 512))
            G("sq1", (w_in, 0, KC, C_SQ + 512, 512))
            gdn_kv_groups()
            G("gq0", (w_in, 0, KC, C_GQ, 512))
            G("gq1", (w_in, 0, KC, C_GQ + 512, 512))
            G("gz0", (w_in, 0, KC, C_GZ, 512))
            G("gz1", (w_in, 0, KC, C_GZ + 512, 512))
            for og in range(8):
                G("ms%d" % og, (w_in, 0, KC, C_GS + og * 256, 256), (w_brs, 0, 8, og * 256, 256))
                G("mg%d" % og, (w_in, 0, KC, C_GG + og * 256, 256), (w_brg, 0, 8, og * 256, 256))
            for og in range(4):
                G("mix%d" % og, (w_mix, 0, KC, og * 512, 512))
            G("mq", (w_mq, 0, KC, 0, 512))
            G("mo", (w_mo, 0, 4, 0, 2048))
            for j0 in range(0, NFF, 2):
                npair = min(2, NFF - j0)
                G("up%d" % j0, (w_up, 0, KC, j0 * P, npair * P), (w_up, 0, KC, DFF + j0 * P, npair * P))
            if ti > 0:
                for c in range(16):
                    G("dn%d" % c, (w_dn, 0, NFF, c * P, P))

        cst = {}
        for name, (ap_d, n) in cvec.items():
            cst[name] = k.alloc(n)
            k.dma("sp", "cst", cst[name].ap, ap_d, W=[cst[name]])
        mneg = [k.alloc(P) for _ in range(2)]
        ident_bf = k.alloc(P, BF16)
        ones_bf = k.alloc(P, BF16)
        ones_f = k.alloc(P)
        esinkT = k.alloc(8)
        expA = k.alloc(8)
        KmT = k.alloc(4 * 256, BF16)
        Vm = [k.alloc(512, BF16) for _ in range(2)]
        S = [k.alloc(512) for _ in range(2)]
        ghalo = k.alloc(24 * 3)
        fhalo = k.alloc(86 * 2)
        kTd = [k.alloc(P + TMAX, BF16) for _ in range(2)]
        Vd = [k.alloc(2 * P, BF16) for _ in range(1 + TMAX // P)]
        xf_all = k.alloc(16 * TMAX)
        xf3 = xf_all.ap.rearrange("p (c t) -> p c t", c=16)
        x_f = [T(xf_all.ap[:, c * TMAX:(c + 1) * TMAX], [Buf()]) for c in range(16)]
        x_bf = k.alloc(16 * TMAX, BF16)
        x_bf3 = x_bf.ap.rearrange("p (c t) -> p c t", c=16)
        y_swaT = k.alloc(8 * TMAX, BF16)
        y_gdnT = k.alloc(8 * TMAX, BF16)
        ysw3 = y_swaT.ap.rearrange("p (c t) -> p c t", c=8)
        ygd3 = y_gdnT.ap.rearrange("p (c t) -> p c t", c=8)
        ws = WStream(k, groups, nslot=2)

        k.op("dve", lambda g: g.memset(ones_f.ap, 1.0), W=[ones_f])
        k.op("dve", lambda g: g.memset(ones_bf.ap, 1.0), W=[ones_bf])
        for t in S + [ghalo, fhalo] + kTd + Vd:
            k.op("dve", lambda g, t=t: g.memset(t.ap, 0.0), W=[t])
        for kb in range(2):
            k.dma("sp", "cst", mneg[kb].ap, maskneg_d[kb], W=[mneg[kb]])
        for b in _bufs(list(cst.values()) + mneg):
            b.w = ("cst", k.cnt["cst"])
        k.copy(ident_bf.ap, cst["ident"].ap, [cst["ident"]], [ident_bf], e="dve")
        k.act(esinkT.ap, cst["sinkT"].ap, AF.Exp, [cst["sinkT"]], [esinkT])
        k.act(expA.ap, cst["alog"].ap, AF.Exp, [cst["alog"]], [expA])
        m0 = k.mark()
        k.dma("pool", "mem", x_bf3[:, :, 0:256], memT.rearrange("(c p) t -> p c t", p=P), W=[x_bf])
        slot, (wv,) = ws.next("mk")
        for h in range(4):
            ps = k.ps()
            for kc in range(KC):
                k.mm(ps.ap[:, 0:256], wv[:, kc, h * P:(h + 1) * P], x_bf3[:, kc, 0:256], kc == 0, kc == KC - 1,
                     [slot, x_bf], [ps])
            k.copy(KmT.ap[:, h * 256:(h + 1) * 256], ps.ap[:, 0:256], [ps], [KmT])
        slot, (wv,) = ws.next("mv")
        for mb in range(2):
            ps = k.ps()
            for kc in range(KC):
                k.mm(ps.ap[:, 0:512], x_bf3[:, kc, mb * P:(mb + 1) * P], wv[:, kc, :], kc == 0, kc == KC - 1,
                     [slot, x_bf], [ps])
            k.copy(Vm[mb].ap, ps.ap, [ps], [Vm[mb]])
        k.release(m0, [])

        def load_x(c0, n, want_f32):
            k.dma("pool", "xb", x_bf3[:, :, 0:n], xT[:, c0:c0 + n].rearrange("(c p) t -> p c t", p=P), W=[x_bf])
            if want_f32:
                k.dma("sp", "xf", xf3[:, :, 0:n], xT[:, c0:c0 + n].rearrange("(c p) t -> p c t", p=P), W=x_f)

        def fm_linear(tag, act3, actbufs, kcn, n, consume, nchunks=4, view=0):
            slot, views = ws.next(tag)
            wv = views[view]
            for oc in range(nchunks):
                ps = k.ps()
                for kc in range(kcn):
                    k.mm(ps.ap[:, 0:n], wv[:, kc, oc * P:(oc + 1) * P], act3[:, kc, 0:n], kc == 0, kc == kcn - 1,
                         [slot] + actbufs, [ps], inc=(kc == kcn - 1))
                consume(oc, ps)
            return slot, views

        def swa_kv(n):
            nb = n // P
            slot, (wv,) = ws.next("skv")
            for g in range(2):
                ps = k.ps()
                for kc in range(KC):
                    k.mm(ps.ap[:, 0:n], wv[:, kc, g * P:(g + 1) * P], x_bf3[:, kc, 0:n], kc == 0, kc == KC - 1,
                         [slot, x_bf], [ps])
                k.copy(kTd[g].ap[:, P:P + n], ps.ap[:, 0:n], [ps], [kTd[g]])
            for blk in range(nb):
                ps = k.ps()
                for kc in range(KC):
                    k.mm(ps.ap[:, 0:256], x_bf3[:, kc, blk * P:(blk + 1) * P], wv[:, kc, 256:512], kc == 0,
                         kc == KC - 1, [slot, x_bf], [ps])
                k.copy(Vd[1 + blk].ap, ps.ap[:, 0:256], [ps], [Vd[1 + blk]], e="dve")

        def swa_rotate(n):
            nb = n // P
            for g in range(2):
                k.copy(kTd[g].ap[:, 0:P], kTd[g].ap[:, n:n + P], [kTd[g]], [kTd[g]], e="dve")
            k.copy(Vd[0].ap, Vd[nb].ap, [Vd[nb]], [Vd[0]], e="dve")

        def swa_attn(n, first_own):
            nb = n // P
            mk = k.mark()
            qT = k.alloc(8 * TMAX, BF16)
            q3 = qT.ap.rearrange("p (c t) -> p c t", c=8)
            tmp = [k.alloc(512) for _ in range(2)]
            PTs = [[[k.alloc(512, BF16) for _ in range(2)] for _ in range(2)] for _ in range(2)]
            den = k.alloc(512)
            rden = k.alloc(512)
            biasm = [k.alloc(16 * P) for _ in range(2)]
            for kb in range(2):
                k.dma("sp", "bm", biasm[kb].ap, biasT_d[kb], W=[biasm[kb]])
            for b in _bufs(biasm):
                b.w = ("bm", k.cnt["bm"])
            for kb in range(2):
                b3 = biasm[kb].ap.rearrange("p (h q) -> p h q", h=16)
                k.tt(b3, b3, mneg[kb].ap.unsqueeze(1).to_broadcast([P, 16, P]), ALU.add, [biasm[kb], mneg[kb]], [biasm[kb]])
            for hh in range(2):
                fm_linear("sq%d" % hh, x_bf3, [x_bf], KC, n,
                          lambda oc, ps, hh=hh: k.copy(q3[:, hh * 4 + oc, 0:n], ps.ap[:, 0:n], [ps], [qT]))
            tcnt = [0]
            units = [(blk, g) for blk in range(nb) for g in range(2)]

            def scores(u):
                blk, g = units[u]
                qs = slice(blk * P, (blk + 1) * P)
                PTu = PTs[u % 2]
                for par in range(2):
                    half = slice(par * 64, par * 64 + 64)
                    for kb in range(2):
                        ps = k.ps()
                        k.mm(ps.ap.rearrange("p (a b) -> p a b", a=4),
                             kTd[g].ap[half, (blk + kb) * P:(blk + kb + 1) * P],
                             q3[half, 4 * g:4 * g + 4, qs], True, True, [kTd[g], qT], [ps])
                        bm = biasm[kb].ap.rearrange("p (c two q) -> p c two q", two=2, q=P)[:, 4 * g:4 * g + 4, par, :]
                        t = tmp[tcnt[0] % 2]
                        tcnt[0] += 1
                        k.stt(t.ap.rearrange("p (a b) -> p a b", a=4), ps.ap.rearrange("p (a b) -> p a b", a=4),
                              0.125, bm, ALU.mult, ALU.add, [ps, biasm[kb]], [t])
                        pt = PTu[par][kb]
                        k.act(pt.ap, t.ap, AF.Exp, [t], [pt])
                        if first_own and blk == 0 and kb == 0:
                            k.ts(pt.ap, pt.ap, cst["valid"].ap[:, 0:1], None, ALU.mult, None, [pt, cst["valid"]], [pt])

            def pv(u):
                blk, g = units[u]
                qs = slice(blk * P, (blk + 1) * P)
                PTu = PTs[u % 2]
                for par in range(2):
                    half = slice(par * 64, par * 64 + 64)
                    psv = k.ps()
                    psd = k.ps()
                    for kb in range(2):
                        k.mm(psv.ap, Vd[blk + kb].ap[:, g * P:(g + 1) * P], PTu[par][kb].ap, kb == 0, kb == 1,
                             [Vd[blk + kb], PTu[par][kb]], [psv])
                    for kb in range(2):
                        k.mm(psd.ap, ones_bf.ap, PTu[par][kb].ap, kb == 0, kb == 1, [ones_bf, PTu[par][kb]], [psd])
                    d3 = den.ap.rearrange("p (a b) -> p a b", a=4)
                    r3 = rden.ap.rearrange("p (a b) -> p a b", a=4)
                    k.tt(d3[half], psd.ap.rearrange("p (a b) -> p a b", a=4)[half],
                         esinkT.ap[half, 4 * g:4 * g + 4].unsqueeze(2).to_broadcast([64, 4, P]), ALU.add,
                         [psd, esinkT], [den])
                    k.act(d3[half], d3[half], AF.Ln, [den], [den])
                    k.act(r3[half], d3[half], AF.Exp, [den], [rden], scale=-1.0)
                    k.tt(ysw3[half, 4 * g:4 * g + 4, qs], psv.ap.rearrange("p (a b) -> p a b", a=4)[half], r3[half],
                         ALU.mult, [psv, rden], [y_swaT])

            scores(0)
            for u in range(len(units)):
                if u + 1 < len(units):
                    scores(u + 1)
                pv(u)
            k.release(mk, [qT, den, rden] + tmp + PTs + biasm)

        def gdn(n, full):
            nb = n // P
            mk = k.mark()
            kTn = k.alloc(8 * TMAX, BF16)
            vTn = k.alloc(8 * TMAX, BF16)
            qTn = k.alloc(8 * TMAX, BF16) if full else None
            k3 = kTn.ap.rearrange("p (c t) -> p c t", c=8)
            v3 = vTn.ap.rearrange("p (c t) -> p c t", c=8)
            q3 = qTn.ap.rearrange("p (c t) -> p c t", c=8) if full else None
            mk_conv = None
            bgd = []
            for blk in range(nb):
                d = {}
                for nm in ("beta", "nbeta", "gpos", "gcp", "glast", "eg", "ekd", "gl", "bge", "tmp"):
                    d[nm] = k.alloc(8)
                bgd.append(d)
            mk_conv = k.mark()
            pre = [k.alloc(3 + TMAX) for _ in range(2)]
            acc = [k.alloc(TMAX) for _ in range(2)]
            sl = [k.alloc(TMAX) for _ in range(4)]
            sq = [k.alloc(TMAX) for _ in range(4)]
            rt = []
            gcw = cst["gcw"].ap.rearrange("p (c j) -> p c j", j=4)
            gh3 = ghalo.ap.rearrange("p (c j) -> p c j", j=3)
            cnt = [0]
            cnt4 = [0]
            pend = []

            def flushB():
                items = list(pend)
                del pend[:]
                pss = []
                for (kind, h, s_, sq_) in items:
                    psn = k.ps()
                    k.mm(psn.ap[:, 0:n], ones_f.ap, sq_.ap[:, 0:n], True, True, [ones_f, sq_], [psn])
                    pss.append(psn)
                for (kind, h, s_, sq_), psn in zip(items, pss):
                    k.act(sq_.ap[:, 0:n], psn.ap[:, 0:n], AF.Ln, [psn], [sq_], bias=cst_eps6.ap[:, 0:1])
                for (kind, h, s_, sq_) in items:
                    k.act(sq_.ap[:, 0:n], sq_.ap[:, 0:n], AF.Exp, [sq_], [sq_], scale=-0.5)
                for (kind, h, s_, sq_) in items:
                    if kind == "k":
                        k.tt(k3[:, h, 0:n], s_.ap[:, 0:n], sq_.ap[:, 0:n], ALU.mult, [s_, sq_], [kTn])
                    else:
                        k.stt(q3[:, h, 0:n], s_.ap[:, 0:n], float(P ** -0.5), sq_.ap[:, 0:n], ALU.mult, ALU.mult,
                              [s_, sq_], [qTn])

            def conv_chunk(kind, h, ps, first):
                if first:
                    flushB()
                ci = {"q": 0, "k": 1, "v": 2}[kind] * 8 + h
                i = cnt[0] % 2
                cnt[0] += 1
                pr, ac = pre[i], acc[i]
                k.copy(pr.ap[:, 0:3], gh3[:, ci, :], [ghalo], [pr], e="dve")
                k.copy(pr.ap[:, 3:3 + n], ps.ap[:, 0:n], [ps], [pr])
                k.copy(gh3[:, ci, :], pr.ap[:, n:n + 3], [pr], [ghalo], e="dve")
                k.ts(ac.ap[:, 0:n], pr.ap[:, 0:n], gcw[:, ci, 0:1], None, ALU.mult, None, [pr, cst["gcw"]], [ac])
                for j in range(1, 4):
                    k.stt(ac.ap[:, 0:n], pr.ap[:, j:j + n], gcw[:, ci, j:j + 1], ac.ap[:, 0:n], ALU.mult, ALU.add,
                          [pr, cst["gcw"], ac], [ac])
                if kind == "v":
                    k.act(v3[:, h, 0:n], ac.ap[:, 0:n], AF.Silu, [ac], [vTn])
                    return
                i4 = cnt4[0] % 4
                cnt4[0] += 1
                s_, sq_ = sl[i4], sq[i4]
                k.act(s_.ap[:, 0:n], ac.ap[:, 0:n], AF.Silu, [ac], [s_])
                k.act(sq_.ap[:, 0:n], s_.ap[:, 0:n], AF.Square, [s_], [sq_])
                pend.append((kind, h, s_, sq_))

            for kind in ("k", "v"):
                for hh in range(2):
                    fm_linear("g%s%d" % (kind, hh), x_bf3, [x_bf], KC, n,
                              lambda oc, ps, kind=kind, hh=hh: conv_chunk(kind, hh * 4 + oc, ps, oc == 0))
            flushB()
            ck("g_conv")
            slot_ba, (wba,) = ws.next("gba")
            bg = []
            for blk in range(nb):
                ps = k.ps()
                for kc in range(KC):
                    k.mm(ps.ap[:, 0:16], x_bf3[:, kc, blk * P:(blk + 1) * P], wba[:, kc, :], kc == 0, kc == KC - 1,
                         [slot_ba, x_bf], [ps])
                d = bgd[blk]
                k.act(d["beta"].ap, ps.ap[:, 0:8], AF.Sigmoid, [ps], [d["beta"]])
                k.tt(d["tmp"].ap, ps.ap[:, 8:16], cst["dtb"].ap, ALU.add, [ps, cst["dtb"]], [d["tmp"]])
                k.act(d["tmp"].ap, d["tmp"].ap, AF.Exp, [d["tmp"]], [d["tmp"]])
                k.act(d["tmp"].ap, d["tmp"].ap, AF.Ln, [d["tmp"]], [d["tmp"]], bias=cst_one.ap[:, 0:1])
                k.tt(d["gpos"].ap, d["tmp"].ap, expA.ap, ALU.mult, [d["tmp"], expA], [d["gpos"]])
                k.ts(d["nbeta"].ap, d["beta"].ap, -1.0, None, ALU.mult, None, [d["beta"]], [d["nbeta"]])
                ps2 = k.ps()
                k.mm(ps2.ap[:, 0:8], cst["triu"].ap, d["gpos"].ap, True, True, [cst["triu"], d["gpos"]], [ps2])
                k.mm(ps2.ap[:, 8:16], ones_f.ap, d["gpos"].ap, True, True, [ones_f, d["gpos"]], [ps2])
                k.copy(d["gcp"].ap, ps2.ap[:, 0:8], [ps2], [d["gcp"]], e="dve")
                k.copy(d["glast"].ap, ps2.ap[:, 8:16], [ps2], [d["glast"]], e="dve")
                k.act(d["eg"].ap, d["gcp"].ap, AF.Exp, [d["gcp"]], [d["eg"]], scale=-1.0)
                k.act(d["gl"].ap, d["glast"].ap, AF.Exp, [d["glast"]], [d["gl"]], scale=-1.0)
                k.tt(d["tmp"].ap, d["gcp"].ap, d["glast"].ap, ALU.subtract, [d["gcp"], d["glast"]], [d["tmp"]])
                k.act(d["ekd"].ap, d["tmp"].ap, AF.Exp, [d["tmp"]], [d["ekd"]])
                k.tt(d["bge"].ap, d["beta"].ap, d["eg"].ap, ALU.mult, [d["beta"], d["eg"]], [d["bge"]])
                bg.append(d)
            ck("g_bg")
            if full:
                for hh in range(2):
                    fm_linear("gq%d" % hh, x_bf3, [x_bf], KC, n,
                              lambda oc, ps, hh=hh: conv_chunk("q", hh * 4 + oc, ps, oc == 0))
                flushB()

            k.release(mk_conv, pre + acc + sl + sq + rt)
            def a4(dt=F32):
                return k.alloc(512, dt)

            Ug = [k.alloc(P) for _ in range(2)]
            k_tok, v_tok, aiT, XTb, vbeta, kbg, wTb, qdT, kd, Sb, vnew = [a4(BF16) for _ in range(11)]
            NCH = 4
            chs = [(a4(), a4(), a4()) for _ in range(NCH)]
            c1s = [a4() for _ in range(2)]
            c2, u_sb = a4(), a4()
            o_sb = u_sb
            egB = c1s[0]
            ss = k.alloc(4)

            def v4(t):
                return t.ap.rearrange("p (a b) -> p a b", a=4)

            idf = cst["ident"].ap.unsqueeze(1).to_broadcast([P, 4, P])
            strict_b = cst["strict"].ap.unsqueeze(1).to_broadcast([P, 4, P])
            triu_b = cst["triu"].ap.unsqueeze(1).to_broadcast([P, 4, P])
            normw_b = cst["normw"].ap.unsqueeze(1).to_broadcast([P, 4, P])
            JS = [slice(j * P, (j + 1) * P) for j in range(4)]

            def make_psB(d, hg):
                psB = k.ps()
                for j in range(4):
                    h = 4 * hg + j
                    ug = Ug[j % 2]
                    k.ts(ug.ap, cst["triu"].ap, d["gpos"].ap[:, h:h + 1], None, ALU.mult, None,
                         [cst["triu"], d["gpos"]], [ug])
                    k.mm(psB.ap[:, JS[j]], ones_f.ap, ug.ap, True, True, [ones_f, ug], [psB])
                return psB

            chains = [(blk, hg) for blk in range(nb) for hg in range(2)]
            for b0 in range(0, len(chains), NCH):
                batch = chains[b0:b0 + NCH]
                for s_i, (blk, hg) in enumerate(batch):
                    d = bg[blk]
                    ts_ = slice(blk * P, (blk + 1) * P)
                    Nn, NT, XT = chs[s_i]
                    c1 = c1s[s_i % 2]
                    psB = make_psB(d, hg)
                    for j in range(4):
                        h = 4 * hg + j
                        k.ts(c1.ap[:, JS[j]], psB.ap[:, JS[j]], d["gcp"].ap[:, h:h + 1], 0.0,
                             ALU.subtract, ALU.min, [psB, d["gcp"]], [c1])
                    k.act(c1.ap, c1.ap, AF.Exp, [c1], [c1])
                    k.tt(v4(c1), v4(c1), strict_b, ALU.mult, [c1, cst["strict"]], [c1])
                    psK = k.ps()
                    for j in range(4):
                        h = 4 * hg + j
                        k.mm(psK.ap[:, JS[j]], k3[:, h, ts_], k3[:, h, ts_], True, True, [kTn], [psK])
                    for j in range(4):
                        h = 4 * hg + j
                        k.stt(Nn.ap[:, JS[j]], psK.ap[:, JS[j]], d["nbeta"].ap[:, h:h + 1],
                              c1.ap[:, JS[j]], ALU.mult, ALU.mult, [psK, d["nbeta"], c1], [Nn])
                    psT = k.ps()
                    for j in range(4):
                        k.mm(psT.ap[:, JS[j]], Nn.ap[:, JS[j]], cst["ident"].ap, True, True, [Nn, cst["ident"]], [psT])
                    k.copy(NT.ap, psT.ap, [psT], [NT])
                    k.tt(v4(XT), v4(NT), idf, ALU.add, [NT, cst["ident"]], [XT])
                for it in range(6):
                    pms = []
                    for s_i in range(len(batch)):
                        Nn, NT, XT = chs[s_i]
                        psM = k.ps()
                        for j in range(4):
                            k.mm(psM.ap[:, JS[j]], NT.ap[:, JS[j]], Nn.ap[:, JS[j]], True, True, [NT, Nn], [psM])
                        psMT = None
                        if it < 5:
                            psMT = k.ps()
                            for j in range(4):
                                k.mm(psMT.ap[:, JS[j]], Nn.ap[:, JS[j]], NT.ap[:, JS[j]], True, True, [NT, Nn], [psMT])
                        pms.append((psM, psMT))
                    for s_i in range(len(batch)):
                        Nn, NT, XT = chs[s_i]
                        psM, psMT = pms[s_i]
                        k.copy(Nn.ap, psM.ap, [psM], [Nn])
                        if it < 5:
                            k.copy(NT.ap, psMT.ap, [psMT], [NT], e="dve")
                    pxs = []
                    for s_i in range(len(batch)):
                        Nn, NT, XT = chs[s_i]
                        psX = k.ps()
                        for j in range(4):
                            k.mm(psX.ap[:, JS[j]], Nn.ap[:, JS[j]], XT.ap[:, JS[j]], True, True, [Nn, XT], [psX])
                        pxs.append(psX)
                    for s_i in range(len(batch)):
                        Nn, NT, XT = chs[s_i]
                        k.tt(XT.ap, XT.ap, pxs[s_i].ap, ALU.add, [XT, pxs[s_i]], [XT])
                for s_i, (blk, hg) in enumerate(batch):
                    d = bg[blk]
                    ts_ = slice(blk * P, (blk + 1) * P)
                    hs = slice(4 * hg, 4 * hg + 4)
                    Nn, NT, XT = chs[s_i]
                    k.copy(XTb.ap, XT.ap, [XT], [XTb])
                    if full:
                        psB = make_psB(d, hg)
                        for j in range(4):
                            h = 4 * hg + j
                            k.ts(c2.ap[:, JS[j]], psB.ap[:, JS[j]], d["gcp"].ap[:, h:h + 1], 0.0,
                                 ALU.subtract, ALU.max, [psB, d["gcp"]], [c2])
                        k.act(c2.ap, c2.ap, AF.Exp, [c2], [c2], scale=-1.0)
                        k.tt(v4(c2), v4(c2), triu_b, ALU.mult, [c2, cst["triu"]], [c2])
                        k.act(egB.ap, psB.ap, AF.Exp, [psB], [egB], scale=-1.0)
                        k.tt(v4(qdT), q3[:, hs, ts_], v4(egB), ALU.mult, [qTn, egB], [qdT])
                    for src3, dst in ((k3, k_tok), (v3, v_tok)):
                        pst = k.ps()
                        pstb = pst.ap.bitcast(BF16)
                        for j in range(4):
                            k.tr(pstb[:, JS[j]], src3[:, 4 * hg + j, ts_], ident_bf.ap, [kTn, vTn, ident_bf], [pst])
                        k.copy(dst.ap, pstb[:, 0:512], [pst], [dst])
                    if full:
                        psQ = k.ps()
                        for j in range(4):
                            h = 4 * hg + j
                            k.mm(psQ.ap[:, JS[j]], k3[:, h, ts_], q3[:, h, ts_], True, True, [kTn, qTn], [psQ])
                        k.tt(aiT.ap, psQ.ap, c2.ap, ALU.mult, [psQ, c2], [aiT])
                    for j in range(4):
                        h = 4 * hg + j
                        js = JS[j]
                        k.ts(vbeta.ap[:, js], v_tok.ap[:, js], d["beta"].ap[:, h:h + 1], None, ALU.mult, None,
                             [v_tok, d["beta"]], [vbeta])
                        k.ts(kbg.ap[:, js], k_tok.ap[:, js], d["bge"].ap[:, h:h + 1], None, ALU.mult, None,
                             [k_tok, d["bge"]], [kbg])
                        k.ts(kd.ap[:, js], k_tok.ap[:, js], d["ekd"].ap[:, h:h + 1], None, ALU.mult, None,
                             [k_tok, d["ekd"]], [kd])
                    psU = k.ps()
                    psW = k.ps()
                    for j in range(4):
                        k.mm(psU.ap[:, JS[j]], XTb.ap[:, JS[j]], vbeta.ap[:, JS[j]], True, True, [XTb, vbeta], [psU])
                    for j in range(4):
                        k.mm(psW.ap[:, JS[j]], kbg.ap[:, JS[j]], XTb.ap[:, JS[j]], True, True, [XTb, kbg], [psW])
                    k.copy(u_sb.ap, psU.ap, [psU], [u_sb])
                    k.copy(wTb.ap, psW.ap, [psW], [wTb], e="dve")
                    k.copy(Sb.ap, S[hg].ap, [S[hg]], [Sb])
                    psWS = k.ps()
                    for j in range(4):
                        k.mm(psWS.ap[:, JS[j]], wTb.ap[:, JS[j]], Sb.ap[:, JS[j]], True, True, [wTb, Sb], [psWS])
                    k.tt(vnew.ap, u_sb.ap, psWS.ap, ALU.subtract, [u_sb, psWS], [vnew])
                    if full:
                        psO = k.ps()
                        for j in range(4):
                            k.mm(psO.ap[:, JS[j]], qdT.ap[:, JS[j]], Sb.ap[:, JS[j]], True, False, [qdT, Sb], [psO])
                            k.mm(psO.ap[:, JS[j]], aiT.ap[:, JS[j]], vnew.ap[:, JS[j]], False, True, [aiT, vnew], [psO])
                    psS = k.ps()
                    for j in range(4):
                        k.mm(psS.ap[:, JS[j]], kd.ap[:, JS[j]], vnew.ap[:, JS[j]], True, True, [kd, vnew], [psS])
                    for j in range(4):
                        h = 4 * hg + j
                        k.stt(S[hg].ap[:, JS[j]], S[hg].ap[:, JS[j]], d["gl"].ap[:, h:h + 1], psS.ap[:, JS[j]],
                              ALU.mult, ALU.add, [S[hg], d["gl"], psS], [S[hg]])
                    if full:
                        k.copy(o_sb.ap, psO.ap, [psO], [o_sb])
                        k.tt(c2.ap, o_sb.ap, o_sb.ap, ALU.mult, [o_sb], [c2])
                        k.op("dve", lambda e: e.tensor_reduce(out=ss.ap, in_=v4(c2), axis=AX.X, op=ALU.add), [c2], [ss])
                        k.act(ss.ap, ss.ap, AF.Ln, [ss], [ss], bias=cst_eps6.ap[:, 0:1], scale=1.0 / P)
                        k.act(ss.ap, ss.ap, AF.Exp, [ss], [ss], scale=-0.5)
                        k.tt(v4(o_sb), v4(o_sb), ss.ap.unsqueeze(2).to_broadcast([P, 4, P]), ALU.mult, [o_sb, ss], [o_sb])
                        k.tt(v4(kbg), v4(o_sb), normw_b, ALU.mult, [o_sb, cst["normw"]], [kbg])
                        pst = k.ps()
                        pstb = pst.ap.bitcast(BF16)
                        for j in range(4):
                            k.tr(pstb[:, JS[j]], kbg.ap[:, JS[j]], ident_bf.ap, [kbg, ident_bf], [pst])
                        k.copy(ygd3[:, hs, ts_], pstb[:, 0:512].rearrange("p (a b) -> p a b", a=4), [pst], [y_gdnT])
            if full:
                zs = [k.alloc(TMAX) for _ in range(2)]
                zi = [0]

                def zgate(h, ps):
                    z = zs[zi[0] % 2]
                    zi[0] += 1
                    k.act(z.ap[:, 0:n], ps.ap[:, 0:n], AF.Silu, [ps], [z])
                    k.tt(ygd3[:, h, 0:n], ygd3[:, h, 0:n], z.ap[:, 0:n], ALU.mult, [y_gdnT, z], [y_gdnT])

                for hh in range(2):
                    fm_linear("gz%d" % hh, x_bf3, [x_bf], KC, n, lambda oc, ps, hh=hh: zgate(hh * 4 + oc, ps))
            tl = [kTn, vTn, qTn] + (zs if full else []) + Ug + [k_tok, v_tok, aiT, XTb, vbeta, kbg, wTb, qdT, kd, Sb,
                                                                  vnew, c2, u_sb, ss] + c1s + [t for c in chs for t in c]
            for d in bg:
                tl += list(d.values())
            k.release(mk, tl)

        def layer_norm(n, gname, bname, want_bf):
            mk = k.mark()
            sqs = [k.alloc(TMAX) for _ in range(2)]
            mean, rstd, nmr, t1 = [k.alloc(TMAX) for _ in range(4)]
            tn = [k.alloc(TMAX) for _ in range(2)]
            pss = k.ps()
            psq = k.ps()
            for c in range(16):
                s_ = sqs[c % 2]
                k.mm(pss.ap[:, 0:n], ones_f.ap, x_f[c].ap[:, 0:n], c == 0, c == 15, [ones_f, x_f[c]], [pss])
                k.act(s_.ap[:, 0:n], x_f[c].ap[:, 0:n], AF.Square, [x_f[c]], [s_])
                k.mm(psq.ap[:, 0:n], ones_f.ap, s_.ap[:, 0:n], c == 0, c == 15, [ones_f, s_], [psq])
            k.ts(mean.ap[:, 0:n], pss.ap[:, 0:n], 1.0 / D, None, ALU.mult, None, [pss], [mean])
            k.tt(t1.ap[:, 0:n], mean.ap[:, 0:n], mean.ap[:, 0:n], ALU.mult, [mean], [t1])
            k.stt(t1.ap[:, 0:n], psq.ap[:, 0:n], 1.0 / D, t1.ap[:, 0:n], ALU.mult, ALU.subtract, [psq, t1], [t1])
            k.act(rstd.ap[:, 0:n], t1.ap[:, 0:n], AF.Ln, [t1], [rstd], bias=cst_eps5.ap[:, 0:1])
            k.act(rstd.ap[:, 0:n], rstd.ap[:, 0:n], AF.Exp, [rstd], [rstd], scale=-0.5)
            k.stt(nmr.ap[:, 0:n], mean.ap[:, 0:n], -1.0, rstd.ap[:, 0:n], ALU.mult, ALU.mult, [mean, rstd], [nmr])
            for c in range(16):
                t = tn[c % 2]
                k.tt(t.ap[:, 0:n], x_f[c].ap[:, 0:n], rstd.ap[:, 0:n], ALU.mult, [x_f[c], rstd], [t])
                k.tt(t.ap[:, 0:n], t.ap[:, 0:n], nmr.ap[:, 0:n], ALU.add, [t, nmr], [t])
                k.act(x_f[c].ap[:, 0:n], t.ap[:, 0:n], AF.Identity, [t, cst[gname], cst[bname]], [x_f[c]],
                      bias=cst[bname].ap[:, c:c + 1], scale=cst[gname].ap[:, c:c + 1])
                if want_bf:
                    k.copy(x_bf3[:, c, 0:n], x_f[c].ap[:, 0:n], [x_f[c]], [x_bf], e="dve")
            k.release(mk, sqs + [mean, rstd, nmr, t1] + tn)

        def resid_add(c, ps, n):
            k.stt(x_f[c].ap[:, 0:n], x_f[c].ap[:, 0:n], ALPHA, ps.ap[:, 0:n], ALU.mult, ALU.add, [x_f[c], ps], [x_f[c]])

        def merge_and_mix(n):
            mk = k.mark()
            mixed = k.alloc(16 * TMAX, BF16)
            mx3 = mixed.ap.rearrange("p (c t) -> p c t", c=16)
            m1 = [k.alloc(TMAX) for _ in range(4)]
            sg = [k.alloc(TMAX) for _ in range(2)]
            si = [0]
            for og in range(8):
                for br, ysrc, ybuf in (("ms", ysw3, y_swaT), ("mg", ygd3, y_gdnT)):
                    slot, (wg, wb) = ws.next("%s%d" % (br, og))
                    for oc in range(2):
                        psg = k.ps()
                        for kc in range(KC):
                            k.mm(psg.ap[:, 0:n], wg[:, kc, oc * P:(oc + 1) * P], x_bf3[:, kc, 0:n], kc == 0, kc == KC - 1,
                                 [slot, x_bf], [psg], inc=(kc == KC - 1))
                        psy = k.ps()
                        for kc in range(8):
                            k.mm(psy.ap[:, 0:n], wb[:, kc, oc * P:(oc + 1) * P], ysrc[:, kc, 0:n], kc == 0, kc == 7,
                                 [slot, ybuf], [psy], inc=(kc == 7))
                        s_ = sg[si[0] % 2]
                        si[0] += 1
                        k.act(s_.ap[:, 0:n], psg.ap[:, 0:n], AF.Sigmoid, [psg], [s_])
                        if br == "ms":
                            k.tt(m1[oc].ap[:, 0:n], s_.ap[:, 0:n], psy.ap[:, 0:n], ALU.mult, [s_, psy], [m1[oc]])
                        else:
                            k.tt(s_.ap[:, 0:n], s_.ap[:, 0:n], psy.ap[:, 0:n], ALU.mult, [s_, psy], [s_])
                            k.tt(mx3[:, og * 2 + oc, 0:n], s_.ap[:, 0:n], m1[oc].ap[:, 0:n], ALU.add, [s_, m1[oc]], [mixed])
            for og in range(4):
                fm_linear("mix%d" % og, mx3, [mixed], KC, n, lambda oc, ps, og=og: resid_add(og * 4 + oc, ps, n))
            k.release(mk, [mixed] + m1 + sg)

        def mem_attn(n):
            mk = k.mark()
            qm = k.alloc(4 * TMAX, BF16)
            om = k.alloc(4 * TMAX, BF16)
            qm3 = qm.ap.rearrange("p (c t) -> p c t", c=4)
            om3 = om.ap.rearrange("p (c t) -> p c t", c=4)
            PTm = [k.alloc(TMAX, BF16) for _ in range(2)]
            rd = k.alloc(TMAX)
            fm_linear("mq", x_bf3, [x_bf], KC, n, lambda oc, ps: k.copy(qm3[:, oc, 0:n], ps.ap[:, 0:n], [ps], [qm]))
            km3 = KmT.ap.rearrange("p (h m) -> p h m", h=4)
            for h in range(4):
                for mb in range(2):
                    ps = k.ps()
                    k.mm(ps.ap[:, 0:n], km3[:, h, mb * P:(mb + 1) * P], qm3[:, h, 0:n], True, True, [KmT, qm], [ps])
                    k.act(PTm[mb].ap[:, 0:n], ps.ap[:, 0:n], AF.Exp, [ps], [PTm[mb]], scale=float(P ** -0.5))
                pso = k.ps()
                psd = k.ps()
                for mb in range(2):
                    k.mm(pso.ap[:, 0:n], Vm[mb].ap[:, h * P:(h + 1) * P], PTm[mb].ap[:, 0:n], mb == 0, mb == 1,
                         [Vm[mb], PTm[mb]], [pso])
                for mb in range(2):
                    k.mm(psd.ap[:, 0:n], ones_bf.ap, PTm[mb].ap[:, 0:n], mb == 0, mb == 1, [ones_bf, PTm[mb]], [psd])
                k.act(rd.ap[:, 0:n], psd.ap[:, 0:n], AF.Ln, [psd], [rd])
                k.act(rd.ap[:, 0:n], rd.ap[:, 0:n], AF.Exp, [rd], [rd], scale=-1.0)
                k.tt(om3[:, h, 0:n], pso.ap[:, 0:n], rd.ap[:, 0:n], ALU.mult, [pso, rd], [om])
            slot, (wv,) = ws.next("mo")
            for c in range(16):
                ps = k.ps()
                for kc in range(4):
                    k.mm(ps.ap[:, 0:n], wv[:, kc, c * P:(c + 1) * P], om3[:, kc, 0:n], kc == 0, kc == 3, [slot, om], [ps], inc=(kc == 3))
                resid_add(c, ps, n)
            k.release(mk, [qm, om, rd] + PTm)

        def ffn(n, halo_only):
            mk = k.mark()
            a = None if halo_only else k.alloc(NFF * TMAX, BF16)
            a3 = None if halo_only else a.ap.rearrange("p (c t) -> p c t", c=NFF)
            hb = [k.alloc(2 + TMAX) for _ in range(4)]
            tc_ = [k.alloc(TMAX) for _ in range(4)]
            fcw = cst["fcw"].ap.rearrange("p (c j) -> p c j", j=3)
            fcb = cst["fcb"].ap
            fh3 = fhalo.ap.rearrange("p (c j) -> p c j", j=2)
            cn = [0]

            def conv(ci, ps):
                i = cn[0] % 4
                cn[0] += 1
                h_, t_ = hb[i], tc_[i]
                k.copy(h_.ap[:, 0:2], fh3[:, ci, :], [fhalo], [h_], e="dve")
                k.copy(h_.ap[:, 2:2 + n], ps.ap[:, 0:n], [ps], [h_])
                if halo_only:
                    k.ts(fh3[:, ci, :], h_.ap[:, n:n + 2], cst["valid"].ap[:, 0:1], None, ALU.mult, None,
                         [h_, cst["valid"]], [fhalo])
                    return None
                k.copy(fh3[:, ci, :], h_.ap[:, n:n + 2], [h_], [fhalo], e="dve")
                k.ts(t_.ap[:, 0:n], h_.ap[:, 0:n], fcw[:, ci, 0:1], fcb[:, ci:ci + 1], ALU.mult, ALU.add,
                     [h_, cst["fcw"], cst["fcb"]], [t_])
                for j in (1, 2):
                    k.stt(t_.ap[:, 0:n], h_.ap[:, j:j + n], fcw[:, ci, j:j + 1], t_.ap[:, 0:n], ALU.mult, ALU.add,
                          [h_, cst["fcw"], t_], [t_])
                return t_

            for j0 in range(0, NFF, 2):
                npair = min(2, NFF - j0)
                slot, (wg, wu) = ws.next("up%d" % j0)
                for jj in range(npair):
                    j = j0 + jj
                    psg = k.ps()
                    for kc in range(KC):
                        k.mm(psg.ap[:, 0:n], wg[:, kc, jj * P:(jj + 1) * P], x_bf3[:, kc, 0:n], kc == 0, kc == KC - 1,
                             [slot, x_bf], [psg], inc=(kc == KC - 1))
                    psu = k.ps()
                    for kc in range(KC):
                        k.mm(psu.ap[:, 0:n], wu[:, kc, jj * P:(jj + 1) * P], x_bf3[:, kc, 0:n], kc == 0, kc == KC - 1,
                             [slot, x_bf], [psu], inc=(kc == KC - 1))
                    tg = conv(j, psg)
                    tu = conv(NFF + j, psu)
                    if not halo_only:
                        k.act(tg.ap[:, 0:n], tg.ap[:, 0:n], AF.Silu, [tg], [tg])
                        k.tt(a3[:, j, 0:n], tg.ap[:, 0:n], tu.ap[:, 0:n], ALU.mult, [tg, tu], [a])
            if not halo_only:
                for c in range(16):
                    slot, (wv,) = ws.next("dn%d" % c)
                    ps = k.ps()
                    for kc in range(NFF):
                        k.mm(ps.ap[:, 0:n], wv[:, kc, :], a3[:, kc, 0:n], kc == 0, kc == NFF - 1, [slot, a], [ps], inc=(kc == NFF - 1))
                    resid_add(c, ps, n)
            k.release(mk, [a] + hb + tc_)

        def do_tap(name, t, ncols):
            if tap is not None and tap[0] == name:
                k.dma("sp", "out", dbg[:, 0:ncols], t, R=[x_f, x_bf, y_swaT, y_gdnT, S, kTd, Vd])

        cst_eps6 = k.alloc(1)
        cst_eps5 = k.alloc(1)
        cst_one = k.alloc(1)
        k.op("dve", lambda g: g.memset(cst_eps6.ap, 1e-6), W=[cst_eps6])
        k.op("dve", lambda g: g.memset(cst_eps5.ap, EPS), W=[cst_eps5])
        k.op("dve", lambda g: g.memset(cst_one.ap, 1.0), W=[cst_one])

        def ck(name):
            if stop == name:
                raise _Stop()

        def body():
            ck("setup")
            run_all()

        def run_all():
          for ti, (c0, n) in enumerate(pre_tiles):
            load_x(c0, n, False)
            ck("pre_load")
            if ti == len(pre_tiles) - 1:
                swa_kv(n)
                swa_rotate(n)
                ck("pre_kv")
            gdn(n, False)
            ck("pre_gdn")
          for ti, (c0, n) in enumerate(main_tiles):
            halo = ti == 0
            load_x(c0, n, True)
            swa_kv(n)
            ck("kv")
            swa_attn(n, ti == 1)
            swa_rotate(n)
            ck("attn")
            if ti == 1:
                do_tap("yswa", y_swaT.ap.bitcast(F32)[:, 0:4 * TMAX], 4 * TMAX)
            gdn(n, True)
            ck("gdn")
            if ti == 1:
                do_tap("ygdn", y_gdnT.ap.bitcast(F32)[:, 0:4 * TMAX], 4 * TMAX)
                do_tap("S", S[0].ap, 512)
            merge_and_mix(n)
            ck("merge")
            layer_norm(n, "ln1g", "ln1b", True)
            ck("ln1")
            if ti == 1:
                do_tap("x1", x_f[0].ap, TMAX)
            mem_attn(n)
            ck("mem")
            layer_norm(n, "ln2g", "ln2b", True)
            if ti == 1:
                do_tap("x2", x_f[0].ap, TMAX)
            ffn(n, halo)
            ck("ffn")
            if not halo:
                layer_norm(n, "ln3g", "ln3b", False)
                oc0 = c0 - ntok_own
                k.dma("sp", "out", yT[:, oc0:oc0 + n].rearrange("(c p) t -> p c t", p=P), xf3[:, :, 0:n], R=x_f)

        try:
            body()
            assert ws.taken == len(groups), (ws.taken, len(groups))
        except _Stop:
            pass
        k.wait_all("sp")
        k.wait_all("act")
        build.stats = dict(ninstr=k.ninstr, peak_kb=k.peak / 1024.0)
    return nc


def _t5_bucket(dist):
    d = np.maximum(dist, 1).astype(np.float32)
    large = 16 + (np.log(d / np.float32(16)) / np.float32(np.log(128 / 16)) * np.float32(16)).astype(np.int32)
    large = np.minimum(large, 31)
    return np.where(dist < 16, dist, large)


def _pc(v, n):
    return np.ascontiguousarray(np.asarray(v, np.float32).reshape(n, P).T)


def prepare_inputs(inp, ntok_own, n_batch):
    f = lambda a: np.ascontiguousarray(np.asarray(a, dtype=np.float32))
    w_in = f(inp["w_in"][0])
    sk = w_in[:, C_SK:C_SK + 128]
    sv = w_in[:, C_SV:C_SV + 128]
    w_skv = np.concatenate([sk[:, 0:64], sk[:, 0:64], sk[:, 64:128], sk[:, 64:128],
                            sv[:, 0:64], sv[:, 0:64], sv[:, 64:128], sv[:, 64:128]], axis=1)
    rel_bias = f(inp["rel_bias"])
    kk = np.arange(P)[:, None]
    qq = np.arange(P)[None, :]
    biasT = np.zeros((2, P, 16, P), np.float32)
    maskneg = np.zeros((2, P, P), np.float32)
    for kb in range(2):
        dist = qq - kk + (P if kb == 0 else 0)
        inwin = (dist >= 0) & (dist < 128)
        bkt = _t5_bucket(np.maximum(dist, 0))
        g = rel_bias[bkt]
        g = np.where(inwin[:, :, None], g, np.float32(0))
        biasT[kb] = np.transpose(g, (0, 2, 1))
        maskneg[kb] = np.where(inwin, np.float32(0), np.float32(-30000.0))
    gcw = f(inp["gdn_conv_w"][0])
    gcw_l = np.ascontiguousarray(np.transpose(gcw.reshape(4, 24, P), (2, 1, 0))).reshape(P, 96)
    fw = f(inp["ffn_conv_w"][0])
    fcw_l = np.ascontiguousarray(np.transpose(fw.reshape(3, 86, P), (2, 1, 0))).reshape(P, 258)
    sinks = f(inp["swa_sinks"][0])
    sinkT = np.zeros((P, 8), np.float32)
    for c in range(8):
        sinkT[0:64, c] = sinks[2 * c]
        sinkT[64:128, c] = sinks[2 * c + 1]
    ii = np.arange(P)
    common = {
        "w_in": w_in, "w_skv": np.ascontiguousarray(w_skv),
        "w_brs": f(inp["w_br_swa"][0]), "w_brg": f(inp["w_br_gdn"][0]), "w_mix": f(inp["w_mix_o"][0]),
        "w_mq": f(inp["w_mem_q"][0]), "w_mkv": f(inp["w_mem_kv"][0]), "w_mo": f(inp["w_mem_o"][0]),
        "w_up": f(inp["w_up"][0]), "w_dn": f(inp["w_down"][0]),
        "biasT": biasT.reshape(2, P, 16 * P), "maskneg": maskneg,
        "ln1g": _pc(inp["ln1_g"][0], 16), "ln1b": _pc(inp["ln1_b"][0], 16),
        "ln2g": _pc(inp["ln2_g"][0], 16), "ln2b": _pc(inp["ln2_b"][0], 16),
        "ln3g": _pc(inp["ln3_g"][0], 16), "ln3b": _pc(inp["ln3_b"][0], 16),
        "gcw": gcw_l, "fcw": fcw_l, "fcb": _pc(inp["ffn_conv_b"][0], 86), "sinkT": sinkT,
        "alog": np.ascontiguousarray(np.broadcast_to(f(inp["gdn_a_log"][0])[None, :], (P, 8))),
        "dtb": np.ascontiguousarray(np.broadcast_to(f(inp["gdn_dt_bias"][0])[None, :], (P, 8))),
        "normw": np.ascontiguousarray(np.broadcast_to(f(inp["gdn_norm_w"][0])[None, :], (P, P))),
        "ident": np.eye(P, dtype=np.float32),
        "tril": (ii[:, None] >= ii[None, :]).astype(np.float32),
        "strict": (ii[:, None] > ii[None, :]).astype(np.float32),
        "triu": (ii[:, None] <= ii[None, :]).astype(np.float32),
    }
    x = np.asarray(inp["x"], np.float32)
    mem = np.asarray(inp["mem"], np.float32)
    maps = []
    for b in range(n_batch):
        xbT = np.ascontiguousarray(x[b].T)
        memT = np.ascontiguousarray(mem[b].T)
        for h in range(2):
            m = dict(common)
            if h == 0:
                xl = np.concatenate([np.zeros((D, ntok_own), np.float32), xbT[:, 0:ntok_own]], axis=1)
            else:
                xl = xbT[:, 0:2 * ntok_own]
            m["xT"] = np.ascontiguousarray(xl)
            m["memT"] = memT
            m["valid"] = np.full((P, 1), float(h), np.float32)
            maps.append(m)
    return maps


def run(inp, ntok_own, n_batch, tap=None, stop=None):
    nc = build(ntok_own, tap, stop)
    maps = prepare_inputs(inp, ntok_own, n_batch)
    res = run_bass_kernel_spmd(nc, maps, core_ids=list(range(len(maps))))
    out = np.zeros((n_batch, 2 * ntok_own, D), np.float32)
    for b in range(n_batch):
        for h in range(2):
            out[b, h * ntok_own:(h + 1) * ntok_own, :] = res.results[2 * b + h]["yT"].T
    dbg = [r.get("dbg") for r in res.results] if tap is not None else None
    return out, dbg


def kernel(**inputs):
    out, _ = run(inputs, 4096, 4)
    return out
```

```python
import numpy as np
import ml_dtypes
from contextlib import ExitStack
import concourse.bass as bass
import concourse.mybir as mybir
from concourse.bass_utils import run_bass_kernel_spmd

F32 = mybir.dt.float32
BF16 = mybir.dt.bfloat16
AF = mybir.ActivationFunctionType
ALU = mybir.AluOpType
AX = mybir.AxisListType

P = 128
D = 2048
KC = 16
IN_DIM = 9488
DFF = 5504
NFF = 43
ALPHA = 2.0 ** 0.25
EPS = 1e-5
TMAX = 512
ARENA_KB = 203

C_SQ, C_SK, C_SV, C_GQ, C_GK, C_GV, C_GZ, C_GB, C_GA, C_GS, C_GG = (
    0, 1024, 1152, 1280, 2304, 3328, 4352, 5376, 5384, 5392, 7440)


class Buf:
    __slots__ = ("w", "r")

    def __init__(self, r=None):
        self.w = None
        self.r = dict(r) if r else {}


class T:
    def __init__(self, ap, bufs):
        self.ap = ap
        self.bufs = bufs

    def __getitem__(self, idx):
        return self.ap[idx]


def _bufs(lst):
    out = []
    for x in lst:
        if x is None:
            continue
        if isinstance(x, Buf):
            out.append(x)
        elif isinstance(x, T):
            out.extend(x.bufs)
        else:
            out.extend(_bufs(x))
    return out


class Kb:
    def __init__(self, nc, es):
        self.nc = nc
        self.eng = {"pe": nc.tensor, "dve": nc.vector, "act": nc.scalar, "pool": nc.gpsimd, "sp": nc.sync}
        self.sem = {}
        self.cnt = {}
        self.seen = {e: {} for e in self.eng}
        self.es = es
        for e in ("pe", "dve", "act", "pool"):
            self.newkey(e)
        self.arena = es.enter_context(nc.sbuf_tensor("arena", [P, ARENA_KB * 256], F32))
        self.off = 0
        self.free_deps = {}
        self.peak = 0
        self.psum = es.enter_context(nc.psum_tensor("psum", [P, 4096], F32))
        self.pbanks = [T(self.psum[:, i * 512:(i + 1) * 512], [Buf()]) for i in range(8)]
        self.pidx = 0
        self.ninstr = 0

    def newkey(self, name):
        self.sem[name] = self.es.enter_context(self.nc.semaphore(name))
        self.cnt[name] = 0

    def alloc(self, nelem, dtype=F32, nbuf=1):
        size = {F32: 4, BF16: 2}[dtype]
        nbytes = (nelem * size + 63) // 64 * 64
        assert self.off + nbytes <= ARENA_KB * 1024, ("SBUF arena overflow", self.off, nbytes)
        ap = self.arena[:, self.off // 4:(self.off + nbytes) // 4]
        if dtype != F32:
            ap = ap.bitcast(dtype)
        ap = ap[:, 0:nelem]
        self.off += nbytes
        self.peak = max(self.peak, self.off)
        return T(ap, [Buf(self.free_deps)])

    def mark(self):
        return self.off

    def release(self, mark, tiles):
        for b in _bufs(tiles):
            if b.w:
                self.free_deps[b.w[0]] = max(self.free_deps.get(b.w[0], 0), b.w[1])
            for k, v in b.r.items():
                self.free_deps[k] = max(self.free_deps.get(k, 0), v)
        self.off = mark

    def ps(self):
        t = self.pbanks[self.pidx]
        self.pidx = (self.pidx + 1) % 8
        return t

    def _wait(self, e, key, val):
        if key == e and e == "pe":
            return
        if self.seen[e].get(key, 0) >= val:
            return
        self.eng[e].wait_ge(self.sem[key], val)
        self.seen[e][key] = val

    def _deps(self, R, W):
        deps = {}
        for b in R:
            if b.w:
                deps[b.w[0]] = max(deps.get(b.w[0], 0), b.w[1])
        for b in W:
            if b.w:
                deps[b.w[0]] = max(deps.get(b.w[0], 0), b.w[1])
            for k, v in b.r.items():
                deps[k] = max(deps.get(k, 0), v)
        return deps

    def op(self, e, fn, R=(), W=(), inc=True):
        R = _bufs(R)
        W = _bufs(W)
        for k, v in self._deps(R, W).items():
            self._wait(e, k, v)
        ins = fn(self.eng[e])
        if inc:
            self.cnt[e] += 1
            ins.then_inc(self.sem[e], 1)
            seq = self.cnt[e]
        else:
            assert e == "pe"
            seq = self.cnt[e] + 1
        for b in R:
            b.r[e] = seq
        for b in W:
            b.w = (e, seq)
            b.r = {}
        self.ninstr += 1
        return ins

    def dma(self, q, key, out, in_, R=(), W=()):
        R = _bufs(R)
        W = _bufs(W)
        for k, v in self._deps(R, W).items():
            if k != key:
                self._wait(q, k, v)
        self.eng[q].dma_start(out=out, in_=in_).then_inc(self.sem[key], 16)
        self.cnt[key] += 16
        seq = self.cnt[key]
        for b in R:
            b.r[key] = seq
        for b in W:
            b.w = (key, seq)
            b.r = {}
        self.ninstr += 1

    def wait_all(self, e):
        for k, v in self.cnt.items():
            if v:
                self._wait(e, k, v)

    def tt(self, out, in0, in1, op, R, W, e="dve"):
        return self.op(e, lambda g: g.tensor_tensor(out=out, in0=in0, in1=in1, op=op), R, W)

    def ts(self, out, in0, s1, s2, op0, op1, R, W, e="dve"):
        if op1 is None:
            return self.op(e, lambda g: g.tensor_scalar(out=out, in0=in0, scalar1=s1, scalar2=None, op0=op0), R, W)
        return self.op(e, lambda g: g.tensor_scalar(out=out, in0=in0, scalar1=s1, scalar2=s2, op0=op0, op1=op1), R, W)

    def stt(self, out, in0, sc, in1, op0, op1, R, W):
        return self.op("dve", lambda g: g.scalar_tensor_tensor(out=out, in0=in0, scalar=sc, in1=in1, op0=op0, op1=op1), R, W)

    def act(self, out, in_, func, R, W, bias=None, scale=None):
        kw = {}
        if bias is not None:
            kw["bias"] = bias
        if scale is not None:
            kw["scale"] = scale
        return self.op("act", lambda g: g.activation(out=out, in_=in_, func=func, **kw), R, W)

    def copy(self, out, in_, R, W, e="act"):
        if e == "act":
            return self.op("act", lambda g: g.activation(out=out, in_=in_, func=AF.Copy), R, W)
        return self.op(e, lambda g: g.tensor_copy(out=out, in_=in_), R, W)

    def mm(self, out, lhsT, rhs, start, stop, R, W, inc=True):
        return self.op("pe", lambda g: g.matmul(out, lhsT=lhsT, rhs=rhs, start=start, stop=stop), R, W, inc=inc)

    def tr(self, out, in_, ident, R, W):
        return self.op("pe", lambda g: g.transpose(out, in_, ident), R, W)


class WStream:
    def __init__(self, k, groups, nslot=2, slot_elems=8192):
        self.k = k
        self.groups = groups
        self.nslot = nslot
        self.slots = [k.alloc(slot_elems, BF16) for _ in range(nslot)]
        for i in range(nslot):
            k.newkey("w%d" % i)
        self.issued = 0
        self.taken = 0

    def _issue(self):
        i = self.issued
        if i >= len(self.groups):
            return
        tag, parts = self.groups[i]
        s = i % self.nslot
        slot = self.slots[s]
        off = 0
        for (w, r0, kc, c0, nc_) in parts:
            dst = slot.ap[:, off:off + kc * nc_].rearrange("p (k n) -> p k n", k=kc)
            src = w[r0:r0 + kc * P, c0:c0 + nc_].rearrange("(k p) n -> p k n", p=P)
            self.k.dma("pool", "w%d" % s, dst, src, R=(), W=[slot])
            off += kc * nc_
        self.issued += 1

    def next(self, tag):
        while self.issued < min(self.taken + self.nslot, len(self.groups)):
            self._issue()
        gtag, parts = self.groups[self.taken]
        assert gtag == tag, (gtag, tag, self.taken)
        slot = self.slots[self.taken % self.nslot]
        self.taken += 1
        views = []
        off = 0
        for (w, r0, kc, c0, nc_) in parts:
            views.append(slot.ap[:, off:off + kc * nc_].rearrange("p (k n) -> p k n", k=kc))
            off += kc * nc_
        return slot, views

    def prefetch(self):
        while self.issued < min(self.taken + self.nslot, len(self.groups)):
            self._issue()


def tile_plan(ntok_own):
    pre = []
    c = 0
    end = ntok_own - P
    while c < end:
        n = min(TMAX, end - c)
        pre.append((c, n))
        c += n
    main = [(ntok_own - P, P)]
    c = ntok_own
    while c < 2 * ntok_own:
        n = min(TMAX, 2 * ntok_own - c)
        main.append((c, n))
        c += n
    return pre, main


class _Stop(Exception):
    pass


def build(ntok_own, tap=None, stop=None):
    nc = bass.Bass("TRN2", target_bir_lowering=False)
    NT = 2 * ntok_own

    def din(name, shape):
        return nc.dram_tensor(name, list(shape), F32, kind="ExternalInput").ap()

    xT = din("xT", (D, NT))
    memT = din("memT", (D, 256))
    w_in = din("w_in", (D, IN_DIM))
    w_skv = din("w_skv", (D, 512))
    w_brs = din("w_brs", (1024, D))
    w_brg = din("w_brg", (1024, D))
    w_mix = din("w_mix", (D, D))
    w_mq = din("w_mq", (D, 512))
    w_mkv = din("w_mkv", (D, 1024))
    w_mo = din("w_mo", (512, D))
    w_up = din("w_up", (D, 2 * DFF))
    w_dn = din("w_dn", (DFF, D))
    biasT_d = din("biasT", (2, P, 16 * P))
    maskneg_d = din("maskneg", (2, P, P))
    cvec = {}
    for name, n in (("ln1g", 16), ("ln1b", 16), ("ln2g", 16), ("ln2b", 16), ("ln3g", 16), ("ln3b", 16),
                    ("gcw", 96), ("fcw", 258), ("fcb", 86), ("sinkT", 8), ("alog", 8), ("dtb", 8),
                    ("normw", 128), ("valid", 1), ("ident", 128), ("tril", 128), ("strict", 128), ("triu", 128)):
        cvec[name] = (din(name, (P, n)), n)
    yT = nc.dram_tensor("yT", [D, ntok_own], F32, kind="ExternalOutput").ap()
    dbg = None
    if tap is not None:
        dbg = nc.dram_tensor("dbg", [P, tap[1]], F32, kind="ExternalOutput").ap()

    pre_tiles, main_tiles = tile_plan(ntok_own)

    with ExitStack() as es:
        k = Kb(nc, es)
        for key in ("cst", "xf", "xb", "out", "mem", "bm"):
            k.newkey(key)

        groups = []

        def G(tag, *parts):
            groups.append((tag, list(parts)))

        G("mk", (w_mkv, 0, KC, 0, 512))
        G("mv", (w_mkv, 0, KC, 512, 512))

        def gdn_kv_groups():
            for kind, c0 in (("k", C_GK), ("v", C_GV)):
                for hh in range(2):
                    G("g%s%d" % (kind, hh), (w_in, 0, KC, c0 + hh * 512, 512))
            G("gba", (w_in, 0, KC, C_GB, 16))

        for ti, (c0, n) in enumerate(pre_tiles):
            if ti == len(pre_tiles) - 1:
                G("skv", (w_skv, 0, KC, 0, 512))
            gdn_kv_groups()
        for ti, (c0, n) in enumerate(main_tiles):
            G("skv", (w_skv, 0, KC, 0, 512))
            G("sq0", (w_in, 0, KC, C_SQ, 512))
            G("sq1", (w_in, 0, KC, C_SQ + 512, 512))
            gdn_kv_groups()
            G("gq0", (w_in, 0, KC, C_GQ, 512))
            G("gq1", (w_in, 0, KC, C_GQ + 512, 512))
            G("gz0", (w_in, 0, KC, C_GZ, 512))
            G("gz1", (w_in, 0, KC, C_GZ + 512, 512))
            for og in range(8):
                G("ms%d" % og, (w_in, 0, KC, C_GS + og * 256, 256), (w_brs, 0, 8, og * 256, 256))
                G("mg%d" % og, (w_in, 0, KC, C_GG + og * 256, 256), (w_brg, 0, 8, og * 256, 256))
            for og in range(4):
                G("mix%d" % og, (w_mix, 0, KC, og * 512, 512))
            G("mq", (w_mq, 0, KC, 0, 512))
            G("mo", (w_mo, 0, 4, 0, 2048))
            for j0 in range(0, NFF, 2):
                npair = min(2, NFF - j0)
                G("up%d" % j0, (w_up, 0, KC, j0 * P, npair * P), (w_up, 0, KC, DFF + j0 * P, npair * P))
            if ti > 0:
                for cp in range(8):
                    G("dnA%d" % cp, (w_dn, 0, 22, cp * 256, 256))
                    G("dnB%d" % cp, (w_dn, 22 * P, 21, cp * 256, 256))

        cst = {}
        for name, (ap_d, n) in cvec.items():
            cst[name] = k.alloc(n)
            k.dma("sp", "cst", cst[name].ap, ap_d, W=[cst[name]])
        mneg = [k.alloc(P) for _ in range(2)]
        ident_bf = k.alloc(P, BF16)
        ones_bf = k.alloc(P, BF16)
        ones_f = k.alloc(P)
        esinkT = k.alloc(8)
        expA = k.alloc(8)
        KmT = k.alloc(4 * 256, BF16)
        Vm = [k.alloc(512, BF16) for _ in range(2)]
        S = [k.alloc(512) for _ in range(2)]
        ghalo = k.alloc(24 * 3)
        fhalo = k.alloc(86 * 2)
        kTd = [k.alloc(P + TMAX, BF16) for _ in range(2)]
        Vd = [k.alloc(2 * P, BF16) for _ in range(1 + TMAX // P)]
        xf_all = k.alloc(16 * TMAX)
        xf3 = xf_all.ap.rearrange("p (c t) -> p c t", c=16)
        x_f = [T(xf_all.ap[:, c * TMAX:(c + 1) * TMAX], [Buf()]) for c in range(16)]
        x_bf = k.alloc(16 * TMAX, BF16)
        x_bf3 = x_bf.ap.rearrange("p (c t) -> p c t", c=16)
        y_swaT = k.alloc(8 * TMAX, BF16)
        y_gdnT = k.alloc(8 * TMAX, BF16)
        ysw3 = y_swaT.ap.rearrange("p (c t) -> p c t", c=8)
        ygd3 = y_gdnT.ap.rearrange("p (c t) -> p c t", c=8)
        ws = WStream(k, groups, nslot=2)

        k.op("dve", lambda g: g.memset(ones_f.ap, 1.0), W=[ones_f])
        k.op("dve", lambda g: g.memset(ones_bf.ap, 1.0), W=[ones_bf])
        for t in S + [ghalo, fhalo] + kTd + Vd:
            k.op("dve", lambda g, t=t: g.memset(t.ap, 0.0), W=[t])
        for kb in range(2):
            k.dma("sp", "cst", mneg[kb].ap, maskneg_d[kb], W=[mneg[kb]])
        for b in _bufs(list(cst.values()) + mneg):
            b.w = ("cst", k.cnt["cst"])
        k.copy(ident_bf.ap, cst["ident"].ap, [cst["ident"]], [ident_bf], e="dve")
        k.act(esinkT.ap, cst["sinkT"].ap, AF.Exp, [cst["sinkT"]], [esinkT])
        k.act(expA.ap, cst["alog"].ap, AF.Exp, [cst["alog"]], [expA])
        m0 = k.mark()
        k.dma("pool", "mem", x_bf3[:, :, 0:256], memT.rearrange("(c p) t -> p c t", p=P), W=[x_bf])
        slot, (wv,) = ws.next("mk")
        for h in range(4):
            ps = k.ps()
            for kc in range(KC):
                k.mm(ps.ap[:, 0:256], wv[:, kc, h * P:(h + 1) * P], x_bf3[:, kc, 0:256], kc == 0, kc == KC - 1,
                     [slot, x_bf], [ps])
            k.copy(KmT.ap[:, h * 256:(h + 1) * 256], ps.ap[:, 0:256], [ps], [KmT])
        slot, (wv,) = ws.next("mv")
        for mb in range(2):
            ps = k.ps()
            for kc in range(KC):
                k.mm(ps.ap[:, 0:512], x_bf3[:, kc, mb * P:(mb + 1) * P], wv[:, kc, :], kc == 0, kc == KC - 1,
                     [slot, x_bf], [ps])
            k.copy(Vm[mb].ap, ps.ap, [ps], [Vm[mb]])
        k.release(m0, [])

        def load_x(c0, n, want_f32):
            k.dma("pool", "xb", x_bf3[:, :, 0:n], xT[:, c0:c0 + n].rearrange("(c p) t -> p c t", p=P), W=[x_bf])
            if want_f32:
                k.dma("sp", "xf", xf3[:, :, 0:n], xT[:, c0:c0 + n].rearrange("(c p) t -> p c t", p=P), W=x_f)

        def fm_linear(tag, act3, actbufs, kcn, n, consume, nchunks=4, view=0):
            slot, views = ws.next(tag)
            wv = views[view]
            for oc in range(nchunks):
                ps = k.ps()
                for kc in range(kcn):
                    k.mm(ps.ap[:, 0:n], wv[:, kc, oc * P:(oc + 1) * P], act3[:, kc, 0:n], kc == 0, kc == kcn - 1,
                         [slot] + actbufs, [ps], inc=(kc == kcn - 1))
                consume(oc, ps)
            return slot, views

        def swa_kv(n):
            nb = n // P
            slot, (wv,) = ws.next("skv")
            for g in range(2):
                ps = k.ps()
                for kc in range(KC):
                    k.mm(ps.ap[:, 0:n], wv[:, kc, g * P:(g + 1) * P], x_bf3[:, kc, 0:n], kc == 0, kc == KC - 1,
                         [slot, x_bf], [ps])
                k.copy(kTd[g].ap[:, P:P + n], ps.ap[:, 0:n], [ps], [kTd[g]])
            for blk in range(nb):
                ps = k.ps()
                for kc in range(KC):
                    k.mm(ps.ap[:, 0:256], x_bf3[:, kc, blk * P:(blk + 1) * P], wv[:, kc, 256:512], kc == 0,
                         kc == KC - 1, [slot, x_bf], [ps])
                k.copy(Vd[1 + blk].ap, ps.ap[:, 0:256], [ps], [Vd[1 + blk]], e="dve")

        def swa_rotate(n):
            nb = n // P
            for g in range(2):
                k.copy(kTd[g].ap[:, 0:P], kTd[g].ap[:, n:n + P], [kTd[g]], [kTd[g]], e="dve")
            k.copy(Vd[0].ap, Vd[nb].ap, [Vd[nb]], [Vd[0]], e="dve")

        def swa_attn(n, first_own):
            nb = n // P
            mk = k.mark()
            qT = k.alloc(8 * TMAX, BF16)
            q3 = qT.ap.rearrange("p (c t) -> p c t", c=8)
            tmp = [k.alloc(512) for _ in range(2)]
            PTs = [[[k.alloc(512, BF16) for _ in range(2)] for _ in range(2)] for _ in range(2)]
            den = k.alloc(512)
            rden = k.alloc(512)
            biasm = [k.alloc(16 * P) for _ in range(2)]
            for kb in range(2):
                k.dma("sp", "bm", biasm[kb].ap, biasT_d[kb], W=[biasm[kb]])
            for b in _bufs(biasm):
                b.w = ("bm", k.cnt["bm"])
            for kb in range(2):
                b3 = biasm[kb].ap.rearrange("p (h q) -> p h q", h=16)
                k.tt(b3, b3, mneg[kb].ap.unsqueeze(1).to_broadcast([P, 16, P]), ALU.add, [biasm[kb], mneg[kb]], [biasm[kb]])
            for hh in range(2):
                fm_linear("sq%d" % hh, x_bf3, [x_bf], KC, n,
                          lambda oc, ps, hh=hh: k.copy(q3[:, hh * 4 + oc, 0:n], ps.ap[:, 0:n], [ps], [qT]))
            tcnt = [0]
            units = [(blk, g) for blk in range(nb) for g in range(2)]

            def scores(u):
                blk, g = units[u]
                qs = slice(blk * P, (blk + 1) * P)
                PTu = PTs[u % 2]
                for par in range(2):
                    half = slice(par * 64, par * 64 + 64)
                    for kb in range(2):
                        ps = k.ps()
                        k.mm(ps.ap.rearrange("p (a b) -> p a b", a=4),
                             kTd[g].ap[half, (blk + kb) * P:(blk + kb + 1) * P],
                             q3[half, 4 * g:4 * g + 4, qs], True, True, [kTd[g], qT], [ps])
                        bm = biasm[kb].ap.rearrange("p (c two q) -> p c two q", two=2, q=P)[:, 4 * g:4 * g + 4, par, :]
                        t = tmp[tcnt[0] % 2]
                        tcnt[0] += 1
                        k.stt(t.ap.rearrange("p (a b) -> p a b", a=4), ps.ap.rearrange("p (a b) -> p a b", a=4),
                              0.125, bm, ALU.mult, ALU.add, [ps, biasm[kb]], [t])
                        pt = PTu[par][kb]
                        k.act(pt.ap, t.ap, AF.Exp, [t], [pt])
                        if first_own and blk == 0 and kb == 0:
                            k.ts(pt.ap, pt.ap, cst["valid"].ap[:, 0:1], None, ALU.mult, None, [pt, cst["valid"]], [pt])

            def pv(u):
                blk, g = units[u]
                qs = slice(blk * P, (blk + 1) * P)
                PTu = PTs[u % 2]
                for par in range(2):
                    half = slice(par * 64, par * 64 + 64)
                    psv = k.ps()
                    psd = k.ps()
                    for kb in range(2):
                        k.mm(psv.ap, Vd[blk + kb].ap[:, g * P:(g + 1) * P], PTu[par][kb].ap, kb == 0, kb == 1,
                             [Vd[blk + kb], PTu[par][kb]], [psv])
                    for kb in range(2):
                        k.mm(psd.ap, ones_bf.ap, PTu[par][kb].ap, kb == 0, kb == 1, [ones_bf, PTu[par][kb]], [psd])
                    d3 = den.ap.rearrange("p (a b) -> p a b", a=4)
                    r3 = rden.ap.rearrange("p (a b) -> p a b", a=4)
                    k.tt(d3[half], psd.ap.rearrange("p (a b) -> p a b", a=4)[half],
                         esinkT.ap[half, 4 * g:4 * g + 4].unsqueeze(2).to_broadcast([64, 4, P]), ALU.add,
                         [psd, esinkT], [den])
                    k.act(d3[half], d3[half], AF.Ln, [den], [den])
                    k.act(r3[half], d3[half], AF.Exp, [den], [rden], scale=-1.0)
                    k.tt(ysw3[half, 4 * g:4 * g + 4, qs], psv.ap.rearrange("p (a b) -> p a b", a=4)[half], r3[half],
                         ALU.mult, [psv, rden], [y_swaT])

            scores(0)
            for u in range(len(units)):
                if u + 1 < len(units):
                    scores(u + 1)
                pv(u)
            k.release(mk, [qT, den, rden] + tmp + PTs + biasm)

        def gdn(n, full):
            nb = n // P
            mk = k.mark()
            kTn = k.alloc(8 * TMAX, BF16)
            vTn = k.alloc(8 * TMAX, BF16)
            qTn = k.alloc(8 * TMAX, BF16) if full else None
            k3 = kTn.ap.rearrange("p (c t) -> p c t", c=8)
            v3 = vTn.ap.rearrange("p (c t) -> p c t", c=8)
            q3 = qTn.ap.rearrange("p (c t) -> p c t", c=8) if full else None
            mk_conv = None
            bgd = []
            for blk in range(nb):
                d = {}
                for nm in ("beta", "nbeta", "gpos", "gcp", "glast", "eg", "ekd", "gl", "bge", "tmp"):
                    d[nm] = k.alloc(8)
                bgd.append(d)
            mk_conv = k.mark()
            pre = [k.alloc(3 + TMAX) for _ in range(2)]
            acc = [k.alloc(TMAX) for _ in range(2)]
            sl = [k.alloc(TMAX) for _ in range(4)]
            sq = [k.alloc(TMAX) for _ in range(4)]
            rt = []
            gcw = cst["gcw"].ap.rearrange("p (c j) -> p c j", j=4)
            gh3 = ghalo.ap.rearrange("p (c j) -> p c j", j=3)
            cnt = [0]
            cnt4 = [0]
            pend = []

            def flushB():
                items = list(pend)
                del pend[:]
                pss = []
                for (kind, h, s_, sq_) in items:
                    psn = k.ps()
                    k.mm(psn.ap[:, 0:n], ones_f.ap, sq_.ap[:, 0:n], True, True, [ones_f, sq_], [psn])
                    pss.append(psn)
                for (kind, h, s_, sq_), psn in zip(items, pss):
                    k.act(sq_.ap[:, 0:n], psn.ap[:, 0:n], AF.Ln, [psn], [sq_], bias=cst_eps6.ap[:, 0:1])
                for (kind, h, s_, sq_) in items:
                    k.act(sq_.ap[:, 0:n], sq_.ap[:, 0:n], AF.Exp, [sq_], [sq_], scale=-0.5)
                for (kind, h, s_, sq_) in items:
                    if kind == "k":
                        k.tt(k3[:, h, 0:n], s_.ap[:, 0:n], sq_.ap[:, 0:n], ALU.mult, [s_, sq_], [kTn])
                    else:
                        k.stt(q3[:, h, 0:n], s_.ap[:, 0:n], float(P ** -0.5), sq_.ap[:, 0:n], ALU.mult, ALU.mult,
                              [s_, sq_], [qTn])

            def conv_chunk(kind, h, ps, first):
                if first:
                    flushB()
                ci = {"q": 0, "k": 1, "v": 2}[kind] * 8 + h
                i = cnt[0] % 2
                cnt[0] += 1
                pr, ac = pre[i], acc[i]
                k.copy(pr.ap[:, 0:3], gh3[:, ci, :], [ghalo], [pr], e="dve")
                k.copy(pr.ap[:, 3:3 + n], ps.ap[:, 0:n], [ps], [pr])
                k.copy(gh3[:, ci, :], pr.ap[:, n:n + 3], [pr], [ghalo], e="dve")
                k.ts(ac.ap[:, 0:n], pr.ap[:, 0:n], gcw[:, ci, 0:1], None, ALU.mult, None, [pr, cst["gcw"]], [ac])
                for j in range(1, 4):
                    k.stt(ac.ap[:, 0:n], pr.ap[:, j:j + n], gcw[:, ci, j:j + 1], ac.ap[:, 0:n], ALU.mult, ALU.add,
                          [pr, cst["gcw"], ac], [ac])
                if kind == "v":
                    k.act(v3[:, h, 0:n], ac.ap[:, 0:n], AF.Silu, [ac], [vTn])
                    return
                i4 = cnt4[0] % 4
                cnt4[0] += 1
                s_, sq_ = sl[i4], sq[i4]
                k.act(s_.ap[:, 0:n], ac.ap[:, 0:n], AF.Silu, [ac], [s_])
                k.act(sq_.ap[:, 0:n], s_.ap[:, 0:n], AF.Square, [s_], [sq_])
                pend.append((kind, h, s_, sq_))

            for kind in ("k", "v"):
                for hh in range(2):
                    fm_linear("g%s%d" % (kind, hh), x_bf3, [x_bf], KC, n,
                              lambda oc, ps, kind=kind, hh=hh: conv_chunk(kind, hh * 4 + oc, ps, oc == 0))
            flushB()
            ck("g_conv")
            slot_ba, (wba,) = ws.next("gba")
            bg = []
            for blk in range(nb):
                ps = k.ps()
                for kc in range(KC):
                    k.mm(ps.ap[:, 0:16], x_bf3[:, kc, blk * P:(blk + 1) * P], wba[:, kc, :], kc == 0, kc == KC - 1,
                         [slot_ba, x_bf], [ps])
                d = bgd[blk]
                k.act(d["beta"].ap, ps.ap[:, 0:8], AF.Sigmoid, [ps], [d["beta"]])
                k.tt(d["tmp"].ap, ps.ap[:, 8:16], cst["dtb"].ap, ALU.add, [ps, cst["dtb"]], [d["tmp"]])
                k.act(d["tmp"].ap, d["tmp"].ap, AF.Exp, [d["tmp"]], [d["tmp"]])
                k.act(d["tmp"].ap, d["tmp"].ap, AF.Ln, [d["tmp"]], [d["tmp"]], bias=cst_one.ap[:, 0:1])
                k.tt(d["gpos"].ap, d["tmp"].ap, expA.ap, ALU.mult, [d["tmp"], expA], [d["gpos"]])
                k.ts(d["nbeta"].ap, d["beta"].ap, -1.0, None, ALU.mult, None, [d["beta"]], [d["nbeta"]])
                ps2 = k.ps()
                k.mm(ps2.ap[:, 0:8], cst["triu"].ap, d["gpos"].ap, True, True, [cst["triu"], d["gpos"]], [ps2])
                k.mm(ps2.ap[:, 8:16], ones_f.ap, d["gpos"].ap, True, True, [ones_f, d["gpos"]], [ps2])
                k.copy(d["gcp"].ap, ps2.ap[:, 0:8], [ps2], [d["gcp"]], e="dve")
                k.copy(d["glast"].ap, ps2.ap[:, 8:16], [ps2], [d["glast"]], e="dve")
                k.act(d["eg"].ap, d["gcp"].ap, AF.Exp, [d["gcp"]], [d["eg"]], scale=-1.0)
                k.act(d["gl"].ap, d["glast"].ap, AF.Exp, [d["glast"]], [d["gl"]], scale=-1.0)
                k.tt(d["tmp"].ap, d["gcp"].ap, d["glast"].ap, ALU.subtract, [d["gcp"], d["glast"]], [d["tmp"]])
                k.act(d["ekd"].ap, d["tmp"].ap, AF.Exp, [d["tmp"]], [d["ekd"]])
                k.tt(d["bge"].ap, d["beta"].ap, d["eg"].ap, ALU.mult, [d["beta"], d["eg"]], [d["bge"]])
                bg.append(d)
            ck("g_bg")
            if full:
                for hh in range(2):
                    fm_linear("gq%d" % hh, x_bf3, [x_bf], KC, n,
                              lambda oc, ps, hh=hh: conv_chunk("q", hh * 4 + oc, ps, oc == 0))
                flushB()

            k.release(mk_conv, pre + acc + sl + sq + rt)
            def a4(dt=F32):
                return k.alloc(512, dt)

            Ug = [k.alloc(P) for _ in range(4)]
            k_tok, v_tok, aiT, XTb, vbeta, kbg, wTb, qdT, kd, Sb, vnew = [a4(BF16) for _ in range(11)]
            NCH = 4
            chs = [(a4(), a4(), a4()) for _ in range(NCH)]
            c1s = [a4() for _ in range(NCH)]
            c2, u_sb, sq_o = a4(), a4(), a4()
            o_sb = u_sb
            on_b = a4(BF16)
            egB = c1s[0]
            ss = k.alloc(4)

            def v4(t):
                return t.ap.rearrange("p (a b) -> p a b", a=4)

            idf = cst["ident"].ap.unsqueeze(1).to_broadcast([P, 4, P])
            strict_b = cst["strict"].ap.unsqueeze(1).to_broadcast([P, 4, P])
            triu_b = cst["triu"].ap.unsqueeze(1).to_broadcast([P, 4, P])
            normw_b = cst["normw"].ap.unsqueeze(1).to_broadcast([P, 4, P])
            JS = [slice(j * P, (j + 1) * P) for j in range(4)]

            def make_psB(d, hg):
                psB = k.ps()
                for j in range(4):
                    h = 4 * hg + j
                    ug = Ug[j]
                    k.ts(ug.ap, cst["triu"].ap, d["gpos"].ap[:, h:h + 1], None, ALU.mult, None,
                         [cst["triu"], d["gpos"]], [ug])
                    k.mm(psB.ap[:, JS[j]], ones_f.ap, ug.ap, True, True, [ones_f, ug], [psB])
                return psB

            chains = [(blk, hg) for blk in range(nb) for hg in range(2)]
            for b0 in range(0, len(chains), NCH):
                batch = chains[b0:b0 + NCH]
                nbat = len(batch)
                psBs = [make_psB(bg[blk], hg) for (blk, hg) in batch]
                for s_i, (blk, hg) in enumerate(batch):
                    d = bg[blk]
                    c1 = c1s[s_i]
                    for j in range(4):
                        h = 4 * hg + j
                        k.ts(c1.ap[:, JS[j]], psBs[s_i].ap[:, JS[j]], d["gcp"].ap[:, h:h + 1], 0.0,
                             ALU.subtract, ALU.min, [psBs[s_i], d["gcp"]], [c1])
                for s_i in range(nbat):
                    c1 = c1s[s_i]
                    k.act(c1.ap, c1.ap, AF.Exp, [c1], [c1])
                psKs = []
                for s_i, (blk, hg) in enumerate(batch):
                    ts_ = slice(blk * P, (blk + 1) * P)
                    psK = k.ps()
                    for j in range(4):
                        h = 4 * hg + j
                        k.mm(psK.ap[:, JS[j]], k3[:, h, ts_], k3[:, h, ts_], True, True, [kTn], [psK])
                    psKs.append(psK)
                for s_i in range(nbat):
                    c1 = c1s[s_i]
                    k.tt(v4(c1), v4(c1), strict_b, ALU.mult, [c1, cst["strict"]], [c1])
                for s_i, (blk, hg) in enumerate(batch):
                    d = bg[blk]
                    Nn, NT, XT = chs[s_i]
                    c1 = c1s[s_i]
                    for j in range(4):
                        h = 4 * hg + j
                        k.stt(Nn.ap[:, JS[j]], psKs[s_i].ap[:, JS[j]], d["nbeta"].ap[:, h:h + 1],
                              c1.ap[:, JS[j]], ALU.mult, ALU.mult, [psKs[s_i], d["nbeta"], c1], [Nn])
                psTs = []
                for s_i in range(nbat):
                    Nn, NT, XT = chs[s_i]
                    psT = k.ps()
                    for j in range(4):
                        k.mm(psT.ap[:, JS[j]], Nn.ap[:, JS[j]], cst["ident"].ap, True, True, [Nn, cst["ident"]], [psT])
                    psTs.append(psT)
                for s_i in range(nbat):
                    Nn, NT, XT = chs[s_i]
                    k.copy(NT.ap, psTs[s_i].ap, [psTs[s_i]], [NT])
                for s_i in range(nbat):
                    Nn, NT, XT = chs[s_i]
                    k.tt(v4(XT), v4(NT), idf, ALU.add, [NT, cst["ident"]], [XT])
                for it in range(6):
                    pms = []
                    for s_i in range(len(batch)):
                        Nn, NT, XT = chs[s_i]
                        psM = k.ps()
                        for j in range(4):
                            k.mm(psM.ap[:, JS[j]], NT.ap[:, JS[j]], Nn.ap[:, JS[j]], True, True, [NT, Nn], [psM])
                        psMT = None
                        if it < 5:
                            psMT = k.ps()
                            for j in range(4):
                                k.mm(psMT.ap[:, JS[j]], Nn.ap[:, JS[j]], NT.ap[:, JS[j]], True, True, [NT, Nn], [psMT])
                        pms.append((psM, psMT))
                    for s_i in range(len(batch)):
                        Nn, NT, XT = chs[s_i]
                        psM, psMT = pms[s_i]
                        k.copy(Nn.ap, psM.ap, [psM], [Nn])
                        if it < 5:
                            k.copy(NT.ap, psMT.ap, [psMT], [NT], e="dve")
                    pxs = []
                    for s_i in range(len(batch)):
                        Nn, NT, XT = chs[s_i]
                        psX = k.ps()
                        for j in range(4):
                            k.mm(psX.ap[:, JS[j]], Nn.ap[:, JS[j]], XT.ap[:, JS[j]], True, True, [Nn, XT], [psX])
                        pxs.append(psX)
                    for s_i in range(len(batch)):
                        Nn, NT, XT = chs[s_i]
                        k.tt(XT.ap, XT.ap, pxs[s_i].ap, ALU.add, [XT, pxs[s_i]], [XT])
                for s_i, (blk, hg) in enumerate(batch):
                    d = bg[blk]
                    ts_ = slice(blk * P, (blk + 1) * P)
                    hs = slice(4 * hg, 4 * hg + 4)
                    Nn, NT, XT = chs[s_i]
                    k.copy(XTb.ap, XT.ap, [XT], [XTb])
                    if full:
                        psB = make_psB(d, hg)
                        for j in range(4):
                            h = 4 * hg + j
                            k.ts(c2.ap[:, JS[j]], psB.ap[:, JS[j]], d["gcp"].ap[:, h:h + 1], 0.0,
                                 ALU.subtract, ALU.max, [psB, d["gcp"]], [c2])
                        k.act(c2.ap, c2.ap, AF.Exp, [c2], [c2], scale=-1.0)
                        k.tt(v4(c2), v4(c2), triu_b, ALU.mult, [c2, cst["triu"]], [c2])
                        k.act(egB.ap, psB.ap, AF.Exp, [psB], [egB], scale=-1.0)
                        k.tt(v4(qdT), q3[:, hs, ts_], v4(egB), ALU.mult, [qTn, egB], [qdT])
                    for src3, dst in ((k3, k_tok), (v3, v_tok)):
                        pst = k.ps()
                        pstb = pst.ap.bitcast(BF16)
                        for j in range(4):
                            k.tr(pstb[:, JS[j]], src3[:, 4 * hg + j, ts_], ident_bf.ap, [kTn, vTn, ident_bf], [pst])
                        k.copy(dst.ap, pstb[:, 0:512], [pst], [dst])
                    if full:
                        psQ = k.ps()
                        for j in range(4):
                            h = 4 * hg + j
                            k.mm(psQ.ap[:, JS[j]], k3[:, h, ts_], q3[:, h, ts_], True, True, [kTn, qTn], [psQ])
                        k.tt(aiT.ap, psQ.ap, c2.ap, ALU.mult, [psQ, c2], [aiT])
                    for j in range(4):
                        h = 4 * hg + j
                        js = JS[j]
                        k.ts(vbeta.ap[:, js], v_tok.ap[:, js], d["beta"].ap[:, h:h + 1], None, ALU.mult, None,
                             [v_tok, d["beta"]], [vbeta])
                        k.ts(kbg.ap[:, js], k_tok.ap[:, js], d["bge"].ap[:, h:h + 1], None, ALU.mult, None,
                             [k_tok, d["bge"]], [kbg])
                        k.ts(kd.ap[:, js], k_tok.ap[:, js], d["ekd"].ap[:, h:h + 1], None, ALU.mult, None,
                             [k_tok, d["ekd"]], [kd])
                    psU = k.ps()
                    psW = k.ps()
                    for j in range(4):
                        k.mm(psU.ap[:, JS[j]], XTb.ap[:, JS[j]], vbeta.ap[:, JS[j]], True, True, [XTb, vbeta], [psU])
                    for j in range(4):
                        k.mm(psW.ap[:, JS[j]], kbg.ap[:, JS[j]], XTb.ap[:, JS[j]], True, True, [XTb, kbg], [psW])
                    k.copy(u_sb.ap, psU.ap, [psU], [u_sb])
                    k.copy(wTb.ap, psW.ap, [psW], [wTb], e="dve")
                    k.copy(Sb.ap, S[hg].ap, [S[hg]], [Sb])
                    psWS = k.ps()
                    for j in range(4):
                        k.mm(psWS.ap[:, JS[j]], wTb.ap[:, JS[j]], Sb.ap[:, JS[j]], True, True, [wTb, Sb], [psWS])
                    k.tt(vnew.ap, u_sb.ap, psWS.ap, ALU.subtract, [u_sb, psWS], [vnew])
                    if full:
                        psO = k.ps()
                        for j in range(4):
                            k.mm(psO.ap[:, JS[j]], qdT.ap[:, JS[j]], Sb.ap[:, JS[j]], True, False, [qdT, Sb], [psO])
                            k.mm(psO.ap[:, JS[j]], aiT.ap[:, JS[j]], vnew.ap[:, JS[j]], False, True, [aiT, vnew], [psO])
                    psS = k.ps()
                    for j in range(4):
                        k.mm(psS.ap[:, JS[j]], kd.ap[:, JS[j]], vnew.ap[:, JS[j]], True, True, [kd, vnew], [psS])
                    for j in range(4):
                        h = 4 * hg + j
                        k.stt(S[hg].ap[:, JS[j]], S[hg].ap[:, JS[j]], d["gl"].ap[:, h:h + 1], psS.ap[:, JS[j]],
                              ALU.mult, ALU.add, [S[hg], d["gl"], psS], [S[hg]])
                    if full:
                        k.copy(o_sb.ap, psO.ap, [psO], [o_sb])
                        k.tt(sq_o.ap, o_sb.ap, o_sb.ap, ALU.mult, [o_sb], [sq_o])
                        k.op("dve", lambda e: e.tensor_reduce(out=ss.ap, in_=v4(sq_o), axis=AX.X, op=ALU.add), [sq_o], [ss])
                        k.act(ss.ap, ss.ap, AF.Ln, [ss], [ss], bias=cst_eps6.ap[:, 0:1], scale=1.0 / P)
                        k.act(ss.ap, ss.ap, AF.Exp, [ss], [ss], scale=-0.5)
                        k.tt(v4(o_sb), v4(o_sb), ss.ap.unsqueeze(2).to_broadcast([P, 4, P]), ALU.mult, [o_sb, ss], [o_sb])
                        k.tt(v4(on_b), v4(o_sb), normw_b, ALU.mult, [o_sb, cst["normw"]], [on_b])
                        pst = k.ps()
                        pstb = pst.ap.bitcast(BF16)
                        for j in range(4):
                            k.tr(pstb[:, JS[j]], on_b.ap[:, JS[j]], ident_bf.ap, [on_b, ident_bf], [pst])
                        k.copy(ygd3[:, hs, ts_], pstb[:, 0:512].rearrange("p (a b) -> p a b", a=4), [pst], [y_gdnT])
            if full:
                zs = [k.alloc(TMAX) for _ in range(2)]
                zi = [0]

                def zgate(h, ps):
                    z = zs[zi[0] % 2]
                    zi[0] += 1
                    k.act(z.ap[:, 0:n], ps.ap[:, 0:n], AF.Silu, [ps], [z])
                    k.tt(ygd3[:, h, 0:n], ygd3[:, h, 0:n], z.ap[:, 0:n], ALU.mult, [y_gdnT, z], [y_gdnT])

                for hh in range(2):
                    fm_linear("gz%d" % hh, x_bf3, [x_bf], KC, n, lambda oc, ps, hh=hh: zgate(hh * 4 + oc, ps))
            tl = [kTn, vTn, qTn] + (zs if full else []) + Ug + [k_tok, v_tok, aiT, XTb, vbeta, kbg, wTb, qdT, kd, Sb,
                                                                  vnew, c2, u_sb, sq_o, on_b, ss] + c1s + [t for c in chs for t in c]
            for d in bg:
                tl += list(d.values())
            k.release(mk, tl)

        def layer_norm(n, gname, bname, want_bf):
            mk = k.mark()
            sqs = [k.alloc(TMAX) for _ in range(2)]
            mean, rstd, nmr, t1 = [k.alloc(TMAX) for _ in range(4)]
            tn = [k.alloc(TMAX) for _ in range(2)]
            pss = k.ps()
            psq = k.ps()
            for c in range(16):
                s_ = sqs[c % 2]
                k.mm(pss.ap[:, 0:n], ones_f.ap, x_f[c].ap[:, 0:n], c == 0, c == 15, [ones_f, x_f[c]], [pss])
                k.act(s_.ap[:, 0:n], x_f[c].ap[:, 0:n], AF.Square, [x_f[c]], [s_])
                k.mm(psq.ap[:, 0:n], ones_f.ap, s_.ap[:, 0:n], c == 0, c == 15, [ones_f, s_], [psq])
            k.ts(mean.ap[:, 0:n], pss.ap[:, 0:n], 1.0 / D, None, ALU.mult, None, [pss], [mean])
            k.tt(t1.ap[:, 0:n], mean.ap[:, 0:n], mean.ap[:, 0:n], ALU.mult, [mean], [t1])
            k.stt(t1.ap[:, 0:n], psq.ap[:, 0:n], 1.0 / D, t1.ap[:, 0:n], ALU.mult, ALU.subtract, [psq, t1], [t1])
            k.act(rstd.ap[:, 0:n], t1.ap[:, 0:n], AF.Ln, [t1], [rstd], bias=cst_eps5.ap[:, 0:1])
            k.act(rstd.ap[:, 0:n], rstd.ap[:, 0:n], AF.Exp, [rstd], [rstd], scale=-0.5)
            k.stt(nmr.ap[:, 0:n], mean.ap[:, 0:n], -1.0, rstd.ap[:, 0:n], ALU.mult, ALU.mult, [mean, rstd], [nmr])
            for c in range(16):
                t = tn[c % 2]
                k.tt(t.ap[:, 0:n], x_f[c].ap[:, 0:n], rstd.ap[:, 0:n], ALU.mult, [x_f[c], rstd], [t])
                k.tt(t.ap[:, 0:n], t.ap[:, 0:n], nmr.ap[:, 0:n], ALU.add, [t, nmr], [t])
                k.act(x_f[c].ap[:, 0:n], t.ap[:, 0:n], AF.Identity, [t, cst[gname], cst[bname]], [x_f[c]],
                      bias=cst[bname].ap[:, c:c + 1], scale=cst[gname].ap[:, c:c + 1])
                if want_bf:
                    k.copy(x_bf3[:, c, 0:n], x_f[c].ap[:, 0:n], [x_f[c]], [x_bf], e="dve")
            k.release(mk, sqs + [mean, rstd, nmr, t1] + tn)

        def resid_add(c, ps, n):
            k.stt(x_f[c].ap[:, 0:n], x_f[c].ap[:, 0:n], ALPHA, ps.ap[:, 0:n], ALU.mult, ALU.add, [x_f[c], ps], [x_f[c]])

        def merge_and_mix(n):
            mk = k.mark()
            mixed = k.alloc(16 * TMAX, BF16)
            mx3 = mixed.ap.rearrange("p (c t) -> p c t", c=16)
            m1 = [k.alloc(TMAX) for _ in range(4)]
            sg = [k.alloc(TMAX) for _ in range(2)]
            si = [0]
            for og in range(8):
                for br, ysrc, ybuf in (("ms", ysw3, y_swaT), ("mg", ygd3, y_gdnT)):
                    slot, (wg, wb) = ws.next("%s%d" % (br, og))
                    for oc in range(2):
                        psg = k.ps()
                        for kc in range(KC):
                            k.mm(psg.ap[:, 0:n], wg[:, kc, oc * P:(oc + 1) * P], x_bf3[:, kc, 0:n], kc == 0, kc == KC - 1,
                                 [slot, x_bf], [psg], inc=(kc == KC - 1))
                        psy = k.ps()
                        for kc in range(8):
                            k.mm(psy.ap[:, 0:n], wb[:, kc, oc * P:(oc + 1) * P], ysrc[:, kc, 0:n], kc == 0, kc == 7,
                                 [slot, ybuf], [psy], inc=(kc == 7))
                        s_ = sg[si[0] % 2]
                        si[0] += 1
                        k.act(s_.ap[:, 0:n], psg.ap[:, 0:n], AF.Sigmoid, [psg], [s_])
                        if br == "ms":
                            k.tt(m1[oc].ap[:, 0:n], s_.ap[:, 0:n], psy.ap[:, 0:n], ALU.mult, [s_, psy], [m1[oc]])
                        else:
                            k.tt(s_.ap[:, 0:n], s_.ap[:, 0:n], psy.ap[:, 0:n], ALU.mult, [s_, psy], [s_])
                            k.tt(mx3[:, og * 2 + oc, 0:n], s_.ap[:, 0:n], m1[oc].ap[:, 0:n], ALU.add, [s_, m1[oc]], [mixed])
            for og in range(4):
                fm_linear("mix%d" % og, mx3, [mixed], KC, n, lambda oc, ps, og=og: resid_add(og * 4 + oc, ps, n))
            k.release(mk, [mixed] + m1 + sg)

        def mem_attn(n):
            mk = k.mark()
            qm = k.alloc(4 * TMAX, BF16)
            om = k.alloc(4 * TMAX, BF16)
            qm3 = qm.ap.rearrange("p (c t) -> p c t", c=4)
            om3 = om.ap.rearrange("p (c t) -> p c t", c=4)
            PTm = [k.alloc(TMAX, BF16) for _ in range(2)]
            rd = k.alloc(TMAX)
            fm_linear("mq", x_bf3, [x_bf], KC, n, lambda oc, ps: k.copy(qm3[:, oc, 0:n], ps.ap[:, 0:n], [ps], [qm]))
            km3 = KmT.ap.rearrange("p (h m) -> p h m", h=4)
            for h in range(4):
                for mb in range(2):
                    ps = k.ps()
                    k.mm(ps.ap[:, 0:n], km3[:, h, mb * P:(mb + 1) * P], qm3[:, h, 0:n], True, True, [KmT, qm], [ps])
                    k.act(PTm[mb].ap[:, 0:n], ps.ap[:, 0:n], AF.Exp, [ps], [PTm[mb]], scale=float(P ** -0.5))
                pso = k.ps()
                psd = k.ps()
                for mb in range(2):
                    k.mm(pso.ap[:, 0:n], Vm[mb].ap[:, h * P:(h + 1) * P], PTm[mb].ap[:, 0:n], mb == 0, mb == 1,
                         [Vm[mb], PTm[mb]], [pso])
                for mb in range(2):
                    k.mm(psd.ap[:, 0:n], ones_bf.ap, PTm[mb].ap[:, 0:n], mb == 0, mb == 1, [ones_bf, PTm[mb]], [psd])
                k.act(rd.ap[:, 0:n], psd.ap[:, 0:n], AF.Ln, [psd], [rd])
                k.act(rd.ap[:, 0:n], rd.ap[:, 0:n], AF.Exp, [rd], [rd], scale=-1.0)
                k.tt(om3[:, h, 0:n], pso.ap[:, 0:n], rd.ap[:, 0:n], ALU.mult, [pso, rd], [om])
            slot, (wv,) = ws.next("mo")
            for c in range(16):
                ps = k.ps()
                for kc in range(4):
                    k.mm(ps.ap[:, 0:n], wv[:, kc, c * P:(c + 1) * P], om3[:, kc, 0:n], kc == 0, kc == 3, [slot, om], [ps], inc=(kc == 3))
                resid_add(c, ps, n)
            k.release(mk, [qm, om, rd] + PTm)

        def ffn(n, halo_only):
            mk = k.mark()
            a = None if halo_only else k.alloc(NFF * TMAX, BF16)
            a3 = None if halo_only else a.ap.rearrange("p (c t) -> p c t", c=NFF)
            hb = [k.alloc(2 + TMAX) for _ in range(4)]
            tc_ = [k.alloc(TMAX) for _ in range(4)]
            fcw = cst["fcw"].ap.rearrange("p (c j) -> p c j", j=3)
            fcb = cst["fcb"].ap
            fh3 = fhalo.ap.rearrange("p (c j) -> p c j", j=2)
            cn = [0]

            def conv(ci, ps):
                i = cn[0] % 4
                cn[0] += 1
                h_, t_ = hb[i], tc_[i]
                k.copy(h_.ap[:, 0:2], fh3[:, ci, :], [fhalo], [h_], e="dve")
                k.copy(h_.ap[:, 2:2 + n], ps.ap[:, 0:n], [ps], [h_])
                if halo_only:
                    k.ts(fh3[:, ci, :], h_.ap[:, n:n + 2], cst["valid"].ap[:, 0:1], None, ALU.mult, None,
                         [h_, cst["valid"]], [fhalo])
                    return None
                k.copy(fh3[:, ci, :], h_.ap[:, n:n + 2], [h_], [fhalo], e="dve")
                k.ts(t_.ap[:, 0:n], h_.ap[:, 0:n], fcw[:, ci, 0:1], fcb[:, ci:ci + 1], ALU.mult, ALU.add,
                     [h_, cst["fcw"], cst["fcb"]], [t_])
                for j in (1, 2):
                    k.stt(t_.ap[:, 0:n], h_.ap[:, j:j + n], fcw[:, ci, j:j + 1], t_.ap[:, 0:n], ALU.mult, ALU.add,
                          [h_, cst["fcw"], t_], [t_])
                return t_

            for j0 in range(0, NFF, 2):
                npair = min(2, NFF - j0)
                slot, (wg, wu) = ws.next("up%d" % j0)
                for jj in range(npair):
                    j = j0 + jj
                    psg = k.ps()
                    for kc in range(KC):
                        k.mm(psg.ap[:, 0:n], wg[:, kc, jj * P:(jj + 1) * P], x_bf3[:, kc, 0:n], kc == 0, kc == KC - 1,
                             [slot, x_bf], [psg], inc=(kc == KC - 1))
                    psu = k.ps()
                    for kc in range(KC):
                        k.mm(psu.ap[:, 0:n], wu[:, kc, jj * P:(jj + 1) * P], x_bf3[:, kc, 0:n], kc == 0, kc == KC - 1,
                             [slot, x_bf], [psu], inc=(kc == KC - 1))
                    tg = conv(j, psg)
                    tu = conv(NFF + j, psu)
                    if not halo_only:
                        k.act(tg.ap[:, 0:n], tg.ap[:, 0:n], AF.Silu, [tg], [tg])
                        k.tt(a3[:, j, 0:n], tg.ap[:, 0:n], tu.ap[:, 0:n], ALU.mult, [tg, tu], [a])
            if not halo_only:
                for cp in range(8):
                    pso = [k.ps(), k.ps()]
                    slot, (wv,) = ws.next("dnA%d" % cp)
                    for oc in range(2):
                        for kc in range(22):
                            k.mm(pso[oc].ap[:, 0:n], wv[:, kc, oc * P:(oc + 1) * P], a3[:, kc, 0:n], kc == 0, False,
                                 [slot, a], [pso[oc]], inc=(kc == 21))
                    slot, (wv,) = ws.next("dnB%d" % cp)
                    for oc in range(2):
                        for kc in range(21):
                            k.mm(pso[oc].ap[:, 0:n], wv[:, kc, oc * P:(oc + 1) * P], a3[:, 22 + kc, 0:n], False, kc == 20,
                                 [slot, a], [pso[oc]], inc=(kc == 20))
                        resid_add(cp * 2 + oc, pso[oc], n)
            k.release(mk, [a] + hb + tc_)

        def do_tap(name, t, ncols):
            if tap is not None and tap[0] == name:
                k.dma("sp", "out", dbg[:, 0:ncols], t, R=[x_f, x_bf, y_swaT, y_gdnT, S, kTd, Vd])

        cst_eps6 = k.alloc(1)
        cst_eps5 = k.alloc(1)
        cst_one = k.alloc(1)
        k.op("dve", lambda g: g.memset(cst_eps6.ap, 1e-6), W=[cst_eps6])
        k.op("dve", lambda g: g.memset(cst_eps5.ap, EPS), W=[cst_eps5])
        k.op("dve", lambda g: g.memset(cst_one.ap, 1.0), W=[cst_one])

        def ck(name):
            if stop == name:
                raise _Stop()

        def body():
            ck("setup")
            run_all()

        def run_all():
          for ti, (c0, n) in enumerate(pre_tiles):
            load_x(c0, n, False)
            ck("pre_load")
            if ti == len(pre_tiles) - 1:
                swa_kv(n)
                swa_rotate(n)
                ck("pre_kv")
            gdn(n, False)
            ck("pre_gdn")
          for ti, (c0, n) in enumerate(main_tiles):
            halo = ti == 0
            load_x(c0, n, True)
            swa_kv(n)
            ck("kv")
            swa_attn(n, ti == 1)
            swa_rotate(n)
            ck("attn")
            if ti == 1:
                do_tap("yswa", y_swaT.ap.bitcast(F32)[:, 0:4 * TMAX], 4 * TMAX)
            gdn(n, True)
            ck("gdn")
            if ti == 1:
                do_tap("ygdn", y_gdnT.ap.bitcast(F32)[:, 0:4 * TMAX], 4 * TMAX)
                do_tap("S", S[0].ap, 512)
            merge_and_mix(n)
            ck("merge")
            layer_norm(n, "ln1g", "ln1b", True)
            ck("ln1")
            if ti == 1:
                do_tap("x1", x_f[0].ap, TMAX)
            mem_attn(n)
            ck("mem")
            layer_norm(n, "ln2g", "ln2b", True)
            if ti == 1:
                do_tap("x2", x_f[0].ap, TMAX)
            ffn(n, halo)
            ck("ffn")
            if not halo:
                layer_norm(n, "ln3g", "ln3b", False)
                oc0 = c0 - ntok_own
                k.dma("sp", "out", yT[:, oc0:oc0 + n].rearrange("(c p) t -> p c t", p=P), xf3[:, :, 0:n], R=x_f)

        try:
            body()
            assert ws.taken == len(groups), (ws.taken, len(groups))
        except _Stop:
            pass
        k.wait_all("sp")
        k.wait_all("act")
        build.stats = dict(ninstr=k.ninstr, peak_kb=k.peak / 1024.0)
    return nc


def _t5_bucket(dist):
    d = np.maximum(dist, 1).astype(np.float32)
    large = 16 + (np.log(d / np.float32(16)) / np.float32(np.log(128 / 16)) * np.float32(16)).astype(np.int32)
    large = np.minimum(large, 31)
    return np.where(dist < 16, dist, large)


def _pc(v, n):
    return np.ascontiguousarray(np.asarray(v, np.float32).reshape(n, P).T)


def prepare_inputs(inp, ntok_own, n_batch):
    f = lambda a: np.ascontiguousarray(np.asarray(a, dtype=np.float32))
    w_in = f(inp["w_in"][0])
    sk = w_in[:, C_SK:C_SK + 128]
    sv = w_in[:, C_SV:C_SV + 128]
    w_skv = np.concatenate([sk[:, 0:64], sk[:, 0:64], sk[:, 64:128], sk[:, 64:128],
                            sv[:, 0:64], sv[:, 0:64], sv[:, 64:128], sv[:, 64:128]], axis=1)
    rel_bias = f(inp["rel_bias"])
    kk = np.arange(P)[:, None]
    qq = np.arange(P)[None, :]
    biasT = np.zeros((2, P, 16, P), np.float32)
    maskneg = np.zeros((2, P, P), np.float32)
    for kb in range(2):
        dist = qq - kk + (P if kb == 0 else 0)
        inwin = (dist >= 0) & (dist < 128)
        bkt = _t5_bucket(np.maximum(dist, 0))
        g = rel_bias[bkt]
        g = np.where(inwin[:, :, None], g, np.float32(0))
        biasT[kb] = np.transpose(g, (0, 2, 1))
        maskneg[kb] = np.where(inwin, np.float32(0), np.float32(-30000.0))
    gcw = f(inp["gdn_conv_w"][0])
    gcw_l = np.ascontiguousarray(np.transpose(gcw.reshape(4, 24, P), (2, 1, 0))).reshape(P, 96)
    fw = f(inp["ffn_conv_w"][0])
    fcw_l = np.ascontiguousarray(np.transpose(fw.reshape(3, 86, P), (2, 1, 0))).reshape(P, 258)
    sinks = f(inp["swa_sinks"][0])
    sinkT = np.zeros((P, 8), np.float32)
    for c in range(8):
        sinkT[0:64, c] = sinks[2 * c]
        sinkT[64:128, c] = sinks[2 * c + 1]
    ii = np.arange(P)
    common = {
        "w_in": w_in, "w_skv": np.ascontiguousarray(w_skv),
        "w_brs": f(inp["w_br_swa"][0]), "w_brg": f(inp["w_br_gdn"][0]), "w_mix": f(inp["w_mix_o"][0]),
        "w_mq": f(inp["w_mem_q"][0]), "w_mkv": f(inp["w_mem_kv"][0]), "w_mo": f(inp["w_mem_o"][0]),
        "w_up": f(inp["w_up"][0]), "w_dn": f(inp["w_down"][0]),
        "biasT": biasT.reshape(2, P, 16 * P), "maskneg": maskneg,
        "ln1g": _pc(inp["ln1_g"][0], 16), "ln1b": _pc(inp["ln1_b"][0], 16),
        "ln2g": _pc(inp["ln2_g"][0], 16), "ln2b": _pc(inp["ln2_b"][0], 16),
        "ln3g": _pc(inp["ln3_g"][0], 16), "ln3b": _pc(inp["ln3_b"][0], 16),
        "gcw": gcw_l, "fcw": fcw_l, "fcb": _pc(inp["ffn_conv_b"][0], 86), "sinkT": sinkT,
        "alog": np.ascontiguousarray(np.broadcast_to(f(inp["gdn_a_log"][0])[None, :], (P, 8))),
        "dtb": np.ascontiguousarray(np.broadcast_to(f(inp["gdn_dt_bias"][0])[None, :], (P, 8))),
        "normw": np.ascontiguousarray(np.broadcast_to(f(inp["gdn_norm_w"][0])[None, :], (P, P))),
        "ident": np.eye(P, dtype=np.float32),
        "tril": (ii[:, None] >= ii[None, :]).astype(np.float32),
        "strict": (ii[:, None] > ii[None, :]).astype(np.float32),
        "triu": (ii[:, None] <= ii[None, :]).astype(np.float32),
    }
    x = np.asarray(inp["x"], np.float32)
    mem = np.asarray(inp["mem"], np.float32)
    maps = []
    for b in range(n_batch):
        xbT = np.ascontiguousarray(x[b].T)
        memT = np.ascontiguousarray(mem[b].T)
        for h in range(2):
            m = dict(common)
            if h == 0:
                xl = np.concatenate([np.zeros((D, ntok_own), np.float32), xbT[:, 0:ntok_own]], axis=1)
            else:
                xl = xbT[:, 0:2 * ntok_own]
            m["xT"] = np.ascontiguousarray(xl)
            m["memT"] = memT
            m["valid"] = np.full((P, 1), float(h), np.float32)
            maps.append(m)
    return maps


def run(inp, ntok_own, n_batch, tap=None, stop=None):
    nc = build(ntok_own, tap, stop)
    maps = prepare_inputs(inp, ntok_own, n_batch)
    res = run_bass_kernel_spmd(nc, maps, core_ids=list(range(len(maps))))
    out = np.zeros((n_batch, 2 * ntok_own, D), np.float32)
    for b in range(n_batch):
        for h in range(2):
            out[b, h * ntok_own:(h + 1) * ntok_own, :] = res.results[2 * b + h]["yT"].T
    dbg = [r.get("dbg") for r in res.results] if tap is not None else None
    return out, dbg


def kernel(**inputs):
    out, _ = run(inputs, 4096, 4)
    return out
```
